# Optimizing a Trainium2 kernel written in Bass

```python
import math
import jax, jax.numpy as jnp
from jax import lax
import numpy as np

D_MODEL = 1024
BATCH = 32
SEQ = 2048
DEPTH = 1
DEC_BATCH = 4
DEC_SEQ = 4096
PAST_LEN = 128

GRID_W = 64
WIN_R = 8
WIN_C = 16
ATT_HEADS = 16
HEAD_DIM = 64
ATT_W = ATT_HEADS * HEAD_DIM
SSD_INNER = 2 * D_MODEL
SSD_HEAD_DIM = 64
SSD_HEADS = SSD_INNER // SSD_HEAD_DIM
SSD_GROUPS = 8
SSD_HPG = SSD_HEADS // SSD_GROUPS
SSD_STATE = 128
SSD_CONV = 5
CONV_DIM = SSD_INNER + 2 * SSD_GROUPS * SSD_STATE
CHUNK = 128
D_FF = 2816
FFN_CONV = 3
IN_SIZES = (ATT_W, ATT_W, ATT_W, SSD_INNER, CONV_DIM, SSD_HEADS, SSD_HEADS, D_MODEL, D_MODEL)
IN_W = 3 * ATT_W + SSD_INNER + CONV_DIM + 2 * SSD_HEADS + 2 * D_MODEL
EPS = 1e-6

kernel_name = "hybrid_natten_ssd_encoder"


def rmsnorm(x, g):
    xf = x.astype(jnp.float32)
    y = xf * lax.rsqrt(jnp.mean(xf * xf, axis=-1, keepdims=True) + EPS)
    return (y * g.astype(jnp.float32)).astype(x.dtype)


def split_cols(t, sizes):
    return jnp.split(t, np.cumsum(np.array(sizes))[:-1].tolist(), axis=-1)


def dwconv_centered(u, w, bias):
    k = w.shape[0]
    out = lax.conv_general_dilated(u, w[:, None, :], window_strides=(1,),
                                   padding=[(k // 2, k // 2)],
                                   dimension_numbers=('NWC', 'WIO', 'NWC'),
                                   feature_group_count=u.shape[-1])
    return out + bias


def neighborhood_attention_2d(q, k, v, rel_bias):
    b, L, H, Dh = q.shape
    rows = L // GRID_W
    kr = min(WIN_R, rows)
    n_cb = GRID_W // WIN_C
    band = 2 * WIN_C
    qg = q.reshape(b, rows, GRID_W, H, Dh)
    kg = k.reshape(b, rows, GRID_W, H, Dh)
    vg = v.reshape(b, rows, GRID_W, H, Dh)
    qcol = np.arange(GRID_W).reshape(n_cb, WIN_C)
    col_start = np.clip(qcol - WIN_C // 2, 0, GRID_W - WIN_C)
    band_start = np.clip(np.arange(n_cb) * WIN_C - WIN_C // 2, 0, GRID_W - band)
    band_cols = band_start[:, None] + np.arange(band)
    kc = band_cols[:, None, :]
    col_ok = jnp.asarray((kc >= col_start[..., None]) & (kc < col_start[..., None] + WIN_C))
    col_bias_idx = np.clip(kc - qcol[:, :, None] + WIN_C - 1, 0, 2 * WIN_C - 2)
    k_cols = kg[:, :, band_cols]
    v_cols = vg[:, :, band_cols]
    row_start = jnp.clip(jnp.arange(rows) - WIN_R // 2, 0, rows - kr)

    def one_row(r):
        rs = row_start[r]
        kb = lax.dynamic_slice_in_dim(k_cols, rs, kr, axis=1)
        vb = lax.dynamic_slice_in_dim(v_cols, rs, kr, axis=1)
        qr = lax.dynamic_index_in_dim(qg, r, axis=1, keepdims=False).reshape(b, n_cb, WIN_C, H, Dh)
        s = jnp.einsum('bmqhd,bimkhd->bhmqik', qr, kb).astype(jnp.float32)
        row_idx = rs + jnp.arange(kr) - r + WIN_R - 1
        bias = rel_bias[:, row_idx][:, :, col_bias_idx]
        bias = jnp.transpose(bias, (0, 2, 3, 1, 4)).astype(jnp.float32)
        s = jnp.where(col_ok[:, :, None, :], s + bias, -jnp.inf)
        p = jax.nn.softmax(s.reshape(b, H, n_cb, WIN_C, kr * band), axis=-1)
        p = p.reshape(b, H, n_cb, WIN_C, kr, band).astype(v.dtype)
        o = jnp.einsum('bhmqik,bimkhd->bmqhd', p, vb)
        return o.reshape(b, GRID_W, H, Dh)

    out = lax.map(one_row, jnp.arange(rows))
    return jnp.moveaxis(out, 0, 1).reshape(b, L, H * Dh)


def ssd_chunked(x, dt, A, B, C):
    b, L, G, R, P = x.shape
    N = B.shape[-1]
    nc = L // CHUNK
    xq = x.reshape(b, nc, CHUNK, G, R, P)
    dtq = dt.reshape(b, nc, CHUNK, G, R)
    Bq = B.reshape(b, nc, CHUNK, G, N)
    Cq = C.reshape(b, nc, CHUNK, G, N)
    acs = jnp.cumsum(dtq * A, axis=2)
    xdt = xq * dtq[..., None]
    causal = jnp.tril(jnp.ones((CHUNK, CHUNK), bool))[None, None, :, :, None, None]
    seg = acs[:, :, :, None] - acs[:, :, None, :]
    decay = jnp.exp(jnp.where(causal, seg, -jnp.inf))
    scores = jnp.einsum('bcign,bcjgn->bcijg', Cq, Bq)
    y_diag = jnp.einsum('bcijgr,bcjgrp->bcigrp', scores[..., None] * decay, xdt)
    decay_to_end = jnp.exp(acs[:, :, -1:] - acs)
    chunk_states = jnp.einsum('bcjgn,bcjgrp->bcgrpn', Bq, xdt * decay_to_end[..., None])
    chunk_decay = jnp.exp(acs[:, :, -1])

    def step(h, inp):
        s, d = inp
        return h * d[..., None, None] + s, h

    h0 = jnp.zeros((b, G, R, P, N), jnp.float32)
    _, prev = lax.scan(step, h0, (jnp.moveaxis(chunk_states, 1, 0), jnp.moveaxis(chunk_decay, 1, 0)))
    prev = jnp.moveaxis(prev, 0, 1)
    y_off = jnp.einsum('bcign,bcgrpn->bcigrp', Cq, prev) * jnp.exp(acs)[..., None]
    return (y_diag + y_off).reshape(b, L, G, R, P)


def encoder_layer(x, g_mix, w_in, b_gate, q_norm_g, k_norm_g, rel_bias, ssd_conv_w, ssd_conv_b,
                  dt_bias_f, dt_bias_b, A_log_f, A_log_b, D_skip, ssd_norm_g, w_att_out, w_ssd_out,
                  w_o, g_ffn, w_up, ffn_conv_w, ffn_conv_b, w_down):
    b, L, _ = x.shape
    f32 = jnp.float32
    h = rmsnorm(x, g_mix)
    q, k, v, z, xbc, dtf, dtb, ga, gs = split_cols(h @ w_in, IN_SIZES)

    q = rmsnorm(q.reshape(b, L, ATT_HEADS, HEAD_DIM), q_norm_g) * (HEAD_DIM ** -0.5)
    k = rmsnorm(k.reshape(b, L, ATT_HEADS, HEAD_DIM), k_norm_g)
    v = v.reshape(b, L, ATT_HEADS, HEAD_DIM)
    att = neighborhood_attention_2d(q, k, v, rel_bias)

    xbc = jax.nn.silu(dwconv_centered(xbc, ssd_conv_w, ssd_conv_b))
    xs, Bm, Cm = split_cols(xbc, (SSD_INNER, SSD_GROUPS * SSD_STATE, SSD_GROUPS * SSD_STATE))
    xs = xs.reshape(b, L, SSD_GROUPS, SSD_HPG, SSD_HEAD_DIM).astype(f32)
    Bm = Bm.reshape(b, L, SSD_GROUPS, SSD_STATE).astype(f32)
    Cm = Cm.reshape(b, L, SSD_GROUPS, SSD_STATE).astype(f32)
    dt_f = jax.nn.softplus(dtf.astype(f32) + dt_bias_f.astype(f32)).reshape(b, L, SSD_GROUPS, SSD_HPG)
    dt_b = jax.nn.softplus(dtb.astype(f32) + dt_bias_b.astype(f32)).reshape(b, L, SSD_GROUPS, SSD_HPG)
    A_f = -jnp.exp(A_log_f.astype(f32)).reshape(SSD_GROUPS, SSD_HPG)
    A_b = -jnp.exp(A_log_b.astype(f32)).reshape(SSD_GROUPS, SSD_HPG)
    rev = lambda t: jnp.flip(t, axis=1)
    y_f = ssd_chunked(xs, dt_f, A_f, Bm, Cm)
    y_b = rev(ssd_chunked(rev(xs), rev(dt_b), A_b, rev(Bm), rev(Cm)))
    y = y_f + y_b + xs * D_skip.astype(f32).reshape(SSD_GROUPS, SSD_HPG, 1)
    y = y.reshape(b, L, SSD_INNER) * jax.nn.silu(z.astype(f32))
    yg = y.reshape(b, L, SSD_GROUPS, SSD_INNER // SSD_GROUPS)
    yg = yg * lax.rsqrt(jnp.mean(yg * yg, axis=-1, keepdims=True) + EPS)
    y_ssd = (yg.reshape(b, L, SSD_INNER) * ssd_norm_g.astype(f32)).astype(x.dtype)

    ba, bs = jnp.split(b_gate, 2)
    merged = jax.nn.sigmoid(ga + ba) * (att @ w_att_out) + jax.nn.sigmoid(gs + bs) * (y_ssd @ w_ssd_out)
    x = x + merged @ w_o

    u = dwconv_centered(rmsnorm(x, g_ffn) @ w_up, ffn_conv_w, ffn_conv_b)
    a, g = jnp.split(u, 2, axis=-1)
    return x + (a * jax.nn.silu(g)) @ w_down


def encoder_trunk(x, params):
    for layer in range(DEPTH):
        x = encoder_layer(x, *[p[layer] for p in params])
    return x


def setup_inputs(seed: int = 0) -> dict:
    key = jax.random.key(seed)
    ks = jax.random.split(key, 26)
    nrm = lambda k, shape, s: jax.random.normal(k, shape, jnp.float32) * s

    def dt_bias(k):
        dt = jnp.exp(jax.random.uniform(k, (DEPTH, SSD_HEADS), jnp.float32, math.log(1e-3), math.log(1e-1)))
        return dt + jnp.log(-jnp.expm1(-dt))

    return {
        "x_prompt": nrm(ks[0], (BATCH, SEQ, D_MODEL), 1.0),
        "x_sample": nrm(ks[1], (DEC_BATCH, DEC_SEQ, D_MODEL), 1.0),
        "g_mix": 1.0 + nrm(ks[2], (DEPTH, D_MODEL), 0.05),
        "w_in": nrm(ks[3], (DEPTH, D_MODEL, IN_W), D_MODEL ** -0.5),
        "b_gate": nrm(ks[4], (DEPTH, 2 * D_MODEL), 0.1),
        "q_norm_g": 1.0 + nrm(ks[5], (DEPTH, HEAD_DIM), 0.05),
        "k_norm_g": 1.0 + nrm(ks[6], (DEPTH, HEAD_DIM), 0.05),
        "rel_bias": nrm(ks[7], (DEPTH, ATT_HEADS, 2 * WIN_R - 1, 2 * WIN_C - 1), 0.5),
        "ssd_conv_w": nrm(ks[8], (DEPTH, SSD_CONV, CONV_DIM), SSD_CONV ** -0.5),
        "ssd_conv_b": nrm(ks[9], (DEPTH, CONV_DIM), 0.02),
        "dt_bias_f": dt_bias(ks[10]),
        "dt_bias_b": dt_bias(ks[11]),
        "A_log_f": jnp.log(jax.random.uniform(ks[12], (DEPTH, SSD_HEADS), jnp.float32, 1.0, 16.0)),
        "A_log_b": jnp.log(jax.random.uniform(ks[13], (DEPTH, SSD_HEADS), jnp.float32, 1.0, 16.0)),
        "D_skip": 1.0 + nrm(ks[14], (DEPTH, SSD_HEADS), 0.1),
        "ssd_norm_g": 1.0 + nrm(ks[15], (DEPTH, SSD_INNER), 0.05),
        "w_att_out": nrm(ks[16], (DEPTH, ATT_W, D_MODEL), ATT_W ** -0.5),
        "w_ssd_out": nrm(ks[17], (DEPTH, SSD_INNER, D_MODEL), SSD_INNER ** -0.5),
        "w_o": nrm(ks[18], (DEPTH, D_MODEL, D_MODEL), D_MODEL ** -0.5),
        "g_ffn": 1.0 + nrm(ks[19], (DEPTH, D_MODEL), 0.05),
        "w_up": nrm(ks[20], (DEPTH, D_MODEL, 2 * D_FF), D_MODEL ** -0.5),
        "ffn_conv_w": nrm(ks[21], (DEPTH, FFN_CONV, 2 * D_FF), FFN_CONV ** -0.5),
        "ffn_conv_b": nrm(ks[22], (DEPTH, 2 * D_FF), 0.02),
        "w_down": nrm(ks[23], (DEPTH, D_FF, D_MODEL), D_FF ** -0.5),
    }


def reference(x_prompt, x_sample, g_mix, w_in, b_gate, q_norm_g, k_norm_g, rel_bias, ssd_conv_w,
              ssd_conv_b, dt_bias_f, dt_bias_b, A_log_f, A_log_b, D_skip, ssd_norm_g, w_att_out,
              w_ssd_out, w_o, g_ffn, w_up, ffn_conv_w, ffn_conv_b, w_down):
    params = (g_mix, w_in, b_gate, q_norm_g, k_norm_g, rel_bias, ssd_conv_w, ssd_conv_b,
              dt_bias_f, dt_bias_b, A_log_f, A_log_b, D_skip, ssd_norm_g, w_att_out, w_ssd_out,
              w_o, g_ffn, w_up, ffn_conv_w, ffn_conv_b, w_down)
    y_prompt = encoder_trunk(x_prompt, params)
    y_sample = encoder_trunk(x_sample, params)
    return (y_prompt, y_sample)
```

```python
import contextlib
import numpy as np
import ml_dtypes
import concourse.bass as bass
import concourse.mybir as mybir
from concourse.bass_utils import run_bass_kernel_spmd

F32 = mybir.dt.float32
BF16 = mybir.dt.bfloat16
AF = mybir.ActivationFunctionType
ALU = mybir.AluOpType
AX = mybir.AxisListType

D_MODEL = 1024
GRID_W = 64
ATT_HEADS = 16
HEAD_DIM = 64
SSD_INNER = 2048
SSD_HEADS = 32
SSD_GROUPS = 8
D_FF = 2816
IN_W = 11328
EPS = 1e-6
NEG = -30000.0

C_Q, C_K, C_V, C_Z, C_XBC, C_DTF, C_DTB, C_GA, C_GS = 0, 1024, 2048, 3072, 5120, 9216, 9248, 9280, 10304


class _Op:
    __slots__ = ("eng", "fn", "deps", "is_dma", "needs_inc", "tok", "idx")


class Prog:
    ENGINES = ("tensor", "vector", "scalar", "gpsimd", "sync")

    def __init__(self, nc, n_dma_sems=12, dma_queues=("sync", "gpsimd", "scalar")):
        self.nc = nc
        self.ops = {e: [] for e in self.ENGINES}
        self.res = {}
        self.n_dma_sems = n_dma_sems
        self.dma_queues = dma_queues
        self.all_ops = []

    def _add(self, eng, fn, reads, writes, is_dma):
        op = _Op()
        op.eng, op.fn, op.is_dma, op.needs_inc, op.tok = eng, fn, is_dma, is_dma, None
        deps = []
        for r in reads:
            st = self.res.get(r)
            if st is not None and st[0] is not None:
                deps.append(st[0])
        for w in writes:
            st = self.res.get(w)
            if st is not None:
                if st[0] is not None:
                    deps.append(st[0])
                deps.extend(st[1])
        for r in reads:
            st = self.res.setdefault(r, [None, []])
            st[1].append(op)
        for w in writes:
            self.res[w] = [op, []]
        seen = set()
        op.deps = []
        for d in deps:
            if id(d) in seen or d is op:
                continue
            seen.add(id(d))
            if (not d.is_dma) and d.eng == eng == "tensor" and not is_dma:
                continue
            op.deps.append(d)
            d.needs_inc = True
        op.idx = len(self.all_ops)
        self.all_ops.append(op)
        self.ops[eng].append(op)
        return op

    def op(self, eng, fn, reads=(), writes=()):
        return self._add(eng, fn, reads, writes, False)

    def dma(self, eng, fn, reads=(), writes=()):
        return self._add(eng, fn, reads, writes, True)

    def setup(self, es):
        nc = self.nc
        self.csem = {e: es.enter_context(nc.semaphore(f"c_{e}")) for e in ("tensor", "vector", "scalar", "gpsimd")}
        self.dsem = {q: [es.enter_context(nc.semaphore(f"d_{q}{i}")) for i in range(self.n_dma_sems)]
                     for q in self.dma_queues}
        self.cnt = {e: 0 for e in self.csem}
        self.dcnt = {q: [0] * self.n_dma_sems for q in self.dma_queues}
        self.drr = {q: 0 for q in self.dma_queues}

    def emit(self):
        nc = self.nc
        csem, dsem, cnt, dcnt, drr = self.csem, self.dsem, self.cnt, self.dcnt, self.drr
        prev_tok = {}
        for op in self.all_ops:
            if op.is_dma:
                q = op.eng
                i = drr[q]
                drr[q] = (i + 1) % self.n_dma_sems
                prev = dcnt[q][i]
                dcnt[q][i] += 16
                op.tok = (dsem[q][i], dcnt[q][i], ("d", q, i))
                prev_tok[id(op)] = (dsem[q][i], prev, ("d", q, i)) if prev > 0 else None
            elif op.needs_inc:
                cnt[op.eng] += 1
                op.tok = (csem[op.eng], cnt[op.eng], ("c", op.eng))
        n_dma_sems, dma_queues = self.n_dma_sems, self.dma_queues
        all_ops_by_eng = self.ops

        def make(engname):
            ops = all_ops_by_eng[engname]

            def body(eng):
                waited = {}

                def wait(tok):
                    if tok is None:
                        return
                    sem, val, key = tok
                    if waited.get(key, 0) >= val:
                        return
                    eng.wait_ge(sem, val)
                    waited[key] = val

                for op in ops:
                    for d in op.deps:
                        wait(d.tok)
                    if op.is_dma:
                        wait(prev_tok[id(op)])
                        op.fn(eng).then_inc(op.tok[0], 16)
                    else:
                        ins = op.fn(eng)
                        if op.needs_inc:
                            ins.then_inc(op.tok[0], 1)
                if engname == "sync":
                    for q in dma_queues:
                        for i in range(n_dma_sems):
                            if dcnt[q][i] > 0:
                                wait((dsem[q][i], dcnt[q][i], ("d", q, i)))
                    for e in csem:
                        if cnt[e] > 0:
                            wait((csem[e], cnt[e], ("c", e)))
            return body

        with nc.Block() as blk:
            blk.tensor(make("tensor"))
            blk.vector(make("vector"))
            blk.scalar(make("scalar"))
            blk.gpsimd(make("gpsimd"))
            blk.sync(make("sync"))
        self.n_emitted = getattr(self, "n_emitted", 0) + len(self.all_ops)
        self.ops = {e: [] for e in self.ENGINES}
        self.res = {}
        self.all_ops = []


class Ring:
    def __init__(self, es, nc, name, shape, dtype, n, psum=False):
        alloc = nc.psum_tensor if psum else nc.sbuf_tensor
        self.t = [es.enter_context(alloc(f"{name}{i}", list(shape), dtype)) for i in range(n)]
        self.k = [f"{name}{i}" for i in range(n)]
        self.i = 0

    def next(self):
        i = self.i
        self.i = (i + 1) % len(self.t)
        return self.t[i], self.k[i]


class SlotRing:
    def __init__(self, items):
        self.items = items
        self.i = 0

    def next(self):
        it = self.items[self.i]
        self.i = (self.i + 1) % len(self.items)
        return it


class Cfg:
    def __init__(self, nseg=5, seg=2048, chains=((0,), (1,), (2,), (3, 4)), tg=5120, debug=False, stop_after=9):
        self.NSEG, self.SEG, self.chains, self.debug = nseg, seg, chains, debug
        self.stop_after = stop_after
        self.NTOK = nseg * seg
        self.TG = min(tg, self.NTOK)
        self.XP = seg + 4
        self.HP = seg + 2
        self.NT = self.NTOK // 128
        self.TPS = seg // 128
        assert self.NTOK % self.TG == 0 and self.TG % 512 == 0 and seg % 512 == 0


K_IDENT, K_BLK64, K_LE, K_GT, K_GE, K_LT, K_ONES, NK = 0, 1, 2, 3, 4, 5, 6, 7


def host_consts():
    k = np.arange(128)[:, None]
    i = np.arange(128)[None, :]
    m = np.zeros((NK, 128, 128), np.float32)
    m[K_IDENT] = (k == i)
    m[K_BLK64] = ((k // 64) == (i // 64)) / 64.0
    m[K_LE] = (k <= i)
    m[K_GT] = (k > i)
    m[K_GE] = (k >= i)
    m[K_LT] = (k < i)
    m[K_ONES] = 1.0
    return np.ascontiguousarray(m.transpose(1, 0, 2).reshape(128, NK * 128))


LAST_INPUT_NAMES = []


def build(cfg):
    nc = bass.Bass("TRN2", target_bir_lowering=False)
    NTOK, SEG, NSEG = cfg.NTOK, cfg.SEG, cfg.NSEG
    dbg = cfg.debug
    skind = "ExternalOutput" if dbg else "Internal"

    LAST_INPUT_NAMES.clear()

    def din(name, shape, dt=F32):
        LAST_INPUT_NAMES.append(name)
        return nc.dram_tensor(name, list(shape), dt, kind="ExternalInput").ap()

    def dscr(name, shape, dt):
        return nc.dram_tensor(name, list(shape), dt, kind=skind).ap()

    D = {}
    D["x"] = din("x", [NTOK, 1024])
    D["w_in"] = din("w_in", [1024, IN_W])
    D["consts"] = din("consts", [128, NK * 128])
    D["gmixT"] = din("gmixT", [128, 8])
    D["bgT"] = din("bgT", [128, 16])
    D["qkg"] = din("qkg", [128, 2])
    D["dtbias"] = din("dtbias", [128, 64])
    D["flags"] = din("flags", [128, 8])
    D["convw"] = din("convw", [128, 160])
    D["convbc"] = din("convbc", [128, 32])
    D["convbr"] = din("convbr", [1, 4096])
    D["alog"] = din("alog", [128, 64])
    D["dskip"] = din("dskip", [128, 32])
    D["normg"] = din("normg", [128, 2048])
    D["biasx"] = din("biasx", [16, ND, 128, 128])
    D["cmask"] = din("cmask", [ND, 128, 128])
    D["halfsel"] = din("halfsel", [128, 128], BF16)
    D["rowmask"] = din("rowmask", [2, cfg.NT * 768], BF16)
    D["w_att_out"] = din("w_att_out", [1024, 1024])
    D["w_ssd_out"] = din("w_ssd_out", [2048, 1024])
    D["w_o"] = din("w_o", [1024, 1024])
    D["w_up"] = din("w_up", [1024, 5632])
    D["w_down"] = din("w_down", [2816, 1024])
    D["gffnT"] = din("gffnT", [128, 8])
    D["fcw"] = din("fcw", [128, 132])
    D["fcb"] = din("fcb", [128, 44])
    D["qT"] = dscr("qT", [1024, NTOK], BF16)
    D["kT"] = dscr("kT", [1024, NTOK], BF16)
    D["v"] = dscr("v", [NTOK, 1024], BF16)
    D["zs"] = dscr("zs", [NTOK, 2048], BF16)
    D["xbcT"] = dscr("xbcT", [4096, NSEG * cfg.XP], BF16)
    D["dt"] = dscr("dt", [NTOK, 64], F32)
    D["gT"] = dscr("gT", [2048, NTOK], BF16)
    D["yT"] = dscr("yT", [2048, NTOK], BF16)
    D["attT"] = dscr("attT", [1024, NTOK], BF16)
    D["x1"] = dscr("x1", [NTOK, 1024], F32)
    D["h2T"] = dscr("h2T", [1024, NSEG * cfg.HP], BF16)
    D["out"] = nc.dram_tensor("out", [NTOK, 1024], F32, kind="ExternalOutput").ap()

    with contextlib.ExitStack() as ges:
        P = Prog(nc)
        P.setup(ges)
        kc_f = ges.enter_context(nc.sbuf_tensor("kc_f", [128, NK * 128], F32))
        kc_b = ges.enter_context(nc.sbuf_tensor("kc_b", [128, NK * 128], BF16))
        flags = ges.enter_context(nc.sbuf_tensor("flags_sb", [128, 8], F32))
        P.dma("sync", lambda e: e.dma_start(out=kc_f[:, :], in_=D["consts"]), [], ["kc_f"])
        P.dma("sync", lambda e: e.dma_start(out=flags[:, :], in_=D["flags"]), [], ["flags"])
        P.op("vector", lambda e: e.tensor_copy(kc_b[:, :], kc_f[:, :]), ["kc_f"], ["kc_b"])
        G = dict(kc_b=kc_b, kc_f=kc_f, flags=flags)
        phase1(nc, P, cfg, D, G)
        P.emit()
        if cfg.stop_after >= 2:
            phase2(nc, P, cfg, D, G)
            P.emit()
        if cfg.stop_after >= 3:
            phase3(nc, P, cfg, D, G)
            P.emit()
        if cfg.stop_after >= 4:
            phase4a(nc, P, cfg, D, G)
            P.emit()
        if cfg.stop_after >= 5:
            phase4b(nc, P, cfg, D, G)
            P.emit()
    return nc


def KB(G, k):
    return G["kc_b"][:, k * 128:(k + 1) * 128]


def phase1(nc, P, cfg, D, G):
    NTOK, SEG, TG = cfg.NTOK, cfg.SEG, cfg.TG
    with contextlib.ExitStack() as es:
        sb = lambda n, s, d: es.enter_context(nc.sbuf_tensor(n, list(s), d))
        hT = sb("hT", [128, 8, TG], BF16)
        gmixT = sb("gmixT_sb", [128, 8], F32)
        bgT = sb("bgT_sb", [128, 16], F32)
        qkg = sb("qkg_sb", [128, 2], F32)
        qkgs = sb("qkgs_sb", [128, 2], F32)
        dtb = sb("dtb_sb", [128, 64], F32)
        zpad = sb("zpad", [128, 32, 2], BF16)
        xin = Ring(es, nc, "xin", [128, 1024], F32, 2)
        xn = Ring(es, nc, "xn", [128, 1024], BF16, 2)
        junk = Ring(es, nc, "junk", [128, 1024], BF16, 1)
        st = Ring(es, nc, "st", [128, 4], F32, 2)
        Wr = Ring(es, nc, "W", [128, 8, 512], BF16, 2)
        sq = Ring(es, nc, "sq", [128, 512], BF16, 2)
        lnb = Ring(es, nc, "lnb", [128, 512], F32, 2)
        ob = Ring(es, nc, "ob", [128, 512], BF16, 4)
        pd = Ring(es, nc, "pd", [128, 4], BF16, 4)
        dts = Ring(es, nc, "dts", [128, 64], F32, 2)
        psT = Ring(es, nc, "psT", [128, 8, 128], BF16, 2, psum=True)
        psM = Ring(es, nc, "psM", [128, 512], F32, 4, psum=True)
        psN = Ring(es, nc, "psN", [128, 512], F32, 2, psum=True)

        P.dma("sync", lambda e: e.dma_start(out=gmixT[:, :], in_=D["gmixT"]), [], ["gmixT"])
        P.dma("sync", lambda e: e.dma_start(out=bgT[:, :], in_=D["bgT"]), [], ["bgT"])
        P.dma("sync", lambda e: e.dma_start(out=qkg[:, :], in_=D["qkg"]), [], ["qkg"])
        P.dma("sync", lambda e: e.dma_start(out=dtb[:, :], in_=D["dtbias"]), [], ["dtb"])
        P.op("vector", lambda e: e.tensor_scalar(qkgs[:, 0:1], qkg[:, 0:1], HEAD_DIM ** -0.5, None, ALU.mult),
             ["qkg"], ["qkgs"])
        P.op("vector", lambda e: e.tensor_copy(qkgs[:, 1:2], qkg[:, 1:2]), ["qkg", "qkgs"], ["qkgs"])
        P.op("vector", lambda e: e.memset(zpad[:, :, :], 0.0), [], ["zpad"])
        xv = D["xbcT"].rearrange("(cc p) t -> p cc t", p=128)
        XP = cfg.XP
        P.dma("gpsimd", lambda e: e.dma_start(out=xv[:, :, 0:2], in_=zpad[:, :, :]), ["zpad"], [])
        P.dma("gpsimd", lambda e: e.dma_start(out=xv[:, :, cfg.NSEG * XP - 2:cfg.NSEG * XP], in_=zpad[:, :, :]),
              ["zpad"], [])

        wv = D["w_in"].rearrange("(kc p) c -> p kc c", p=128)
        ident = KB(G, K_IDENT)
        blk64 = KB(G, K_BLK64)

        def load_w(c0, ncols):
            Wt, Wk = Wr.next()
            P.dma("gpsimd", lambda e: e.dma_start(out=Wt[:, :, 0:ncols], in_=wv[:, :, c0:c0 + ncols]), [], [Wk])
            return Wt, Wk

        for tg0 in range(0, NTOK, TG):
            for ti in range(TG // 128):
                t0 = tg0 + ti * 128
                xt, xk = xin.next()
                P.dma("sync", lambda e, xt=xt, t0=t0: e.dma_start(out=xt[:, :], in_=D["x"][t0:t0 + 128, :]), [], [xk])
                s_, sk = st.next()
                jt, jk = junk.next()
                P.op("scalar", lambda e, jt=jt, xt=xt, s_=s_: e.activation(jt[:, :], xt[:, :], AF.Square, accum_out=s_[:, 0:1]),
                     [xk], [jk, sk])
                P.op("scalar", lambda e, s_=s_: e.activation(s_[:, 1:2], s_[:, 0:1], AF.Ln, bias=EPS, scale=1.0 / 1024),
                     [sk], [sk])
                P.op("scalar", lambda e, s_=s_: e.activation(s_[:, 2:3], s_[:, 1:2], AF.Exp, scale=-0.5), [sk], [sk])
                xnt, xnk = xn.next()
                P.op("vector", lambda e, xnt=xnt, xt=xt, s_=s_: e.tensor_scalar(xnt[:, :], xt[:, :], s_[:, 2:3], None, ALU.mult),
                     [xk, sk], [xnk])
                pt, pk = psT.next()
                for kc in range(8):
                    P.op("tensor", lambda e, pt=pt, xnt=xnt, kc=kc: e.transpose(pt[:, kc, :], xnt[:, kc * 128:(kc + 1) * 128], ident),
                         [xnk, "kc_b"], [pk])
                P.op("vector", lambda e, pt=pt, ti=ti: e.tensor_tensor(
                    hT[:, :, ti * 128:(ti + 1) * 128], pt[:, :, :],
                    gmixT[:, :].unsqueeze(2).to_broadcast([128, 8, 128]), ALU.mult),
                    [pk, "gmixT"], [("hT", ti)])

            def hkeys(ta, tb):
                return [("hT", i) for i in range(ta // 128, tb // 128)]

            fjobs = [("q", C_Q, 1024), ("k", C_K, 1024), ("xbc", C_XBC, 4096), ("g", C_GA, 2048)]
            pend = []
            for name, cbase, ctot in fjobs:
                for cb in range(0, ctot, 512):
                    Wt, Wk = load_w(cbase + cb, 512)
                    for tb in range(0, TG, 512):
                        tglob = tg0 + tb
                        for cc in range(4):
                            f0 = cb + cc * 128
                            pm, pmk = psM.next()
                            for kc in range(8):
                                P.op("tensor", lambda e, pm=pm, Wt=Wt, cc=cc, kc=kc, tb=tb: e.matmul(
                                    pm[:, :], Wt[:, kc, cc * 128:(cc + 1) * 128], hT[:, kc, tb:tb + 512],
                                    start=(kc == 0), stop=(kc == 7)), [Wk] + hkeys(tb, tb + 512), [pmk])
                            while len(pend) > 0 and (name not in ("q", "k") or len(pend) > 1 or True):
                                pend.pop(0)()
                            if name not in ("q", "k"):
                                o, ok = ob.next()
                            if name in ("q", "k"):
                                col = 0 if name == "q" else 1
                                s2, s2k = sq.next()
                                P.op("scalar", lambda e, s2=s2, pm=pm: e.activation(s2[:, :], pm[:, :], AF.Square), [pmk], [s2k])
                                def fin(pm=pm, pmk=pmk, s2=s2, s2k=s2k, col=col, f0=f0, tglob=tglob, name=name):
                                    o, ok = ob.next()
                                    pn, pnk = psN.next()
                                    P.op("tensor", lambda e, pn=pn, s2=s2: e.matmul(pn[:, :], blk64, s2[:, :], start=True, stop=True),
                                         [s2k, "kc_b"], [pnk])
                                    lb, lbk = lnb.next()
                                    P.op("scalar", lambda e, lb=lb, pn=pn: e.activation(lb[:, :], pn[:, :], AF.Ln, bias=EPS), [pnk], [lbk])
                                    P.op("scalar", lambda e, lb=lb: e.activation(lb[:, :], lb[:, :], AF.Exp, scale=-0.5), [lbk], [lbk])
                                    P.op("vector", lambda e, o=o, pm=pm, lb=lb, col=col: e.scalar_tensor_tensor(
                                        o[:, :], pm[:, :], qkgs[:, col:col + 1], lb[:, :], ALU.mult, ALU.mult),
                                        [pmk, lbk, "qkgs"], [ok])
                                    dst = D["qT" if name == "q" else "kT"]
                                    P.dma("sync", lambda e, o=o, dst=dst, f0=f0, tglob=tglob: e.dma_start(
                                        out=dst[f0:f0 + 128, tglob:tglob + 512], in_=o[:, :]), [ok], [])
                                pend.append(fin)
                            elif name == "g":
                                gi = f0 // 128
                                P.op("scalar", lambda e, o=o, pm=pm, gi=gi: e.activation(
                                    o[:, :], pm[:, :], AF.Sigmoid, bias=bgT[:, gi:gi + 1]), [pmk, "bgT"], [ok])
                                P.dma("sync", lambda e, o=o, f0=f0, tglob=tglob: e.dma_start(
                                    out=D["gT"][f0:f0 + 128, tglob:tglob + 512], in_=o[:, :]), [ok], [])
                            else:
                                P.op("vector", lambda e, o=o, pm=pm: e.tensor_copy(o[:, :], pm[:, :]), [pmk], [ok])
                                s = tglob // SEG
                                tin = tglob % SEG
                                c0 = s * XP + 2 + tin
                                P.dma("sync", lambda e, o=o, f0=f0, c0=c0: e.dma_start(
                                    out=D["xbcT"][f0:f0 + 128, c0:c0 + 512], in_=o[:, :]), [ok], [])
                                if tin == 0 and s > 0:
                                    p_, pk_ = pd.next()
                                    P.op("vector", lambda e, p_=p_, pm=pm, s=s: e.tensor_scalar(
                                        p_[:, 0:2], pm[:, 0:2], G["flags"][:, s:s + 1], None, ALU.mult), [pmk, "flags"], [pk_])
                                    cp = (s - 1) * XP + 2 + SEG
                                    P.dma("sync", lambda e, p_=p_, f0=f0, cp=cp: e.dma_start(
                                        out=D["xbcT"][f0:f0 + 128, cp:cp + 2], in_=p_[:, 0:2]), [pk_], [])
                                if tin + 512 == SEG and s + 1 < cfg.NSEG:
                                    p_, pk_ = pd.next()
                                    P.op("vector", lambda e, p_=p_, pm=pm, s=s: e.tensor_scalar(
                                        p_[:, 0:2], pm[:, 510:512], G["flags"][:, s + 1:s + 2], None, ALU.mult), [pmk, "flags"], [pk_])
                                    cp = (s + 1) * XP
                                    P.dma("sync", lambda e, p_=p_, f0=f0, cp=cp: e.dma_start(
                                        out=D["xbcT"][f0:f0 + 128, cp:cp + 2], in_=p_[:, 0:2]), [pk_], [])
            tjobs = [("v", C_V, 1024), ("z", C_Z, 2048), ("dt", C_DTF, 64)]
            for name, cbase, ctot in tjobs:
                for cb in range(0, ctot, 512):
                    ncols = min(512, ctot - cb)
                    Wt, Wk = load_w(cbase + cb, ncols)
                    for tt in range(0, TG, 128):
                        tglob = tg0 + tt
                        pm, pmk = psM.next()
                        for kc in range(8):
                            P.op("tensor", lambda e, pm=pm, Wt=Wt, kc=kc, tt=tt, ncols=ncols: e.matmul(
                                pm[:, 0:ncols], hT[:, kc, tt:tt + 128], Wt[:, kc, 0:ncols],
                                start=(kc == 0), stop=(kc == 7)), [Wk, ("hT", tt // 128)], [pmk])
                        if name == "dt":
                            d_, dk = dts.next()
                            P.op("vector", lambda e, d_=d_, pm=pm: e.tensor_tensor(d_[:, :], pm[:, 0:64], dtb[:, :], ALU.add),
                                 [pmk, "dtb"], [dk])
                            P.op("scalar", lambda e, d_=d_: e.activation(d_[:, :], d_[:, :], AF.Exp), [dk], [dk])
                            P.op("scalar", lambda e, d_=d_: e.activation(d_[:, :], d_[:, :], AF.Ln, bias=1.0), [dk], [dk])
                            P.dma("sync", lambda e, d_=d_, tglob=tglob: e.dma_start(
                                out=D["dt"][tglob:tglob + 128, :], in_=d_[:, :]), [dk], [])
                        else:
                            o, ok = ob.next()
                            if name == "v":
                                P.op("vector", lambda e, o=o, pm=pm: e.tensor_copy(o[:, :], pm[:, :]), [pmk], [ok])
                                P.dma("sync", lambda e, o=o, cb=cb, tglob=tglob: e.dma_start(
                                    out=D["v"][tglob:tglob + 128, cb:cb + 512], in_=o[:, :]), [ok], [])
                            else:
                                P.op("scalar", lambda e, o=o, pm=pm: e.activation(o[:, :], pm[:, :], AF.Silu), [pmk], [ok])
                                P.dma("sync", lambda e, o=o, cb=cb, tglob=tglob: e.dma_start(
                                    out=D["zs"][tglob:tglob + 128, cb:cb + 512], in_=o[:, :]), [ok], [])


def phase2(nc, P, cfg, D, G):
    SEG, TPS, XP = cfg.SEG, cfg.TPS, cfg.XP
    maxc = max(len(c) for c in cfg.chains)
    with contextlib.ExitStack() as es:
        sb = lambda n, s, d: es.enter_context(nc.sbuf_tensor("s2_" + n, list(s), d))
        convw = sb("convw", [128, 32, 5], F32)
        convbc = sb("convbc", [128, 32], F32)
        convbr_r = Ring(es, nc, "s2_convbr", [1, 384], BF16, 2)
        alog = sb("alog", [128, 64], F32)
        Arow = sb("Arow", [128, 64], F32)
        dsk = sb("dsk", [128, 32], F32)
        normg_r = Ring(es, nc, "s2_normg", [128, 256], F32, 2)
        diag = Ring(es, nc, "s2_diag", [128, 4, 5, 128], BF16, 2)
        xin = Ring(es, nc, "s2_xinc", [128, 4, XP], BF16, 2)
        xb_tm = [sb(f"xb_tm{i}", [128, TPS, 384], BF16) for i in range(maxc)]
        BT = [sb(f"BT{i}", [128, SEG], BF16) for i in range(maxc)]
        CT = [sb(f"CT{i}", [128, SEG], BF16) for i in range(maxc)]
        prevb = [sb(f"prevb{i}", [128, TPS, 256], BF16) for i in range(maxc)]
        zs = [sb(f"zs{i}", [128, TPS, 256], BF16) for i in range(maxc)]
        dtseg = [sb(f"dtseg{i}", [128, TPS, 64], F32) for i in range(maxc)]
        dtA = [sb(f"dtA{i}", [128, TPS, 8], F32) for i in range(maxc)]
        dtAb = [sb(f"dtAb{i}", [128, TPS, 8], BF16) for i in range(maxc)]
        decs = [sb(f"decs{i}", [128, 5, TPS, 8], F32) for i in range(maxc)]
        dtee = [sb(f"dtee{i}", [128, TPS, 8], F32) for i in range(maxc)]
        lndt = [sb(f"lndt{i}", [128, TPS, 8], F32) for i in range(maxc)]
        diagD_r = Ring(es, nc, "s2_diagD", [128, 4, 128], BF16, 2)
        Hfs = [sb(f"Hf{i}", [128, 256], F32) for i in range(2)]
        Hbs = [sb(f"Hb{i}", [128, 256], F32) for i in range(2)]
        prevf = [sb(f"prevf{i}", [128, TPS, 256], BF16) for i in range(maxc)]
        Ht = Ring(es, nc, "s2_Ht", [128, 256], F32, 3)
        xdeb = Ring(es, nc, "s2_xdeb", [128, 256], BF16, 4)
        sm = Ring(es, nc, "s2_sm", [128, 2, 128], F32, 3)
        Lr = Ring(es, nc, "s2_Lseg", [128, 4, 128], BF16, 4)
        Er = Ring(es, nc, "s2_Eseg", [128, 4, 128], F32, 4)
        MTr = Ring(es, nc, "s2_MT", [128, 4, 128], BF16, 4)
        ya = Ring(es, nc, "s2_ya", [128, 256], F32, 4)
        yb = Ring(es, nc, "s2_yb", [128, 256], F32, 5)
        yy = Ring(es, nc, "s2_yy", [128, 256], F32, 4)
        yj = Ring(es, nc, "s2_yj", [128, 256], BF16, 1)
        yst = Ring(es, nc, "s2_yst", [128, 4], F32, 4)
        yn = Ring(es, nc, "s2_yn", [128, 256], BF16, 3)
        ystage = Ring(es, nc, "s2_ystage", [128, 2, 512], BF16, 2)
        psG = Ring(es, nc, "s2_ps2g", [128, 512], F32, 4, psum=True)
        bank_sc = es.enter_context(nc.psum_tensor("s2_bank_sc", [128, 512], F32))
        bank_o = es.enter_context(nc.psum_tensor("s2_bank_o", [128, 512], F32))
        bank_y = es.enter_context(nc.psum_tensor("s2_bank_y", [128, 512], F32))
        bank_t = es.enter_context(nc.psum_tensor("s2_bank_t", [128, 2, 128], BF16))
        sc_slots = SlotRing([(bank_sc[:, 0:128], "s2_sc")])
        o_slots = SlotRing([(bank_o, "s2_o")])
        y_slots = SlotRing([(bank_y[:, 0:256], "s2_y")])
        t_slots = SlotRing([(bank_t[:, :, :], "s2_t")])

        P.dma("sync", lambda e: e.dma_start(out=convw[:, :, :], in_=D["convw"].rearrange("p (c j) -> p c j", j=5)), [], ["convw"])
        P.dma("sync", lambda e: e.dma_start(out=convbc[:, :], in_=D["convbc"]), [], ["convbc"])
        P.dma("sync", lambda e: e.dma_start(out=alog[:, :], in_=D["alog"]), [], ["alog"])
        P.dma("sync", lambda e: e.dma_start(out=dsk[:, :], in_=D["dskip"]), [], ["dsk"])
        P.op("scalar", lambda e: e.activation(Arow[:, :], alog[:, :], AF.Exp), ["alog"], ["Arow"])
        P.op("vector", lambda e: e.tensor_scalar(Arow[:, :], Arow[:, :], -1.0, None, ALU.mult), ["Arow"], ["Arow"])

        ident = KB(G, K_IDENT)
        kf = lambda k: G["kc_f"][:, k * 128:(k + 1) * 128]
        ones_row = G["kc_b"][0:1, K_ONES * 128:(K_ONES + 1) * 128]
        xv = D["xbcT"].rearrange("(cc p) t -> p cc t", p=128)
        yTv = D["yT"].rearrange("(cc p) t -> p cc t", p=128)
        TRIS = (K_LE, K_GT, K_GE, K_LT, K_ONES)
        I_LE, I_GT, I_GE, I_LT, I_ONES = 0, 1, 2, 3, 4
        flags = G["flags"]

        for g in range(8):
            ccs = (2 * g, 2 * g + 1, 16 + g, 24 + g)
            dg, dgk = diag.next()
            convbr, cbk = convbr_r.next()
            dD, dDk = diagD_r.next()
            for h in range(4):
                P.op("vector", lambda e, dD=dD, h=h, g=g: e.tensor_scalar(
                    dD[:, h, :], ident, dsk[:, 4 * g + h:4 * g + h + 1], None, ALU.mult), ["kc_b", "dsk"], [(dDk, h)])
            normg, ngk = normg_r.next()
            for ci in range(3):
                P.dma("gpsimd", lambda e, convbr=convbr, ci=ci, ch0=ccs[ci] * 128: e.dma_start(
                    out=convbr[:, ci * 128:(ci + 1) * 128], in_=D["convbr"][:, ch0:ch0 + 128]), [], [(cbk, ci)])
            P.dma("sync", lambda e, normg=normg, g=g: e.dma_start(out=normg[:, :], in_=D["normg"][:, g * 256:(g + 1) * 256]), [], [ngk])
            for ci, cc in enumerate(ccs):
                for j in range(5):
                    P.op("vector", lambda e, dg=dg, ci=ci, cc=cc, j=j: e.tensor_scalar(
                        dg[:, ci, j, :], ident, convw[:, cc, j:j + 1], None, ALU.mult), ["kc_b", "convw"], [dgk])
            for chain in cfg.chains:
                for li, s in enumerate(chain):
                    xt, xk = xin.next()
                    c0 = s * XP
                    P.dma("sync", lambda e, xt=xt, c0=c0, g=g: e.dma_start(out=xt[:, 0:2, :], in_=xv[:, 2 * g:2 * g + 2, c0:c0 + XP]), [], [xk])
                    P.dma("sync", lambda e, xt=xt, c0=c0, g=g: e.dma_start(out=xt[:, 2, :], in_=xv[:, 16 + g, c0:c0 + XP]), [xk], [xk])
                    P.dma("sync", lambda e, xt=xt, c0=c0, g=g: e.dma_start(out=xt[:, 3, :], in_=xv[:, 24 + g, c0:c0 + XP]), [xk], [xk])
                    t0 = s * SEG
                    P.dma("sync", lambda e, li=li, t0=t0, g=g: e.dma_start(
                        out=zs[li][:, :, :], in_=D["zs"][t0:t0 + SEG, g * 256:(g + 1) * 256].rearrange("(c p) f -> p c f", p=128)),
                        [], [("zs", li)])
                    P.dma("sync", lambda e, li=li, t0=t0: e.dma_start(
                        out=dtseg[li][:, :, :], in_=D["dt"][t0:t0 + SEG, :].rearrange("(c p) f -> p c f", p=128)), [], [("dt", li)])
                    for d in range(2):
                        h0 = d * 32 + 4 * g
                        P.op("vector", lambda e, li=li, d=d, h0=h0: e.tensor_tensor(
                            dtA[li][:, :, d * 4:d * 4 + 4], dtseg[li][:, :, h0:h0 + 4],
                            Arow[:, h0:h0 + 4].unsqueeze(1).to_broadcast([128, TPS, 4]), ALU.mult),
                            [("dt", li), "Arow"], [("dtA", li)])
                    P.op("vector", lambda e, li=li: e.tensor_copy(dtAb[li][:, :, :], dtA[li][:, :, :]), [("dtA", li)], [("dtAb", li)])
                    for d in range(2):
                        h0 = d * 32 + 4 * g
                        P.op("scalar", lambda e, li=li, d=d, h0=h0: e.activation(
                            lndt[li][:, :, d * 4:d * 4 + 4], dtseg[li][:, :, h0:h0 + 4], AF.Ln), [("dt", li)], [("lndt", li, d)])
                    pa, pak = psG.next()
                    pb, pbk = psG.next()
                    for ti, tk in enumerate(TRIS):
                        pt_, ptk = (pa, pak) if ti < 4 else (pb, pbk)
                        o0 = (ti % 4) * TPS * 8
                        P.op("tensor", lambda e, pt_=pt_, o0=o0, tk=tk, li=li: e.matmul(
                            pt_[:, o0:o0 + TPS * 8], KB(G, tk), dtAb[li][:, :, :].rearrange("p c h -> p (c h)"),
                            start=True, stop=True), [("dtAb", li), "kc_b"], [ptk])
                    P.op("scalar", lambda e, li=li, pa=pa: e.activation(
                        decs[li][:, 0:4, :, :].rearrange("p a c h -> p (a c h)"), pa[:, 0:4 * TPS * 8], AF.Exp), [pak], [("decs", li)])
                    P.op("scalar", lambda e, li=li, pb=pb: e.activation(
                        decs[li][:, 4, :, :].rearrange("p c h -> p (c h)"), pb[:, 0:TPS * 8], AF.Exp), [pbk, ("decs", li)], [("decs", li)])
                    for d, tri in ((0, I_GT), (1, I_LT)):
                        h0 = d * 32 + 4 * g
                        P.op("vector", lambda e, li=li, d=d, h0=h0, tri=tri: e.tensor_tensor(
                            dtee[li][:, :, d * 4:d * 4 + 4], dtseg[li][:, :, h0:h0 + 4],
                            decs[li][:, tri, :, d * 4:d * 4 + 4], ALU.mult), [("dt", li), ("decs", li)], [("dtee", li)])
                    for c in range(TPS):
                        pc, pck = psG.next()
                        for ci in range(3):
                            ch0 = ccs[ci] * 128
                            for j in range(5):
                                P.op("tensor", lambda e, pc=pc, xt=xt, ci=ci, j=j, c=c, dg=dg: e.matmul(
                                    pc[:, ci * 128:(ci + 1) * 128], xt[:, ci, c * 128 + j:c * 128 + j + 128], dg[:, ci, j, :],
                                    start=(j == 0), stop=False), [xk, dgk], [pck])
                            P.op("tensor", lambda e, pc=pc, ci=ci, convbr=convbr: e.matmul(
                                pc[:, ci * 128:(ci + 1) * 128], ones_row, convbr[0:1, ci * 128:(ci + 1) * 128], start=False, stop=True),
                                ["kc_b", (cbk, ci)], [pck])
                        P.op("scalar", lambda e, pc=pc, li=li, c=c: e.activation(xb_tm[li][:, c, :], pc[:, 0:384], AF.Silu),
                             [pck], [("xb", li, c)])
                    for tb in range(0, SEG, 512):
                        for ci, dstl, nm in ((2, BT, "BT"), (3, CT, "CT")):
                            pf, pfk = psG.next()
                            for j in range(5):
                                P.op("tensor", lambda e, pf=pf, xt=xt, ci=ci, j=j, tb=tb, dg=dg: e.matmul(
                                    pf[:, :], dg[:, ci, j, :], xt[:, ci, tb + j:tb + j + 512], start=(j == 0), stop=(j == 4)),
                                    [xk, dgk], [pfk])
                            cc = ccs[ci]
                            P.op("scalar", lambda e, pf=pf, dstl=dstl, li=li, tb=tb, cc=cc: e.activation(
                                dstl[li][:, tb:tb + 512], pf[:, :], AF.Silu, bias=convbc[:, cc:cc + 1]),
                                [pfk, "convbc"], [(nm, li, tb // 512)])

                order = [(li, c) for li in range(len(chain)) for c in range(TPS)]
                n_ord = len(order)
                bc = lambda ap: ap.unsqueeze(2).to_broadcast([128, 4, 64])
                r4 = lambda ap: ap.rearrange("p (h q) -> p h q", h=4)

                def state_step(d, li, c, first, bnd_flag):
                    Hs, nm = (Hfs, "Hf") if d == 0 else (Hbs, "Hb")
                    prev = prevf if d == 0 else prevb
                    cur = hpar[d]
                    Hc, Hn = Hs[cur], Hs[1 - cur]
                    if first:
                        P.op("vector", lambda e, Hc=Hc: e.memset(Hc[:, :], 0.0), [], [(nm, cur)])
                    if bnd_flag is not None:
                        P.op("vector", lambda e, Hc=Hc, f=bnd_flag: e.tensor_scalar(
                            Hc[:, :], Hc[:, :], flags[:, f:f + 1], None, ALU.mult), [(nm, cur), "flags"], [(nm, cur)])
                    P.op("scalar", lambda e, Hc=Hc, prev=prev, li=li, c=c: e.activation(prev[li][:, c, :], Hc[:, :], AF.Copy),
                         [(nm, cur)], [("prev", d, li, c)])
                    xd, xdk = xdeb.next()
                    P.op("gpsimd", lambda e, xd=xd, li=li, c=c, d=d: e.tensor_tensor(
                        r4(xd[:, :]), r4(xb_tm[li][:, c, 0:256]), bc(dtee[li][:, c, d * 4:d * 4 + 4]), ALU.mult),
                        [("xb", li, c), ("dtee", li)], [xdk])
                    ps, psk = psG.next()
                    P.op("tensor", lambda e, ps=ps, xd=xd, li=li, c=c: e.matmul(
                        ps[:, 0:256], xb_tm[li][:, c, 256:384], xd[:, :], start=True, stop=True), [("xb", li, c), xdk], [psk])
                    ht, htk = Ht.next()
                    P.op("vector", lambda e, ht=ht, Hc=Hc, li=li, c=c, d=d: e.tensor_tensor(
                        r4(ht[:, :]), r4(Hc[:, :]), bc(decs[li][:, I_ONES, c, d * 4:d * 4 + 4]), ALU.mult),
                        [(nm, cur), ("decs", li)], [htk])
                    P.op("vector", lambda e, ht=ht, ps=ps, Hn=Hn: e.tensor_tensor(Hn[:, :], ht[:, :], ps[:, 0:256], ALU.add),
                         [htk, psk], [(nm, 1 - cur)])
                    hpar[d] = 1 - cur

                hpar = [0, 0]
                for i in range(n_ord):
                    li, c = order[i]
                    fl = chain[li] if (c == 0 and li > 0) else None
                    state_step(0, li, c, i == 0, fl)
                    li2, c2 = order[n_ord - 1 - i]
                    fl2 = chain[li2 + 1] if (c2 == TPS - 1 and li2 + 1 < len(chain)) else None
                    state_step(1, li2, c2, i == 0, fl2)

                stC = {}

                def c0(i):
                    li, c = order[i]
                    S = stC[i] = {"Ls": []}
                    for d, trix in enumerate((K_GT, K_LT)):
                        L, Lk = Lr.next()
                        P.op("gpsimd" if d else "vector", lambda e, L=L, d=d, trix=trix, li=li, c=c: e.tensor_tensor(
                            L[:, :, :], kf(trix).unsqueeze(1).to_broadcast([128, 4, 128]),
                            dtA[li][:, c, d * 4:d * 4 + 4].unsqueeze(2).to_broadcast([128, 4, 128]), ALU.mult),
                            ["kc_f", ("dtA", li)], [Lk])
                        S["Ls"].append((L, Lk))

                def c1(i):
                    li, c = order[i]
                    S = stC[i]
                    tok = slice(c * 128, (c + 1) * 128)
                    pS, pSk = sc_slots.next()
                    P.op("tensor", lambda e, pS=pS, li=li, tok=tok: e.matmul(
                        pS, BT[li][:, tok], CT[li][:, tok], start=True, stop=True),
                        [("BT", li, c // 4), ("CT", li, c // 4)], [pSk])
                    pO, pOk = o_slots.next()
                    P.op("tensor", lambda e, pO=pO, li=li, tok=tok, c=c: e.matmul(
                        pO[:, 0:256], CT[li][:, tok], prevf[li][:, c, :], start=True, stop=True),
                        [("CT", li, c // 4), ("prev", 0, li, c)], [pOk])
                    P.op("tensor", lambda e, pO=pO, li=li, tok=tok, c=c: e.matmul(
                        pO[:, 256:512], CT[li][:, tok], prevb[li][:, c, :], start=True, stop=True),
                        [("CT", li, c // 4), ("prev", 1, li, c)], [pOk])
                    pgs = []
                    for d, triy in enumerate((K_LE, K_GE)):
                        L, Lk = S["Ls"][d]
                        pg, pgk = psG.next()
                        for h in range(4):
                            P.op("tensor", lambda e, pg=pg, L=L, h=h, triy=triy: e.matmul(
                                pg[:, h * 128:(h + 1) * 128], L[:, h, :], KB(G, triy), start=True, stop=True), [Lk, "kc_b"], [pgk])
                        pgs.append((pg, pgk))
                    S.update(pS=pS, pSk=pSk, pO=pO, pOk=pOk, pgs=pgs)

                def c2(i):
                    li, c = order[i]
                    S = stC[i]
                    pS, pSk, pO, pOk = S["pS"], S["pSk"], S["pO"], S["pOk"]
                    sm_, smk = sm.next()
                    P.op("vector", lambda e, sm_=sm_, pS=pS: e.tensor_tensor(sm_[:, 0, :], pS, kf(K_LE), ALU.mult),
                         [pSk, "kc_f"], [(smk, 0)])
                    P.op("vector", lambda e, sm_=sm_, pS=pS: e.tensor_tensor(sm_[:, 1, :], pS, kf(K_GE), ALU.mult),
                         [pSk, "kc_f"], [(smk, 1)])
                    a_, ak = ya.next()
                    b_, bk = yb.next()
                    P.op("vector", lambda e, a_=a_, pO=pO, li=li, c=c: e.tensor_tensor(
                        r4(a_[:, :]), r4(pO[:, 0:256]), bc(decs[li][:, I_LE, c, 0:4]), ALU.mult), [pOk, ("decs", li)], [ak])
                    P.op("vector", lambda e, b_=b_, pO=pO, li=li, c=c: e.tensor_tensor(
                        r4(b_[:, :]), r4(pO[:, 256:512]), bc(decs[li][:, I_GE, c, 4:8]), ALU.mult), [pOk, ("decs", li)], [bk])
                    Es = []
                    for d in range(2):
                        pg, pgk = S["pgs"][d]
                        E, Ek = Er.next()
                        for h in range(4):
                            P.op("scalar", lambda e, E=E, pg=pg, h=h, li=li, c=c, d=d: e.activation(
                                E[:, h, :], pg[:, h * 128:(h + 1) * 128], AF.Exp, bias=lndt[li][:, c, d * 4 + h:d * 4 + h + 1]),
                                [pgk, ("lndt", li, d)], [(Ek, h)])
                        Es.append((E, Ek))
                    S.update(sm_=sm_, smk=smk, a_=a_, ak=ak, b_=b_, bk=bk, Es=Es)

                def c3(i):
                    S = stC[i]
                    sm_, smk = S["sm_"], S["smk"]
                    MTs = []
                    for d in range(2):
                        E, Ek = S["Es"][d]
                        MT, MTk = MTr.next()
                        P.op("vector", lambda e, MT=MT, E=E, sm_=sm_, d=d: e.tensor_tensor(
                            MT[:, :, :], E[:, :, :], sm_[:, d, :].unsqueeze(1).to_broadcast([128, 4, 128]), ALU.mult),
                            [(Ek, 0), (Ek, 1), (Ek, 2), (Ek, 3), (smk, d)], [MTk])
                        MTs.append((MT, MTk))
                    S["MTs"] = MTs

                def c4(i):
                    li, c = order[i]
                    S = stC[i]
                    pY, pYk = y_slots.next()
                    for h in range(4):
                        xs_h = xb_tm[li][:, c, h * 64:(h + 1) * 64]
                        P.op("tensor", lambda e, pY=pY, h=h, xs_h=xs_h, dD=dD: e.matmul(
                            pY[:, h * 64:(h + 1) * 64], dD[:, h, :], xs_h, start=(h == 0), stop=False),
                            [(dDk, h), ("xb", li, c)], [pYk])
                        for d in range(2):
                            MT, MTk = S["MTs"][d]
                            P.op("tensor", lambda e, pY=pY, MT=MT, xs_h=xs_h, h=h, d=d: e.matmul(
                                pY[:, h * 64:(h + 1) * 64], MT[:, h, :], xs_h,
                                start=False, stop=(h == 3 and d == 1)), [MTk, ("xb", li, c)], [pYk])
                    S.update(pY=pY, pYk=pYk)

                def c5(i):
                    S = stC[i]
                    y_, yk = yy.next()
                    P.op("vector", lambda e, y_=y_, pY=S["pY"], a_=S["a_"]: e.tensor_tensor(y_[:, :], pY, a_[:, :], ALU.add),
                         [S["pYk"], S["ak"]], [yk])
                    S.update(y_=y_, yk=yk)

                def c6(i):
                    li, c = order[i]
                    S = stC[i]
                    y_, yk = S["y_"], S["yk"]
                    P.op("gpsimd", lambda e, y_=y_, b_=S["b_"]: e.tensor_tensor(y_[:, :], y_[:, :], b_[:, :], ALU.add), [yk, S["bk"]], [yk])
                    P.op("gpsimd", lambda e, y_=y_, li=li, c=c: e.tensor_tensor(y_[:, :], y_[:, :], zs[li][:, c, :], ALU.mult),
                         [yk, ("zs", li)], [yk])

                def c7(i):
                    S = stC[i]
                    y_, yk = S["y_"], S["yk"]
                    jt, jk = yj.next()
                    s_, sk = yst.next()
                    P.op("scalar", lambda e, jt=jt, y_=y_, s_=s_: e.activation(jt[:, :], y_[:, :], AF.Square, accum_out=s_[:, 0:1]),
                         [yk], [jk, sk])
                    P.op("scalar", lambda e, s_=s_: e.activation(s_[:, 1:2], s_[:, 0:1], AF.Ln, bias=EPS, scale=1.0 / 256), [sk], [sk])
                    P.op("scalar", lambda e, s_=s_: e.activation(s_[:, 2:3], s_[:, 1:2], AF.Exp, scale=-0.5), [sk], [sk])
                    S.update(s_=s_, sk=sk)

                def c8(i):
                    S = stC[i]
                    n_, nk = yn.next()
                    P.op("vector", lambda e, n_=n_, y_=S["y_"], s_=S["s_"], normg=normg: e.scalar_tensor_tensor(
                        n_[:, :], y_[:, :], s_[:, 2:3], normg[:, :], ALU.mult, ALU.mult),
                        [S["yk"], S["sk"], ngk], [nk])
                    S.update(n_=n_, nk=nk)

                def c9(i):
                    S = stC[i]
                    n_, nk = S["n_"], S["nk"]
                    pT, pTk = t_slots.next()
                    for q in range(2):
                        P.op("tensor", lambda e, pT=pT, n_=n_, q=q: e.transpose(pT[:, q, :], n_[:, q * 128:(q + 1) * 128], ident),
                             [nk, "kc_b"], [pTk])
                    S.update(pT=pT, pTk=pTk)

                def c10(i):
                    li, c = order[i]
                    S = stC.pop(i)
                    pT, pTk = S["pT"], S["pTk"]
                    if c % 4 == 0:
                        ysr[0] = ystage.next()
                    ys, ysk = ysr[0]
                    P.op("scalar", lambda e, ys=ys, pT=pT, c=c: e.activation(ys[:, :, (c % 4) * 128:(c % 4 + 1) * 128], pT, AF.Copy),
                         [pTk], [(ysk, c % 4)])
                    if c % 4 == 3:
                        tg = chain[li] * SEG + (c - 3) * 128
                        P.dma("scalar", lambda e, ys=ys, tg=tg, g=g: e.dma_start(
                            out=yTv[:, 2 * g:2 * g + 2, tg:tg + 512], in_=ys[:, :, :]), [(ysk, q) for q in range(4)], [])

                ysr = [None]
                stages = [c0, c1, c2, c3, c4, c5, c6, c7, c8, c9, c10]
                NS = len(stages)
                sorder = [2, 5, 10] + [s for s in reversed(range(NS)) if s not in (2, 5, 10)]
                for t in range(n_ord + NS - 1):
                    for s in sorder:
                        if 0 <= t - s < n_ord:
                            stages[s](t - s)


def rep128(v):
    v = np.asarray(v, np.float32).reshape(1, -1)
    return np.ascontiguousarray(np.broadcast_to(v, (128, v.shape[1])))


def colT(v, n):
    return np.ascontiguousarray(np.asarray(v, np.float32).reshape(n, 128).T)


def shared_inputs(p):
    d = {}
    d["w_in"] = np.ascontiguousarray(p["w_in"][0])
    d["consts"] = host_consts()
    d["gmixT"] = colT(p["g_mix"][0], 8)
    d["bgT"] = colT(p["b_gate"][0], 16)
    d["qkg"] = np.ascontiguousarray(np.stack([np.tile(p["q_norm_g"][0], 2), np.tile(p["k_norm_g"][0], 2)], 1).astype(np.float32))
    d["dtbias"] = rep128(np.concatenate([p["dt_bias_f"][0], p["dt_bias_b"][0]]))
    cw = p["ssd_conv_w"][0]
    d["convw"] = np.ascontiguousarray(cw.reshape(5, 32, 128).transpose(2, 1, 0).reshape(128, 160))
    d["convbc"] = colT(p["ssd_conv_b"][0], 32)
    d["convbr"] = np.ascontiguousarray(p["ssd_conv_b"][0].reshape(1, 4096))
    d["alog"] = rep128(np.concatenate([p["A_log_f"][0], p["A_log_b"][0]]))
    d["dskip"] = rep128(p["D_skip"][0])
    d["normg"] = rep128(p["ssd_norm_g"][0])
    return d


ND = 7


def att_slots(t, tps):
    if t == 0:
        return list(range(-2, 4))
    if t == tps - 1:
        return list(range(-3, 3))
    return list(range(-2, 3))


def bias_gather_index():
    a = np.arange(128)[:, None] // 64
    kc = np.arange(128)[:, None] % 64
    b = np.arange(128)[None, :] // 64
    qc = np.arange(128)[None, :] % 64
    cs = np.clip(qc - 8, 0, 48)
    col_ok = (kc >= cs) & (kc < cs + 16)
    ridx = np.zeros((ND, 128, 128), np.int64)
    cidx = np.zeros((ND, 128, 128), np.int64)
    ok = np.zeros((ND, 128, 128), bool)
    for di in range(ND):
        dr = 2 * (di - 3) + a - b
        dc = kc - qc
        ok[di] = col_ok & (np.abs(dr) <= 7) & (np.abs(dc) <= 15)
        ridx[di] = np.clip(dr + 7, 0, 14)
        cidx[di] = np.clip(dc + 15, 0, 30)
    return ridx, cidx, ok


def att_shared_inputs(p):
    ridx, cidx, ok = bias_gather_index()
    rb = np.asarray(p["rel_bias"][0], np.float32)
    d = {}
    d["biasx"] = np.ascontiguousarray(rb[:, ridx, cidx])
    d["cmask"] = np.ascontiguousarray(np.where(ok, 0.0, NEG).astype(np.float32))
    hs = np.zeros((128, 128), np.float32)
    hs[0, :64] = 1.0
    hs[1, 64:] = 1.0
    d["halfsel"] = hs.astype(ml_dtypes.bfloat16)
    return d


def rowmask_table(cfg, flags_vec):
    tps, nt = cfg.TPS, cfg.NT
    seq_start = np.zeros(cfg.NSEG, np.int64)
    seq_len = np.zeros(cfg.NSEG, np.int64)
    s = 0
    while s < cfg.NSEG:
        e = s
        while e + 1 < cfg.NSEG and flags_vec[e + 1] > 0:
            e += 1
        for k in range(s, e + 1):
            seq_start[k] = s
            seq_len[k] = e - s + 1
        s = e + 1
    tab = np.full((2, nt, 6, 128), NEG, np.float32)
    bq = np.arange(128) // 64
    for T in range(nt):
        sg, t = divmod(T, tps)
        t0 = seq_start[sg] * tps
        rows = seq_len[sg] * tps * 2
        qr = (T - t0) * 2 + bq
        rs = np.clip(qr - 4, 0, rows - 8)
        for si, dlt in enumerate(att_slots(t, tps)):
            KT = T + dlt
            if KT < t0 or KT >= t0 + seq_len[sg] * tps:
                continue
            for a in range(2):
                kr = (KT - t0) * 2 + a
                tab[a, T, si] = np.where((kr >= rs) & (kr < rs + 8), 0.0, NEG)
    return np.ascontiguousarray(tab.reshape(2, nt * 6 * 128)).astype(ml_dtypes.bfloat16)


def phase3(nc, P, cfg, D, G):
    NT, TPS = cfg.NT, cfg.TPS
    NB = NT // 4
    with contextlib.ExitStack() as es:
        sb = lambda n, s, d: es.enter_context(nc.sbuf_tensor("s3_" + n, list(s), d))
        biasC = sb("biasC", [128, 16, ND, 128], F32)
        cmask = sb("cmask", [128, ND, 128], F32)
        halfsel = sb("halfsel", [128, 128], BF16)
        Kr = [sb(f"K{i}", [128, 8, 512], BF16) for i in range(4)]
        Vr = [sb(f"V{i}", [128, 4, 16, 65], BF16) for i in range(4)]
        Qr = Ring(es, nc, "s3_Q", [128, 8, 2, 512], BF16, 2)
        rmr = Ring(es, nc, "s3_rm", [128, 6, 128], BF16, 4)
        Ep = Ring(es, nc, "s3_Ep", [128, 6, 128], F32, 3)
        Eb = Ring(es, nc, "s3_Eb", [128, 6, 128], BF16, 5)
        rec = Ring(es, nc, "s3_rec", [128, 1], F32, 4)
        atm = Ring(es, nc, "s3_atm", [128, 1024], BF16, 3)
        ast = Ring(es, nc, "s3_ast", [128, 8, 512], BF16, 2)
        psA = Ring(es, nc, "s3_psA", [128, 512], F32, 2, psum=True)
        psB = Ring(es, nc, "s3_psB", [128, 512], F32, 2, psum=True)
        psO = Ring(es, nc, "s3_psO", [128, 512], F32, 2, psum=True)
        psT = Ring(es, nc, "s3_psT", [128, 8, 128], BF16, 1, psum=True)
        ident = KB(G, K_IDENT)

        P.dma("sync", lambda e: e.dma_start(out=cmask[:, :, :], in_=D["cmask"].rearrange("d k q -> k d q")), [], ["cmask"])
        P.dma("sync", lambda e: e.dma_start(out=halfsel[:, :], in_=D["halfsel"]), [], ["halfsel"])
        for h in range(16):
            P.dma("sync", lambda e, h=h: e.dma_start(out=biasC[:, h, :, :], in_=D["biasx"][h].rearrange("d k q -> k d q")),
                  [], [("biasC", h)])
            P.op("gpsimd", lambda e, h=h: e.tensor_tensor(biasC[:, h, :, :], biasC[:, h, :, :], cmask[:, :, :], ALU.add),
                 [("biasC", h), "cmask"], [("biasC", h)])
        for i in range(4):
            P.op("gpsimd", lambda e, i=i: e.memset(Vr[i][:, :, :, :], 1.0), [], [("V", i)])
        for i in range(2):
            P.op("vector", lambda e, i=i: e.memset(Qr.t[i][:, :, :, :], 0.0), [], [Qr.k[i]])
        for i in range(4):
            P.op("vector", lambda e, i=i: e.memset(rmr.t[i][:, :, :], 0.0), [], [rmr.k[i]])

        qTv = D["qT"].rearrange("(hp two p) t -> two p hp t", two=2, p=64)
        kTv = D["kT"].rearrange("(hp p) t -> p hp t", p=128)
        aTv = D["attT"].rearrange("(cc p) t -> p cc t", p=128)
        loaded = set()

        def load_kv(b):
            if b < 0 or b >= NB or b in loaded:
                return
            loaded.add(b)
            i = b % 4
            P.dma("sync", lambda e, b=b, i=i: e.dma_start(out=Kr[i][:, :, :], in_=kTv[:, :, b * 512:(b + 1) * 512]), [], [("K", i)])
            for c in range(4):
                P.dma("sync", lambda e, b=b, i=i, c=c: e.dma_start(
                    out=Vr[i][:, c, :, 0:64],
                    in_=D["v"][b * 512 + c * 128:b * 512 + (c + 1) * 128, :].rearrange("p (h d) -> p h d", d=64)),
                    [("V", i)], [("V", i)])

        items = [(b, tt, h) for b in range(NB) for tt in range(4) for h in range(16)]
        stT = {}
        stH = {}

        def part_a(k):
            b, tt, h = items[k]
            T = b * 4 + tt
            if tt == 0 and h == 0:
                load_kv(b - 1)
                load_kv(b)
                load_kv(b + 1)
                Qt, Qk = Qr.next()
                for hh in range(2):
                    P.dma("sync", lambda e, Qt=Qt, b=b, hh=hh: e.dma_start(
                        out=Qt[hh * 64:(hh + 1) * 64, :, hh, :], in_=qTv[hh][:, :, b * 512:(b + 1) * 512]), [Qk], [Qk])
                stT[("b", b)] = (Qt, Qk) + ast.next()
            Qt, Qk, as_, ask = stT[("b", b)]
            t = T % TPS
            slots = att_slots(t, TPS)
            nsl = len(slots)
            if h == 0:
                rm, rmk = rmr.next()
                P.dma("sync", lambda e, rm=rm, T=T: e.dma_start(
                    out=rm[0:2, :, :], in_=D["rowmask"][:, T * 768:(T + 1) * 768].rearrange("a (s q) -> a s q", q=128)), [rmk], [rmk])
                stT[T] = (rm, rmk) + atm.next()
            rm, rmk, at, atk = stT[T]
            hp, hh = divmod(h, 2)
            pr = slice(hh * 64, hh * 64 + 64)
            pa, pak = psA.next()
            pb, pbk = psB.next()
            kts = []
            for si, dlt in enumerate(slots):
                KT = min(max(T + dlt, 0), NT - 1)
                kb, kt = divmod(KT, 4)
                kts.append((kb % 4, kt))
                pt_, ptk = (pa, pak) if si < 4 else (pb, pbk)
                o0 = (si % 4) * 128
                P.op("tensor", lambda e, pt_=pt_, o0=o0, kb=kb, kt=kt, hp=hp, hh=hh, Qt=Qt, tt=tt: e.matmul(
                    pt_[:, o0:o0 + 128], Kr[kb % 4][:, hp, kt * 128:(kt + 1) * 128], Qt[:, hp, hh, tt * 128:(tt + 1) * 128],
                    start=True, stop=False), [("K", kb % 4), Qk], [ptk])
                P.op("tensor", lambda e, pt_=pt_, o0=o0, rm=rm, si=si: e.matmul(
                    pt_[:, o0:o0 + 128], halfsel[:, :], rm[:, si, :], start=False, stop=True), ["halfsel", rmk], [ptk])
            ep, epk = Ep.next()
            d0 = slots[0] + 3
            P.op("vector", lambda e, ep=ep, pa=pa, h=h, d0=d0: e.tensor_tensor(
                ep[:, 0:4, :], pa[:, :].rearrange("p (s q) -> p s q", q=128), biasC[:, h, d0:d0 + 4, :], ALU.add),
                [pak, ("biasC", h)], [(epk, 0)])
            nb_ = nsl - 4
            P.op("vector", lambda e, ep=ep, pb=pb, h=h, d0=d0, nb_=nb_: e.tensor_tensor(
                ep[:, 4:4 + nb_, :], pb[:, 0:nb_ * 128].rearrange("p (s q) -> p s q", q=128),
                biasC[:, h, d0 + 4:d0 + 4 + nb_, :], ALU.add), [pbk, ("biasC", h)], [(epk, 1)])
            eb, ebk = Eb.next()
            P.op("scalar", lambda e, eb=eb, ep=ep, nsl=nsl: e.activation(eb[:, 0:nsl, :], ep[:, 0:nsl, :], AF.Exp),
                 [(epk, 0), (epk, 1)], [ebk])
            stH[k] = (eb, ebk, kts, nsl)

        def part_b(k):
            b, tt, h = items[k]
            T = b * 4 + tt
            eb, ebk, kts, nsl = stH.pop(k)
            rm, rmk, at, atk = stT[T]
            Qt, Qk, as_, ask = stT[("b", b)]
            po, pok = psO.next()
            for si in range(nsl):
                vb, vt = kts[si]
                P.op("tensor", lambda e, po=po, eb=eb, si=si, vb=vb, vt=vt, h=h, nsl=nsl: e.matmul(
                    po[:, 0:65], eb[:, si, :], Vr[vb][:, vt, h, :], start=(si == 0), stop=(si == nsl - 1)),
                    [ebk, ("V", vb)], [pok])
            rc, rck = rec.next()
            P.op("vector", lambda e, rc=rc, po=po: e.reciprocal(rc[:, :], po[:, 64:65]), [pok], [rck])
            P.op("vector", lambda e, at=at, po=po, rc=rc, h=h: e.tensor_scalar(
                at[:, h * 64:(h + 1) * 64], po[:, 0:64], rc[:, 0:1], None, ALU.mult), [pok, rck], [(atk, h)])
            if h == 15:
                pt, ptk2 = psT.next()
                for cc in range(8):
                    P.op("tensor", lambda e, pt=pt, at=at, cc=cc: e.transpose(pt[:, cc, :], at[:, cc * 128:(cc + 1) * 128], ident),
                         [(atk, 2 * cc), (atk, 2 * cc + 1), "kc_b"], [ptk2])
                P.op("scalar", lambda e, as_=as_, pt=pt, tt=tt: e.activation(as_[:, :, tt * 128:(tt + 1) * 128], pt[:, :, :], AF.Copy),
                     [ptk2], [(ask, tt)])
                del stT[T]
                if tt == 3:
                    P.dma("scalar", lambda e, as_=as_, b=b: e.dma_start(out=aTv[:, :, b * 512:(b + 1) * 512], in_=as_[:, :, :]),
                          [(ask, q) for q in range(4)], [])
                    del stT[("b", b)]

        SK = 2
        for k in range(len(items) + SK):
            if k < len(items):
                part_a(k)
            if k - SK >= 0:
                part_b(k - SK)


def phase4a(nc, P, cfg, D, G):
    NTOK, SEG, HP, NSEG = cfg.NTOK, cfg.SEG, cfg.HP, cfg.NSEG
    with contextlib.ExitStack() as es:
        sb = lambda n, s, d: es.enter_context(nc.sbuf_tensor("s4_" + n, list(s), d))
        Wao = sb("Wao", [128, 8, 1024], BF16)
        Wso = sb("Wso", [128, 16, 1024], BF16)
        Wo = sb("Wo", [128, 8, 1024], BF16)
        gffn = sb("gffn", [128, 8], F32)
        zc = sb("zc", [128, 8, 1], BF16)
        aT = Ring(es, nc, "s4_aT", [128, 8, 512], BF16, 2)
        yT = Ring(es, nc, "s4_yT", [128, 16, 512], BF16, 2)
        gT = Ring(es, nc, "s4_gT", [128, 16, 512], BF16, 2)
        xr = Ring(es, nc, "s4_x", [128, 1024], F32, 2)
        t1 = Ring(es, nc, "s4_t1", [128, 512], F32, 2)
        t2 = Ring(es, nc, "s4_t2", [128, 512], F32, 2)
        mT = Ring(es, nc, "s4_mT", [128, 8, 512], BF16, 1)
        x1r = Ring(es, nc, "s4_x1", [128, 1024], F32, 2)
        junk = Ring(es, nc, "s4_junk", [128, 1024], BF16, 1)
        st = Ring(es, nc, "s4_st", [128, 4], F32, 2)
        xn = Ring(es, nc, "s4_xn", [128, 1024], BF16, 2)
        hst = Ring(es, nc, "s4_hst", [128, 8, 512], BF16, 2)
        pdr = Ring(es, nc, "s4_pd", [128, 8, 1], BF16, 4)
        psA = Ring(es, nc, "s4_psA", [128, 512], F32, 2, psum=True)
        psB = Ring(es, nc, "s4_psB", [128, 512], F32, 2, psum=True)
        psX = Ring(es, nc, "s4_psX", [128, 512], F32, 2, psum=True)
        psT = Ring(es, nc, "s4_psT", [128, 8, 128], BF16, 2, psum=True)
        ident = KB(G, K_IDENT)
        flags = G["flags"]

        wao_v = D["w_att_out"].rearrange("(kc p) d -> p kc d", p=128)
        wso_v = D["w_ssd_out"].rearrange("(kc p) d -> p kc d", p=128)
        wo_v = D["w_o"].rearrange("(kc p) d -> p kc d", p=128)
        P.dma("gpsimd", lambda e: e.dma_start(out=Wao[:, :, :], in_=wao_v), [], ["Wao"])
        P.dma("gpsimd", lambda e: e.dma_start(out=Wso[:, 0:8, :], in_=wso_v[:, 0:8, :]), [], [("Wso", 0)])
        P.dma("gpsimd", lambda e: e.dma_start(out=Wso[:, 8:16, :], in_=wso_v[:, 8:16, :]), [], [("Wso", 1)])
        P.dma("gpsimd", lambda e: e.dma_start(out=Wo[:, :, :], in_=wo_v), [], ["Wo"])
        P.dma("sync", lambda e: e.dma_start(out=gffn[:, :], in_=D["gffnT"]), [], ["gffn"])
        P.op("vector", lambda e: e.memset(zc[:, :, :], 0.0), [], ["zc"])
        hv = D["h2T"].rearrange("(kc p) t -> p kc t", p=128)
        P.dma("gpsimd", lambda e: e.dma_start(out=hv[:, :, 0:1], in_=zc[:, :, :], allow_slow_non_contiguous=True), ["zc"], [])
        P.dma("gpsimd", lambda e: e.dma_start(out=hv[:, :, NSEG * HP - 1:NSEG * HP], in_=zc[:, :, :], allow_slow_non_contiguous=True), ["zc"], [])
        aTv = D["attT"].rearrange("(cc p) t -> p cc t", p=128)
        yTv = D["yT"].rearrange("(cc p) t -> p cc t", p=128)
        gTv = D["gT"].rearrange("(cc p) t -> p cc t", p=128)

        def load_blk(tb):
            a_, ak = aT.next()
            y_, yk = yT.next()
            g_, gk = gT.next()
            P.dma("sync", lambda e, a_=a_, tb=tb: e.dma_start(out=a_[:, :, :], in_=aTv[:, :, tb:tb + 512]), [], [ak])
            P.dma("sync", lambda e, y_=y_, tb=tb: e.dma_start(out=y_[:, :, :], in_=yTv[:, :, tb:tb + 512]), [], [yk])
            P.dma("sync", lambda e, g_=g_, tb=tb: e.dma_start(out=g_[:, :, :], in_=gTv[:, :, tb:tb + 512]), [], [gk])
            return a_, ak, y_, yk, g_, gk

        nxt = load_blk(0)
        for tb in range(0, NTOK, 512):
            a_, ak, y_, yk, g_, gk = nxt
            if tb + 512 < NTOK:
                nxt = load_blk(tb + 512)
            m_, mk = mT.next()
            for dmc in range(8):
                pa, pak = psA.next()
                pb, pbk = psB.next()
                for kc in range(8):
                    P.op("tensor", lambda e, pa=pa, a_=a_, kc=kc, dmc=dmc: e.matmul(
                        pa[:, :], Wao[:, kc, dmc * 128:(dmc + 1) * 128], a_[:, kc, :], start=(kc == 0), stop=(kc == 7)),
                        ["Wao", ak], [pak])
                for kc in range(16):
                    P.op("tensor", lambda e, pb=pb, y_=y_, kc=kc, dmc=dmc: e.matmul(
                        pb[:, :], Wso[:, kc, dmc * 128:(dmc + 1) * 128], y_[:, kc, :], start=(kc == 0), stop=(kc == 15)),
                        [("Wso", kc // 8), yk], [pbk])
                u1, u1k = t1.next()
                u2, u2k = t2.next()
                P.op("vector", lambda e, u1=u1, pa=pa, g_=g_, dmc=dmc: e.tensor_tensor(u1[:, :], pa[:, :], g_[:, dmc, :], ALU.mult),
                     [pak, gk], [u1k])
                P.op("vector", lambda e, u2=u2, pb=pb, g_=g_, dmc=dmc: e.tensor_tensor(u2[:, :], pb[:, :], g_[:, 8 + dmc, :], ALU.mult),
                     [pbk, gk], [u2k])
                P.op("gpsimd", lambda e, m_=m_, u1=u1, u2=u2, dmc=dmc: e.tensor_tensor(m_[:, dmc, :], u1[:, :], u2[:, :], ALU.add),
                     [u1k, u2k], [(mk, dmc)])
            hs, hsk = hst.next()
            pend = []
            for tt in range(4):
                t0 = tb + tt * 128
                x_, xk = xr.next()
                P.dma("sync", lambda e, x_=x_, t0=t0: e.dma_start(out=x_[:, :], in_=D["x"][t0:t0 + 128, :]), [], [xk])
                x1, x1k = x1r.next()
                for dh in range(2):
                    px, pxk = psX.next()
                    for kc in range(8):
                        P.op("tensor", lambda e, px=px, m_=m_, kc=kc, tt=tt, dh=dh: e.matmul(
                            px[:, :], m_[:, kc, tt * 128:(tt + 1) * 128], Wo[:, kc, dh * 512:(dh + 1) * 512],
                            start=(kc == 0), stop=(kc == 7)), [(mk, kc), "Wo"], [pxk])
                    P.op("vector", lambda e, x1=x1, px=px, x_=x_, dh=dh: e.tensor_tensor(
                        x1[:, dh * 512:(dh + 1) * 512], px[:, :], x_[:, dh * 512:(dh + 1) * 512], ALU.add), [pxk, xk], [(x1k, dh)])
                def fin(x1=x1, x1k=x1k, tt=tt, hs=hs, hsk=hsk, t0=t0):
                    P.dma("scalar", lambda e, x1=x1, t0=t0: e.dma_start(out=D["x1"][t0:t0 + 128, :], in_=x1[:, :]),
                          [(x1k, 0), (x1k, 1)], [])
                    jt, jk = junk.next()
                    s_, sk = st.next()
                    P.op("scalar", lambda e, jt=jt, x1=x1, s_=s_: e.activation(jt[:, :], x1[:, :], AF.Square, accum_out=s_[:, 0:1]),
                         [(x1k, 0), (x1k, 1)], [jk, sk])
                    P.op("scalar", lambda e, s_=s_: e.activation(s_[:, 1:2], s_[:, 0:1], AF.Ln, bias=EPS, scale=1.0 / 1024), [sk], [sk])
                    P.op("scalar", lambda e, s_=s_: e.activation(s_[:, 2:3], s_[:, 1:2], AF.Exp, scale=-0.5), [sk], [sk])
                    n_, nk = xn.next()
                    P.op("scalar", lambda e, n_=n_, x1=x1, s_=s_: e.activation(n_[:, :], x1[:, :], AF.Copy, scale=s_[:, 2:3]),
                         [(x1k, 0), (x1k, 1), sk], [nk])
                    pt, ptk = psT.next()
                    for kc in range(8):
                        P.op("tensor", lambda e, pt=pt, n_=n_, kc=kc: e.transpose(pt[:, kc, :], n_[:, kc * 128:(kc + 1) * 128], ident),
                             [nk, "kc_b"], [ptk])
                    P.op("vector", lambda e, hs=hs, pt=pt, tt=tt: e.tensor_tensor(
                        hs[:, :, tt * 128:(tt + 1) * 128], pt[:, :, :], gffn[:, :].unsqueeze(2).to_broadcast([128, 8, 128]), ALU.mult),
                        [ptk, "gffn"], [(hsk, tt)])
                if pend:
                    pend.pop(0)()
                pend.append(fin)
            while pend:
                pend.pop(0)()
            s, tin = divmod(tb, SEG)
            c0 = s * HP + 1 + tin
            hkeys = [(hsk, i) for i in range(4)]
            P.dma("scalar", lambda e, hs=hs, c0=c0: e.dma_start(out=hv[:, :, c0:c0 + 512], in_=hs[:, :, :]), hkeys, [])
            if tin == 0 and s > 0:
                p_, pk_ = pdr.next()
                P.op("vector", lambda e, p_=p_, hs=hs, s=s: e.tensor_scalar(p_[:, :, :], hs[:, :, 0:1], flags[:, s:s + 1], None, ALU.mult),
                     [(hsk, 0), "flags"], [pk_])
                cp = (s - 1) * HP + 1 + SEG
                P.dma("scalar", lambda e, p_=p_, cp=cp: e.dma_start(out=hv[:, :, cp:cp + 1], in_=p_[:, :, :], allow_slow_non_contiguous=True), [pk_], [])
            if tin + 512 == SEG and s + 1 < NSEG:
                p_, pk_ = pdr.next()
                P.op("vector", lambda e, p_=p_, hs=hs, s=s: e.tensor_scalar(p_[:, :, :], hs[:, :, 511:512], flags[:, s + 1:s + 2], None, ALU.mult),
                     [(hsk, 3), "flags"], [pk_])
                cp = (s + 1) * HP
                P.dma("scalar", lambda e, p_=p_, cp=cp: e.dma_start(out=hv[:, :, cp:cp + 1], in_=p_[:, :, :], allow_slow_non_contiguous=True), [pk_], [])


def phase4b(nc, P, cfg, D, G):
    NTOK, SEG, HP = cfg.NTOK, cfg.SEG, cfg.HP
    with contextlib.ExitStack() as es:
        sb = lambda n, s, d: es.enter_context(nc.sbuf_tensor("s5_" + n, list(s), d))
        Wup = sb("Wup", [128, 8, 5632], BF16)
        Wdn = sb("Wdn", [128, 22, 1024], BF16)
        fcw = sb("fcw", [128, 44, 3], F32)
        fcb = sb("fcb", [128, 44], F32)
        act = sb("act", [128, 22, 256], BF16)
        h2 = Ring(es, nc, "s5_h2", [128, 8, 258], BF16, 2)
        x1r = Ring(es, nc, "s5_x1", [128, 1024], F32, 2)
        ua = Ring(es, nc, "s5_ua", [128, 256], F32, 3)
        ug = Ring(es, nc, "s5_ug", [128, 256], F32, 3)
        sg = Ring(es, nc, "s5_sg", [128, 256], F32, 3)
        orr = Ring(es, nc, "s5_o", [128, 1024], F32, 2)
        psU = Ring(es, nc, "s5_psU", [128, 512], F32, 4, psum=True)
        psD = Ring(es, nc, "s5_psD", [128, 512], F32, 2, psum=True)

        wup_v = D["w_up"].rearrange("(kc p) f -> p kc f", p=128)
        wdn_v = D["w_down"].rearrange("(fa p) d -> p fa d", p=128)
        for q in range(4):
            P.dma("gpsimd", lambda e, q=q: e.dma_start(out=Wup[:, :, q * 1408:(q + 1) * 1408], in_=wup_v[:, :, q * 1408:(q + 1) * 1408]),
                  [], [("Wup", q)])
        for q in range(2):
            P.dma("gpsimd", lambda e, q=q: e.dma_start(out=Wdn[:, q * 11:(q + 1) * 11, :], in_=wdn_v[:, q * 11:(q + 1) * 11, :]),
                  [], [("Wdn", q)])
        P.dma("sync", lambda e: e.dma_start(out=fcw[:, :, :], in_=D["fcw"].rearrange("p (c j) -> p c j", j=3)), [], ["fcw"])
        P.dma("sync", lambda e: e.dma_start(out=fcb[:, :], in_=D["fcb"]), [], ["fcb"])
        hv = D["h2T"].rearrange("(kc p) t -> p kc t", p=128)

        def load_h(tb):
            s, tin = divmod(tb, SEG)
            c0 = s * HP + tin
            h_, hk = h2.next()
            P.dma("sync", lambda e, h_=h_, c0=c0: e.dma_start(out=h_[:, :, :], in_=hv[:, :, c0:c0 + 258]), [], [hk])
            return h_, hk

        nxt = load_h(0)
        for tb in range(0, NTOK, 256):
            h_, hk = nxt
            if tb + 256 < NTOK:
                nxt = load_h(tb + 256)
            x1s = []
            for tt in range(2):
                x1, x1k = x1r.next()
                P.dma("sync", lambda e, x1=x1, t0=tb + tt * 128: e.dma_start(out=x1[:, :], in_=D["x1"][t0:t0 + 128, :]), [], [x1k])
                x1s.append((x1, x1k))
            pend = []
            for fa in range(22):
                res = []
                for which, cc in ((0, fa), (1, 22 + fa)):
                    pu, puk = psU.next()
                    for kc in range(8):
                        P.op("tensor", lambda e, pu=pu, h_=h_, kc=kc, cc=cc: e.matmul(
                            pu[:, 0:258], Wup[:, kc, cc * 128:(cc + 1) * 128], h_[:, kc, :], start=(kc == 0), stop=(kc == 7)),
                            [("Wup", (cc * 128) // 1408), ("Wup", (cc * 128 + 127) // 1408), hk], [puk])
                    u_, uk = (ua if which == 0 else ug).next()
                    P.op("scalar", lambda e, u_=u_, pu=pu, cc=cc: e.activation(
                        u_[:, :], pu[:, 1:257], AF.Identity, bias=fcb[:, cc:cc + 1], scale=fcw[:, cc, 1:2]), [puk, "fcw", "fcb"], [uk])
                    P.op("vector", lambda e, u_=u_, pu=pu, cc=cc: e.scalar_tensor_tensor(
                        u_[:, :], pu[:, 0:256], fcw[:, cc, 0:1], u_[:, :], ALU.mult, ALU.add), [puk, "fcw", uk], [uk])
                    P.op("vector", lambda e, u_=u_, pu=pu, cc=cc: e.scalar_tensor_tensor(
                        u_[:, :], pu[:, 2:258], fcw[:, cc, 2:3], u_[:, :], ALU.mult, ALU.add), [puk, "fcw", uk], [uk])
                    res.append((u_, uk))
                (a_, ak), (g_, gk) = res

                def fin(a_=a_, ak=ak, g_=g_, gk=gk, fa=fa):
                    s_, sk = sg.next()
                    P.op("scalar", lambda e, s_=s_, g_=g_: e.activation(s_[:, :], g_[:, :], AF.Silu), [gk], [sk])
                    P.op("gpsimd", lambda e, a_=a_, s_=s_, fa=fa: e.tensor_tensor(act[:, fa, :], a_[:, :], s_[:, :], ALU.mult),
                         [ak, sk], [("act", fa)])
                if pend:
                    pend.pop(0)()
                pend.append(fin)
            while pend:
                pend.pop(0)()
            for tt in range(2):
                t0 = tb + tt * 128
                x1, x1k = x1s[tt]
                o_, ok = orr.next()
                for dh in range(2):
                    pd_, pdk = psD.next()
                    for fa in range(22):
                        P.op("tensor", lambda e, pd_=pd_, fa=fa, tt=tt, dh=dh: e.matmul(
                            pd_[:, :], act[:, fa, tt * 128:(tt + 1) * 128], Wdn[:, fa, dh * 512:(dh + 1) * 512],
                            start=(fa == 0), stop=(fa == 21)), [("act", fa), ("Wdn", fa // 11)], [pdk])
                    P.op("vector", lambda e, o_=o_, pd_=pd_, x1=x1, dh=dh: e.tensor_tensor(
                        o_[:, dh * 512:(dh + 1) * 512], pd_[:, :], x1[:, dh * 512:(dh + 1) * 512], ALU.add), [pdk, x1k], [(ok, dh)])
                P.dma("sync", lambda e, o_=o_, t0=t0: e.dma_start(out=D["out"][t0:t0 + 128, :], in_=o_[:, :]),
                      [(ok, 0), (ok, 1)], [])


def ffn_shared_inputs(p):
    d = {}
    d["w_att_out"] = np.ascontiguousarray(p["w_att_out"][0])
    d["w_ssd_out"] = np.ascontiguousarray(p["w_ssd_out"][0])
    d["w_o"] = np.ascontiguousarray(p["w_o"][0])
    d["w_up"] = np.ascontiguousarray(p["w_up"][0])
    d["w_down"] = np.ascontiguousarray(p["w_down"][0])
    d["gffnT"] = colT(p["g_ffn"][0], 8)
    fw = p["ffn_conv_w"][0]
    d["fcw"] = np.ascontiguousarray(fw.reshape(3, 44, 128).transpose(2, 1, 0).reshape(128, 132))
    d["fcb"] = colT(p["ffn_conv_b"][0], 44)
    return d


N_CORES = 8
_NC_CACHE = {}


def full_cfg():
    return Cfg(nseg=5, seg=2048, chains=((0,), (1,), (2,), (3, 4)), tg=5120)


def core_segments(c):
    if c < 4:
        segs = [("p", 3 * c + i, 0) for i in range(3)] + [("s", c, 0), ("s", c, 2048)]
        linked = True
    else:
        segs = [("p", 12 + 5 * (c - 4) + i, 0) for i in range(5)]
        linked = False
    return segs, linked


def kernel(x_prompt, x_sample, g_mix, w_in, b_gate, q_norm_g, k_norm_g, rel_bias, ssd_conv_w,
           ssd_conv_b, dt_bias_f, dt_bias_b, A_log_f, A_log_b, D_skip, ssd_norm_g, w_att_out,
           w_ssd_out, w_o, g_ffn, w_up, ffn_conv_w, ffn_conv_b, w_down):
    p = dict(g_mix=g_mix, w_in=w_in, b_gate=b_gate, q_norm_g=q_norm_g, k_norm_g=k_norm_g, rel_bias=rel_bias,
             ssd_conv_w=ssd_conv_w, ssd_conv_b=ssd_conv_b, dt_bias_f=dt_bias_f, dt_bias_b=dt_bias_b,
             A_log_f=A_log_f, A_log_b=A_log_b, D_skip=D_skip, ssd_norm_g=ssd_norm_g, w_att_out=w_att_out,
             w_ssd_out=w_ssd_out, w_o=w_o, g_ffn=g_ffn, w_up=w_up, ffn_conv_w=ffn_conv_w, ffn_conv_b=ffn_conv_b,
             w_down=w_down)
    p = {k: np.asarray(v, np.float32) for k, v in p.items()}
    x_prompt = np.asarray(x_prompt, np.float32)
    x_sample = np.asarray(x_sample, np.float32)
    cfg = full_cfg()
    if "nc" not in _NC_CACHE:
        _NC_CACHE["nc"] = build(cfg)
    nc = _NC_CACHE["nc"]
    shared = {}
    shared.update(shared_inputs(p))
    shared.update(att_shared_inputs(p))
    shared.update(ffn_shared_inputs(p))
    in_maps = []
    for c in range(N_CORES):
        segs, linked = core_segments(c)
        xs = []
        for which, bi, off in segs:
            src = x_prompt if which == "p" else x_sample
            xs.append(src[bi, off:off + 2048])
        flags = np.zeros((128, 8), np.float32)
        if linked:
            flags[:, 4] = 1.0
        m = dict(shared)
        m["x"] = np.ascontiguousarray(np.concatenate(xs, 0))
        m["flags"] = flags
        m["rowmask"] = rowmask_table(cfg, flags[0])
        in_maps.append(m)
    res = run_bass_kernel_spmd(nc, in_maps, core_ids=list(range(N_CORES)))
    y_prompt = np.empty((32, 2048, 1024), np.float32)
    y_sample = np.empty((4, 4096, 1024), np.float32)
    for c in range(N_CORES):
        o = np.asarray(res.results[c]["out"], np.float32)
        segs, _ = core_segments(c)
        for i, (which, bi, off) in enumerate(segs):
            dst = y_prompt if which == "p" else y_sample
            dst[bi, off:off + 2048] = o[i * 2048:(i + 1) * 2048]
    return y_prompt, y_sample
```

```python
import contextlib
import numpy as np
import ml_dtypes
import concourse.bass as bass
import concourse.mybir as mybir
from concourse.bass_utils import run_bass_kernel_spmd

F32 = mybir.dt.float32
BF16 = mybir.dt.bfloat16
AF = mybir.ActivationFunctionType
ALU = mybir.AluOpType
AX = mybir.AxisListType

D_MODEL = 1024
GRID_W = 64
ATT_HEADS = 16
HEAD_DIM = 64
SSD_INNER = 2048
SSD_HEADS = 32
SSD_GROUPS = 8
D_FF = 2816
IN_W = 11328
EPS = 1e-6
NEG = -30000.0

C_Q, C_K, C_V, C_Z, C_XBC, C_DTF, C_DTB, C_GA, C_GS = 0, 1024, 2048, 3072, 5120, 9216, 9248, 9280, 10304


class _Op:
    __slots__ = ("eng", "fn", "deps", "is_dma", "needs_inc", "tok", "idx")


class Prog:
    ENGINES = ("tensor", "vector", "scalar", "gpsimd", "sync")

    def __init__(self, nc, n_dma_sems=12, dma_queues=("sync", "gpsimd", "scalar")):
        self.nc = nc
        self.ops = {e: [] for e in self.ENGINES}
        self.res = {}
        self.n_dma_sems = n_dma_sems
        self.dma_queues = dma_queues
        self.all_ops = []

    def _add(self, eng, fn, reads, writes, is_dma):
        op = _Op()
        op.eng, op.fn, op.is_dma, op.needs_inc, op.tok = eng, fn, is_dma, is_dma, None
        deps = []
        for r in reads:
            st = self.res.get(r)
            if st is not None and st[0] is not None:
                deps.append(st[0])
        for w in writes:
            st = self.res.get(w)
            if st is not None:
                if st[0] is not None:
                    deps.append(st[0])
                deps.extend(st[1])
        for r in reads:
            st = self.res.setdefault(r, [None, []])
            st[1].append(op)
        for w in writes:
            self.res[w] = [op, []]
        seen = set()
        op.deps = []
        for d in deps:
            if id(d) in seen or d is op:
                continue
            seen.add(id(d))
            if (not d.is_dma) and d.eng == eng == "tensor" and not is_dma:
                continue
            op.deps.append(d)
            d.needs_inc = True
        op.idx = len(self.all_ops)
        self.all_ops.append(op)
        self.ops[eng].append(op)
        return op

    def op(self, eng, fn, reads=(), writes=()):
        return self._add(eng, fn, reads, writes, False)

    def dma(self, eng, fn, reads=(), writes=()):
        return self._add(eng, fn, reads, writes, True)

    def setup(self, es):
        nc = self.nc
        self.csem = {e: es.enter_context(nc.semaphore(f"c_{e}")) for e in ("tensor", "vector", "scalar", "gpsimd")}
        self.dsem = {q: [es.enter_context(nc.semaphore(f"d_{q}{i}")) for i in range(self.n_dma_sems)]
                     for q in self.dma_queues}
        self.cnt = {e: 0 for e in self.csem}
        self.dcnt = {q: [0] * self.n_dma_sems for q in self.dma_queues}
        self.drr = {q: 0 for q in self.dma_queues}

    def emit(self):
        nc = self.nc
        csem, dsem, cnt, dcnt, drr = self.csem, self.dsem, self.cnt, self.dcnt, self.drr
        prev_tok = {}
        for op in self.all_ops:
            if op.is_dma:
                q = op.eng
                i = drr[q]
                drr[q] = (i + 1) % self.n_dma_sems
                prev = dcnt[q][i]
                dcnt[q][i] += 16
                op.tok = (dsem[q][i], dcnt[q][i], ("d", q, i))
                prev_tok[id(op)] = (dsem[q][i], prev, ("d", q, i)) if prev > 0 else None
            elif op.needs_inc:
                cnt[op.eng] += 1
                op.tok = (csem[op.eng], cnt[op.eng], ("c", op.eng))
        n_dma_sems, dma_queues = self.n_dma_sems, self.dma_queues
        all_ops_by_eng = self.ops

        def make(engname):
            ops = all_ops_by_eng[engname]

            def body(eng):
                waited = {}

                def wait(tok):
                    if tok is None:
                        return
                    sem, val, key = tok
                    if waited.get(key, 0) >= val:
                        return
                    eng.wait_ge(sem, val)
                    waited[key] = val

                for op in ops:
                    for d in op.deps:
                        wait(d.tok)
                    if op.is_dma:
                        wait(prev_tok[id(op)])
                        op.fn(eng).then_inc(op.tok[0], 16)
                    else:
                        ins = op.fn(eng)
                        if op.needs_inc:
                            ins.then_inc(op.tok[0], 1)
                if engname == "sync":
                    for q in dma_queues:
                        for i in range(n_dma_sems):
                            if dcnt[q][i] > 0:
                                wait((dsem[q][i], dcnt[q][i], ("d", q, i)))
                    for e in csem:
                        if cnt[e] > 0:
                            wait((csem[e], cnt[e], ("c", e)))
            return body

        with nc.Block() as blk:
            blk.tensor(make("tensor"))
            blk.vector(make("vector"))
            blk.scalar(make("scalar"))
            blk.gpsimd(make("gpsimd"))
            blk.sync(make("sync"))
        self.n_emitted = getattr(self, "n_emitted", 0) + len(self.all_ops)
        self.ops = {e: [] for e in self.ENGINES}
        self.res = {}
        self.all_ops = []


class Ring:
    def __init__(self, es, nc, name, shape, dtype, n, psum=False):
        alloc = nc.psum_tensor if psum else nc.sbuf_tensor
        self.t = [es.enter_context(alloc(f"{name}{i}", list(shape), dtype)) for i in range(n)]
        self.k = [f"{name}{i}" for i in range(n)]
        self.i = 0

    def next(self):
        i = self.i
        self.i = (i + 1) % len(self.t)
        return self.t[i], self.k[i]


class SlotRing:
    def __init__(self, items):
        self.items = items
        self.i = 0

    def next(self):
        it = self.items[self.i]
        self.i = (self.i + 1) % len(self.items)
        return it


class Cfg:
    def __init__(self, nseg=5, seg=2048, chains=((0,), (1,), (2,), (3, 4)), tg=5120, debug=False, stop_after=9):
        self.NSEG, self.SEG, self.chains, self.debug = nseg, seg, chains, debug
        self.stop_after = stop_after
        self.NTOK = nseg * seg
        self.TG = min(tg, self.NTOK)
        self.XP = seg + 4
        self.HP = seg + 2
        self.NT = self.NTOK // 128
        self.TPS = seg // 128
        assert self.NTOK % self.TG == 0 and self.TG % 512 == 0 and seg % 512 == 0


K_IDENT, K_BLK64, K_LE, K_GT, K_GE, K_LT, K_ONES, NK = 0, 1, 2, 3, 4, 5, 6, 7


def host_consts():
    k = np.arange(128)[:, None]
    i = np.arange(128)[None, :]
    m = np.zeros((NK, 128, 128), np.float32)
    m[K_IDENT] = (k == i)
    m[K_BLK64] = ((k // 64) == (i // 64)) / 64.0
    m[K_LE] = (k <= i)
    m[K_GT] = (k > i)
    m[K_GE] = (k >= i)
    m[K_LT] = (k < i)
    m[K_ONES] = 1.0
    return np.ascontiguousarray(m.transpose(1, 0, 2).reshape(128, NK * 128))


LAST_INPUT_NAMES = []


def build(cfg):
    nc = bass.Bass("TRN2", target_bir_lowering=False)
    NTOK, SEG, NSEG = cfg.NTOK, cfg.SEG, cfg.NSEG
    dbg = cfg.debug
    skind = "ExternalOutput" if dbg else "Internal"

    LAST_INPUT_NAMES.clear()

    def din(name, shape, dt=F32):
        LAST_INPUT_NAMES.append(name)
        return nc.dram_tensor(name, list(shape), dt, kind="ExternalInput").ap()

    def dscr(name, shape, dt):
        return nc.dram_tensor(name, list(shape), dt, kind=skind).ap()

    D = {}
    D["x"] = din("x", [NTOK, 1024])
    D["w_in"] = din("w_in", [1024, IN_W])
    D["consts"] = din("consts", [128, NK * 128])
    D["gmixT"] = din("gmixT", [128, 8])
    D["bgT"] = din("bgT", [128, 16])
    D["qkg"] = din("qkg", [128, 2])
    D["dtbias"] = din("dtbias", [128, 64])
    D["flags"] = din("flags", [128, 8])
    D["convw"] = din("convw", [128, 160])
    D["convbc"] = din("convbc", [128, 32])
    D["convbr"] = din("convbr", [1, 4096])
    D["alog"] = din("alog", [128, 64])
    D["dskip"] = din("dskip", [128, 32])
    D["normg"] = din("normg", [128, 2048])
    D["biasx"] = din("biasx", [16, ND, 128, 128])
    D["cmask"] = din("cmask", [ND, 128, 128])
    D["halfsel"] = din("halfsel", [128, 128], BF16)
    D["rowmask"] = din("rowmask", [2, cfg.NT * 768], BF16)
    D["w_att_out"] = din("w_att_out", [1024, 1024])
    D["w_ssd_out"] = din("w_ssd_out", [2048, 1024])
    D["w_o"] = din("w_o", [1024, 1024])
    D["w_up"] = din("w_up", [1024, 5632])
    D["w_down"] = din("w_down", [2816, 1024])
    D["gffnT"] = din("gffnT", [128, 8])
    D["fcw"] = din("fcw", [128, 132])
    D["fcb"] = din("fcb", [128, 44])
    D["qT"] = dscr("qT", [1024, NTOK], BF16)
    D["kT"] = dscr("kT", [1024, NTOK], BF16)
    D["v"] = dscr("v", [NTOK, 1024], BF16)
    D["zs"] = dscr("zs", [NTOK, 2048], BF16)
    D["xbcT"] = dscr("xbcT", [4096, NSEG * cfg.XP], BF16)
    D["dt"] = dscr("dt", [NTOK, 64], F32)
    D["gT"] = dscr("gT", [2048, NTOK], BF16)
    D["yT"] = dscr("yT", [2048, NTOK], BF16)
    D["attT"] = dscr("attT", [1024, NTOK], BF16)
    D["x1"] = dscr("x1", [NTOK, 1024], F32)
    D["h2T"] = dscr("h2T", [1024, NSEG * cfg.HP], BF16)
    D["out"] = nc.dram_tensor("out", [NTOK, 1024], F32, kind="ExternalOutput").ap()

    with contextlib.ExitStack() as ges:
        P = Prog(nc)
        P.setup(ges)
        kc_f = ges.enter_context(nc.sbuf_tensor("kc_f", [128, NK * 128], F32))
        kc_b = ges.enter_context(nc.sbuf_tensor("kc_b", [128, NK * 128], BF16))
        flags = ges.enter_context(nc.sbuf_tensor("flags_sb", [128, 8], F32))
        P.dma("sync", lambda e: e.dma_start(out=kc_f[:, :], in_=D["consts"]), [], ["kc_f"])
        P.dma("sync", lambda e: e.dma_start(out=flags[:, :], in_=D["flags"]), [], ["flags"])
        P.op("vector", lambda e: e.tensor_copy(kc_b[:, :], kc_f[:, :]), ["kc_f"], ["kc_b"])
        G = dict(kc_b=kc_b, kc_f=kc_f, flags=flags)
        phase1(nc, P, cfg, D, G)
        P.emit()
        if cfg.stop_after >= 2:
            phase2(nc, P, cfg, D, G)
            P.emit()
        if cfg.stop_after >= 3:
            phase3(nc, P, cfg, D, G)
            P.emit()
        if cfg.stop_after >= 4:
            phase4a(nc, P, cfg, D, G)
            P.emit()
        if cfg.stop_after >= 5:
            phase4b(nc, P, cfg, D, G)
            P.emit()
    return nc


def KB(G, k):
    return G["kc_b"][:, k * 128:(k + 1) * 128]


def phase1(nc, P, cfg, D, G):
    NTOK, SEG, TG = cfg.NTOK, cfg.SEG, cfg.TG
    with contextlib.ExitStack() as es:
        sb = lambda n, s, d: es.enter_context(nc.sbuf_tensor(n, list(s), d))
        hT = sb("hT", [128, 8, TG], BF16)
        gmixT = sb("gmixT_sb", [128, 8], F32)
        bgT = sb("bgT_sb", [128, 16], F32)
        qkg = sb("qkg_sb", [128, 2], F32)
        qkgs = sb("qkgs_sb", [128, 2], F32)
        dtb = sb("dtb_sb", [128, 64], F32)
        zpad = sb("zpad", [128, 32, 2], BF16)
        xin = Ring(es, nc, "xin", [128, 1024], F32, 2)
        xn = Ring(es, nc, "xn", [128, 1024], BF16, 2)
        junk = Ring(es, nc, "junk", [128, 1024], BF16, 1)
        st = Ring(es, nc, "st", [128, 4], F32, 2)
        Wr = Ring(es, nc, "W", [128, 8, 512], BF16, 2)
        sq = Ring(es, nc, "sq", [128, 512], BF16, 2)
        lnb = Ring(es, nc, "lnb", [128, 512], F32, 2)
        ob = Ring(es, nc, "ob", [128, 512], BF16, 4)
        pd = Ring(es, nc, "pd", [128, 4], BF16, 4)
        dts = Ring(es, nc, "dts", [128, 64], F32, 2)
        psT = Ring(es, nc, "psT", [128, 8, 128], BF16, 2, psum=True)
        psM = Ring(es, nc, "psM", [128, 512], F32, 4, psum=True)
        psN = Ring(es, nc, "psN", [128, 512], F32, 2, psum=True)

        P.dma("sync", lambda e: e.dma_start(out=gmixT[:, :], in_=D["gmixT"]), [], ["gmixT"])
        P.dma("sync", lambda e: e.dma_start(out=bgT[:, :], in_=D["bgT"]), [], ["bgT"])
        P.dma("sync", lambda e: e.dma_start(out=qkg[:, :], in_=D["qkg"]), [], ["qkg"])
        P.dma("sync", lambda e: e.dma_start(out=dtb[:, :], in_=D["dtbias"]), [], ["dtb"])
        P.op("vector", lambda e: e.tensor_scalar(qkgs[:, 0:1], qkg[:, 0:1], HEAD_DIM ** -0.5, None, ALU.mult),
             ["qkg"], ["qkgs"])
        P.op("vector", lambda e: e.tensor_copy(qkgs[:, 1:2], qkg[:, 1:2]), ["qkg", "qkgs"], ["qkgs"])
        P.op("vector", lambda e: e.memset(zpad[:, :, :], 0.0), [], ["zpad"])
        xv = D["xbcT"].rearrange("(cc p) t -> p cc t", p=128)
        XP = cfg.XP
        P.dma("gpsimd", lambda e: e.dma_start(out=xv[:, :, 0:2], in_=zpad[:, :, :]), ["zpad"], [])
        P.dma("gpsimd", lambda e: e.dma_start(out=xv[:, :, cfg.NSEG * XP - 2:cfg.NSEG * XP], in_=zpad[:, :, :]),
              ["zpad"], [])

        wv = D["w_in"].rearrange("(kc p) c -> p kc c", p=128)
        ident = KB(G, K_IDENT)
        blk64 = KB(G, K_BLK64)

        def load_w(c0, ncols):
            Wt, Wk = Wr.next()
            P.dma("gpsimd", lambda e: e.dma_start(out=Wt[:, :, 0:ncols], in_=wv[:, :, c0:c0 + ncols]), [], [Wk])
            return Wt, Wk

        for tg0 in range(0, NTOK, TG):
            for ti in range(TG // 128):
                t0 = tg0 + ti * 128
                xt, xk = xin.next()
                P.dma("sync", lambda e, xt=xt, t0=t0: e.dma_start(out=xt[:, :], in_=D["x"][t0:t0 + 128, :]), [], [xk])
                s_, sk = st.next()
                jt, jk = junk.next()
                P.op("scalar", lambda e, jt=jt, xt=xt, s_=s_: e.activation(jt[:, :], xt[:, :], AF.Square, accum_out=s_[:, 0:1]),
                     [xk], [jk, sk])
                P.op("scalar", lambda e, s_=s_: e.activation(s_[:, 1:2], s_[:, 0:1], AF.Ln, bias=EPS, scale=1.0 / 1024),
                     [sk], [sk])
                P.op("scalar", lambda e, s_=s_: e.activation(s_[:, 2:3], s_[:, 1:2], AF.Exp, scale=-0.5), [sk], [sk])
                xnt, xnk = xn.next()
                P.op("vector", lambda e, xnt=xnt, xt=xt, s_=s_: e.tensor_scalar(xnt[:, :], xt[:, :], s_[:, 2:3], None, ALU.mult),
                     [xk, sk], [xnk])
                pt, pk = psT.next()
                for kc in range(8):
                    P.op("tensor", lambda e, pt=pt, xnt=xnt, kc=kc: e.transpose(pt[:, kc, :], xnt[:, kc * 128:(kc + 1) * 128], ident),
                         [xnk, "kc_b"], [pk])
                P.op("vector", lambda e, pt=pt, ti=ti: e.tensor_tensor(
                    hT[:, :, ti * 128:(ti + 1) * 128], pt[:, :, :],
                    gmixT[:, :].unsqueeze(2).to_broadcast([128, 8, 128]), ALU.mult),
                    [pk, "gmixT"], [("hT", ti)])

            def hkeys(ta, tb):
                return [("hT", i) for i in range(ta // 128, tb // 128)]

            fjobs = [("q", C_Q, 1024), ("k", C_K, 1024), ("xbc", C_XBC, 4096), ("g", C_GA, 2048)]
            pend = []
            for name, cbase, ctot in fjobs:
                for cb in range(0, ctot, 512):
                    Wt, Wk = load_w(cbase + cb, 512)
                    for tb in range(0, TG, 512):
                        tglob = tg0 + tb
                        for cc in range(4):
                            f0 = cb + cc * 128
                            pm, pmk = psM.next()
                            for kc in range(8):
                                P.op("tensor", lambda e, pm=pm, Wt=Wt, cc=cc, kc=kc, tb=tb: e.matmul(
                                    pm[:, :], Wt[:, kc, cc * 128:(cc + 1) * 128], hT[:, kc, tb:tb + 512],
                                    start=(kc == 0), stop=(kc == 7)), [Wk] + hkeys(tb, tb + 512), [pmk])
                            while len(pend) > 0 and (name not in ("q", "k") or len(pend) > 1 or True):
                                pend.pop(0)()
                            if name not in ("q", "k"):
                                o, ok = ob.next()
                            if name in ("q", "k"):
                                col = 0 if name == "q" else 1
                                s2, s2k = sq.next()
                                P.op("scalar", lambda e, s2=s2, pm=pm: e.activation(s2[:, :], pm[:, :], AF.Square), [pmk], [s2k])
                                def fin(pm=pm, pmk=pmk, s2=s2, s2k=s2k, col=col, f0=f0, tglob=tglob, name=name):
                                    o, ok = ob.next()
                                    pn, pnk = psN.next()
                                    P.op("tensor", lambda e, pn=pn, s2=s2: e.matmul(pn[:, :], blk64, s2[:, :], start=True, stop=True),
                                         [s2k, "kc_b"], [pnk])
                                    lb, lbk = lnb.next()
                                    P.op("scalar", lambda e, lb=lb, pn=pn: e.activation(lb[:, :], pn[:, :], AF.Ln, bias=EPS), [pnk], [lbk])
                                    P.op("scalar", lambda e, lb=lb: e.activation(lb[:, :], lb[:, :], AF.Exp, scale=-0.5), [lbk], [lbk])
                                    P.op("vector", lambda e, o=o, pm=pm, lb=lb, col=col: e.scalar_tensor_tensor(
                                        o[:, :], pm[:, :], qkgs[:, col:col + 1], lb[:, :], ALU.mult, ALU.mult),
                                        [pmk, lbk, "qkgs"], [ok])
                                    dst = D["qT" if name == "q" else "kT"]
                                    P.dma("sync", lambda e, o=o, dst=dst, f0=f0, tglob=tglob: e.dma_start(
                                        out=dst[f0:f0 + 128, tglob:tglob + 512], in_=o[:, :]), [ok], [])
                                pend.append(fin)
                            elif name == "g":
                                gi = f0 // 128
                                P.op("scalar", lambda e, o=o, pm=pm, gi=gi: e.activation(
                                    o[:, :], pm[:, :], AF.Sigmoid, bias=bgT[:, gi:gi + 1]), [pmk, "bgT"], [ok])
                                P.dma("sync", lambda e, o=o, f0=f0, tglob=tglob: e.dma_start(
                                    out=D["gT"][f0:f0 + 128, tglob:tglob + 512], in_=o[:, :]), [ok], [])
                            else:
                                P.op("vector", lambda e, o=o, pm=pm: e.tensor_copy(o[:, :], pm[:, :]), [pmk], [ok])
                                s = tglob // SEG
                                tin = tglob % SEG
                                c0 = s * XP + 2 + tin
                                P.dma("sync", lambda e, o=o, f0=f0, c0=c0: e.dma_start(
                                    out=D["xbcT"][f0:f0 + 128, c0:c0 + 512], in_=o[:, :]), [ok], [])
                                if tin == 0 and s > 0:
                                    p_, pk_ = pd.next()
                                    P.op("vector", lambda e, p_=p_, pm=pm, s=s: e.tensor_scalar(
                                        p_[:, 0:2], pm[:, 0:2], G["flags"][:, s:s + 1], None, ALU.mult), [pmk, "flags"], [pk_])
                                    cp = (s - 1) * XP + 2 + SEG
                                    P.dma("sync", lambda e, p_=p_, f0=f0, cp=cp: e.dma_start(
                                        out=D["xbcT"][f0:f0 + 128, cp:cp + 2], in_=p_[:, 0:2]), [pk_], [])
                                if tin + 512 == SEG and s + 1 < cfg.NSEG:
                                    p_, pk_ = pd.next()
                                    P.op("vector", lambda e, p_=p_, pm=pm, s=s: e.tensor_scalar(
                                        p_[:, 0:2], pm[:, 510:512], G["flags"][:, s + 1:s + 2], None, ALU.mult), [pmk, "flags"], [pk_])
                                    cp = (s + 1) * XP
                                    P.dma("sync", lambda e, p_=p_, f0=f0, cp=cp: e.dma_start(
                                        out=D["xbcT"][f0:f0 + 128, cp:cp + 2], in_=p_[:, 0:2]), [pk_], [])
            tjobs = [("v", C_V, 1024), ("z", C_Z, 2048), ("dt", C_DTF, 64)]
            for name, cbase, ctot in tjobs:
                for cb in range(0, ctot, 512):
                    ncols = min(512, ctot - cb)
                    Wt, Wk = load_w(cbase + cb, ncols)
                    for tt in range(0, TG, 128):
                        tglob = tg0 + tt
                        pm, pmk = psM.next()
                        for kc in range(8):
                            P.op("tensor", lambda e, pm=pm, Wt=Wt, kc=kc, tt=tt, ncols=ncols: e.matmul(
                                pm[:, 0:ncols], hT[:, kc, tt:tt + 128], Wt[:, kc, 0:ncols],
                                start=(kc == 0), stop=(kc == 7)), [Wk, ("hT", tt // 128)], [pmk])
                        if name == "dt":
                            d_, dk = dts.next()
                            P.op("vector", lambda e, d_=d_, pm=pm: e.tensor_tensor(d_[:, :], pm[:, 0:64], dtb[:, :], ALU.add),
                                 [pmk, "dtb"], [dk])
                            P.op("scalar", lambda e, d_=d_: e.activation(d_[:, :], d_[:, :], AF.Exp), [dk], [dk])
                            P.op("scalar", lambda e, d_=d_: e.activation(d_[:, :], d_[:, :], AF.Ln, bias=1.0), [dk], [dk])
                            P.dma("sync", lambda e, d_=d_, tglob=tglob: e.dma_start(
                                out=D["dt"][tglob:tglob + 128, :], in_=d_[:, :]), [dk], [])
                        else:
                            o, ok = ob.next()
                            if name == "v":
                                P.op("vector", lambda e, o=o, pm=pm: e.tensor_copy(o[:, :], pm[:, :]), [pmk], [ok])
                                P.dma("sync", lambda e, o=o, cb=cb, tglob=tglob: e.dma_start(
                                    out=D["v"][tglob:tglob + 128, cb:cb + 512], in_=o[:, :]), [ok], [])
                            else:
                                P.op("scalar", lambda e, o=o, pm=pm: e.activation(o[:, :], pm[:, :], AF.Silu), [pmk], [ok])
                                P.dma("sync", lambda e, o=o, cb=cb, tglob=tglob: e.dma_start(
                                    out=D["zs"][tglob:tglob + 128, cb:cb + 512], in_=o[:, :]), [ok], [])


def phase2(nc, P, cfg, D, G):
    SEG, TPS, XP = cfg.SEG, cfg.TPS, cfg.XP
    maxc = max(len(c) for c in cfg.chains)
    with contextlib.ExitStack() as es:
        sb = lambda n, s, d: es.enter_context(nc.sbuf_tensor("s2_" + n, list(s), d))
        convw = sb("convw", [128, 32, 5], F32)
        convbc = sb("convbc", [128, 32], F32)
        convbr_r = Ring(es, nc, "s2_convbr", [1, 384], BF16, 2)
        alog = sb("alog", [128, 64], F32)
        Arow = sb("Arow", [128, 64], F32)
        dsk = sb("dsk", [128, 32], F32)
        normg_r = Ring(es, nc, "s2_normg", [128, 256], F32, 2)
        diag = Ring(es, nc, "s2_diag", [128, 4, 5, 128], BF16, 2)
        xin = Ring(es, nc, "s2_xinc", [128, 4, XP], BF16, 2)
        xb_tm = [sb(f"xb_tm{i}", [128, TPS, 384], BF16) for i in range(maxc)]
        BT = [sb(f"BT{i}", [128, SEG], BF16) for i in range(maxc)]
        CT = [sb(f"CT{i}", [128, SEG], BF16) for i in range(maxc)]
        prevb = [sb(f"prevb{i}", [128, TPS, 256], BF16) for i in range(maxc)]
        zs = [sb(f"zs{i}", [128, TPS, 256], BF16) for i in range(maxc)]
        dtseg = [sb(f"dtseg{i}", [128, TPS, 64], F32) for i in range(maxc)]
        dtA = [sb(f"dtA{i}", [128, TPS, 8], F32) for i in range(maxc)]
        dtAb = [sb(f"dtAb{i}", [128, TPS, 8], BF16) for i in range(maxc)]
        decs = [sb(f"decs{i}", [128, 5, TPS, 8], F32) for i in range(maxc)]
        dtee = [sb(f"dtee{i}", [128, TPS, 8], F32) for i in range(maxc)]
        lndt = [sb(f"lndt{i}", [128, TPS, 8], F32) for i in range(maxc)]
        diagD_r = Ring(es, nc, "s2_diagD", [128, 4, 128], BF16, 2)
        Hfs = [sb(f"Hf{i}", [128, 256], F32) for i in range(2)]
        Hbs = [sb(f"Hb{i}", [128, 256], F32) for i in range(2)]
        prevf = [sb(f"prevf{i}", [128, TPS, 256], BF16) for i in range(maxc)]
        Ht = Ring(es, nc, "s2_Ht", [128, 256], F32, 3)
        xdeb = Ring(es, nc, "s2_xdeb", [128, 256], BF16, 4)
        sm = Ring(es, nc, "s2_sm", [128, 2, 128], F32, 3)
        Lr = Ring(es, nc, "s2_Lseg", [128, 4, 128], BF16, 4)
        Er = Ring(es, nc, "s2_Eseg", [128, 4, 128], F32, 4)
        MTr = Ring(es, nc, "s2_MT", [128, 4, 128], BF16, 4)
        ya = Ring(es, nc, "s2_ya", [128, 256], F32, 4)
        yb = Ring(es, nc, "s2_yb", [128, 256], F32, 5)
        yy = Ring(es, nc, "s2_yy", [128, 256], F32, 4)
        yj = Ring(es, nc, "s2_yj", [128, 256], BF16, 1)
        yst = Ring(es, nc, "s2_yst", [128, 4], F32, 4)
        yn = Ring(es, nc, "s2_yn", [128, 256], BF16, 3)
        ystage = Ring(es, nc, "s2_ystage", [128, 2, 512], BF16, 2)
        psG = Ring(es, nc, "s2_ps2g", [128, 512], F32, 4, psum=True)
        bank_sc = es.enter_context(nc.psum_tensor("s2_bank_sc", [128, 512], F32))
        bank_o = es.enter_context(nc.psum_tensor("s2_bank_o", [128, 512], F32))
        bank_y = es.enter_context(nc.psum_tensor("s2_bank_y", [128, 512], F32))
        bank_t = es.enter_context(nc.psum_tensor("s2_bank_t", [128, 2, 128], BF16))
        sc_slots = SlotRing([(bank_sc[:, 0:128], "s2_sc")])
        o_slots = SlotRing([(bank_o, "s2_o")])
        y_slots = SlotRing([(bank_y[:, 0:256], "s2_y")])
        t_slots = SlotRing([(bank_t[:, :, :], "s2_t")])

        P.dma("sync", lambda e: e.dma_start(out=convw[:, :, :], in_=D["convw"].rearrange("p (c j) -> p c j", j=5)), [], ["convw"])
        P.dma("sync", lambda e: e.dma_start(out=convbc[:, :], in_=D["convbc"]), [], ["convbc"])
        P.dma("sync", lambda e: e.dma_start(out=alog[:, :], in_=D["alog"]), [], ["alog"])
        P.dma("sync", lambda e: e.dma_start(out=dsk[:, :], in_=D["dskip"]), [], ["dsk"])
        P.op("scalar", lambda e: e.activation(Arow[:, :], alog[:, :], AF.Exp), ["alog"], ["Arow"])
        P.op("vector", lambda e: e.tensor_scalar(Arow[:, :], Arow[:, :], -1.0, None, ALU.mult), ["Arow"], ["Arow"])

        ident = KB(G, K_IDENT)
        kf = lambda k: G["kc_f"][:, k * 128:(k + 1) * 128]
        ones_row = G["kc_b"][0:1, K_ONES * 128:(K_ONES + 1) * 128]
        xv = D["xbcT"].rearrange("(cc p) t -> p cc t", p=128)
        yTv = D["yT"].rearrange("(cc p) t -> p cc t", p=128)
        TRIS = (K_LE, K_GT, K_GE, K_LT, K_ONES)
        I_LE, I_GT, I_GE, I_LT, I_ONES = 0, 1, 2, 3, 4
        flags = G["flags"]

        for g in range(8):
            ccs = (2 * g, 2 * g + 1, 16 + g, 24 + g)
            dg, dgk = diag.next()
            convbr, cbk = convbr_r.next()
            dD, dDk = diagD_r.next()
            for h in range(4):
                P.op("vector", lambda e, dD=dD, h=h, g=g: e.tensor_scalar(
                    dD[:, h, :], ident, dsk[:, 4 * g + h:4 * g + h + 1], None, ALU.mult), ["kc_b", "dsk"], [(dDk, h)])
            normg, ngk = normg_r.next()
            for ci in range(3):
                P.dma("gpsimd", lambda e, convbr=convbr, ci=ci, ch0=ccs[ci] * 128: e.dma_start(
                    out=convbr[:, ci * 128:(ci + 1) * 128], in_=D["convbr"][:, ch0:ch0 + 128]), [], [(cbk, ci)])
            P.dma("sync", lambda e, normg=normg, g=g: e.dma_start(out=normg[:, :], in_=D["normg"][:, g * 256:(g + 1) * 256]), [], [ngk])
            for ci, cc in enumerate(ccs):
                for j in range(5):
                    P.op("vector", lambda e, dg=dg, ci=ci, cc=cc, j=j: e.tensor_scalar(
                        dg[:, ci, j, :], ident, convw[:, cc, j:j + 1], None, ALU.mult), ["kc_b", "convw"], [dgk])
            for chain in cfg.chains:
                for li, s in enumerate(chain):
                    xt, xk = xin.next()
                    c0 = s * XP
                    P.dma("sync", lambda e, xt=xt, c0=c0, g=g: e.dma_start(out=xt[:, 0:2, :], in_=xv[:, 2 * g:2 * g + 2, c0:c0 + XP]), [], [xk])
                    P.dma("sync", lambda e, xt=xt, c0=c0, g=g: e.dma_start(out=xt[:, 2, :], in_=xv[:, 16 + g, c0:c0 + XP]), [xk], [xk])
                    P.dma("sync", lambda e, xt=xt, c0=c0, g=g: e.dma_start(out=xt[:, 3, :], in_=xv[:, 24 + g, c0:c0 + XP]), [xk], [xk])
                    t0 = s * SEG
                    P.dma("sync", lambda e, li=li, t0=t0, g=g: e.dma_start(
                        out=zs[li][:, :, :], in_=D["zs"][t0:t0 + SEG, g * 256:(g + 1) * 256].rearrange("(c p) f -> p c f", p=128)),
                        [], [("zs", li)])
                    P.dma("sync", lambda e, li=li, t0=t0: e.dma_start(
                        out=dtseg[li][:, :, :], in_=D["dt"][t0:t0 + SEG, :].rearrange("(c p) f -> p c f", p=128)), [], [("dt", li)])
                    for d in range(2):
                        h0 = d * 32 + 4 * g
                        P.op("vector", lambda e, li=li, d=d, h0=h0: e.tensor_tensor(
                            dtA[li][:, :, d * 4:d * 4 + 4], dtseg[li][:, :, h0:h0 + 4],
                            Arow[:, h0:h0 + 4].unsqueeze(1).to_broadcast([128, TPS, 4]), ALU.mult),
                            [("dt", li), "Arow"], [("dtA", li)])
                    P.op("vector", lambda e, li=li: e.tensor_copy(dtAb[li][:, :, :], dtA[li][:, :, :]), [("dtA", li)], [("dtAb", li)])
                    for d in range(2):
                        h0 = d * 32 + 4 * g
                        P.op("scalar", lambda e, li=li, d=d, h0=h0: e.activation(
                            lndt[li][:, :, d * 4:d * 4 + 4], dtseg[li][:, :, h0:h0 + 4], AF.Ln), [("dt", li)], [("lndt", li, d)])
                    pa, pak = psG.next()
                    pb, pbk = psG.next()
                    for ti, tk in enumerate(TRIS):
                        pt_, ptk = (pa, pak) if ti < 4 else (pb, pbk)
                        o0 = (ti % 4) * TPS * 8
                        P.op("tensor", lambda e, pt_=pt_, o0=o0, tk=tk, li=li: e.matmul(
                            pt_[:, o0:o0 + TPS * 8], KB(G, tk), dtAb[li][:, :, :].rearrange("p c h -> p (c h)"),
                            start=True, stop=True), [("dtAb", li), "kc_b"], [ptk])
                    P.op("scalar", lambda e, li=li, pa=pa: e.activation(
                        decs[li][:, 0:4, :, :].rearrange("p a c h -> p (a c h)"), pa[:, 0:4 * TPS * 8], AF.Exp), [pak], [("decs", li)])
                    P.op("scalar", lambda e, li=li, pb=pb: e.activation(
                        decs[li][:, 4, :, :].rearrange("p c h -> p (c h)"), pb[:, 0:TPS * 8], AF.Exp), [pbk, ("decs", li)], [("decs", li)])
                    for d, tri in ((0, I_GT), (1, I_LT)):
                        h0 = d * 32 + 4 * g
                        P.op("vector", lambda e, li=li, d=d, h0=h0, tri=tri: e.tensor_tensor(
                            dtee[li][:, :, d * 4:d * 4 + 4], dtseg[li][:, :, h0:h0 + 4],
                            decs[li][:, tri, :, d * 4:d * 4 + 4], ALU.mult), [("dt", li), ("decs", li)], [("dtee", li)])
                    for c in range(TPS):
                        pc, pck = psG.next()
                        for ci in range(3):
                            ch0 = ccs[ci] * 128
                            for j in range(5):
                                P.op("tensor", lambda e, pc=pc, xt=xt, ci=ci, j=j, c=c, dg=dg: e.matmul(
                                    pc[:, ci * 128:(ci + 1) * 128], xt[:, ci, c * 128 + j:c * 128 + j + 128], dg[:, ci, j, :],
                                    start=(j == 0), stop=False), [xk, dgk], [pck])
                            P.op("tensor", lambda e, pc=pc, ci=ci, convbr=convbr: e.matmul(
                                pc[:, ci * 128:(ci + 1) * 128], ones_row, convbr[0:1, ci * 128:(ci + 1) * 128], start=False, stop=True),
                                ["kc_b", (cbk, ci)], [pck])
                        P.op("scalar", lambda e, pc=pc, li=li, c=c: e.activation(xb_tm[li][:, c, :], pc[:, 0:384], AF.Silu),
                             [pck], [("xb", li, c)])
                    for tb in range(0, SEG, 512):
                        for ci, dstl, nm in ((2, BT, "BT"), (3, CT, "CT")):
                            pf, pfk = psG.next()
                            for j in range(5):
                                P.op("tensor", lambda e, pf=pf, xt=xt, ci=ci, j=j, tb=tb, dg=dg: e.matmul(
                                    pf[:, :], dg[:, ci, j, :], xt[:, ci, tb + j:tb + j + 512], start=(j == 0), stop=(j == 4)),
                                    [xk, dgk], [pfk])
                            cc = ccs[ci]
                            P.op("scalar", lambda e, pf=pf, dstl=dstl, li=li, tb=tb, cc=cc: e.activation(
                                dstl[li][:, tb:tb + 512], pf[:, :], AF.Silu, bias=convbc[:, cc:cc + 1]),
                                [pfk, "convbc"], [(nm, li, tb // 512)])

                order = [(li, c) for li in range(len(chain)) for c in range(TPS)]
                n_ord = len(order)
                bc = lambda ap: ap.unsqueeze(2).to_broadcast([128, 4, 64])
                r4 = lambda ap: ap.rearrange("p (h q) -> p h q", h=4)

                def state_step(d, li, c, first, bnd_flag):
                    Hs, nm = (Hfs, "Hf") if d == 0 else (Hbs, "Hb")
                    prev = prevf if d == 0 else prevb
                    cur = hpar[d]
                    Hc, Hn = Hs[cur], Hs[1 - cur]
                    if first:
                        P.op("vector", lambda e, Hc=Hc: e.memset(Hc[:, :], 0.0), [], [(nm, cur)])
                    if bnd_flag is not None:
                        P.op("vector", lambda e, Hc=Hc, f=bnd_flag: e.tensor_scalar(
                            Hc[:, :], Hc[:, :], flags[:, f:f + 1], None, ALU.mult), [(nm, cur), "flags"], [(nm, cur)])
                    P.op("scalar", lambda e, Hc=Hc, prev=prev, li=li, c=c: e.activation(prev[li][:, c, :], Hc[:, :], AF.Copy),
                         [(nm, cur)], [("prev", d, li, c)])
                    xd, xdk = xdeb.next()
                    P.op("gpsimd", lambda e, xd=xd, li=li, c=c, d=d: e.tensor_tensor(
                        r4(xd[:, :]), r4(xb_tm[li][:, c, 0:256]), bc(dtee[li][:, c, d * 4:d * 4 + 4]), ALU.mult),
                        [("xb", li, c), ("dtee", li)], [xdk])
                    ps, psk = psG.next()
                    P.op("tensor", lambda e, ps=ps, xd=xd, li=li, c=c: e.matmul(
                        ps[:, 0:256], xb_tm[li][:, c, 256:384], xd[:, :], start=True, stop=True), [("xb", li, c), xdk], [psk])
                    ht, htk = Ht.next()
                    P.op("vector", lambda e, ht=ht, Hc=Hc, li=li, c=c, d=d: e.tensor_tensor(
                        r4(ht[:, :]), r4(Hc[:, :]), bc(decs[li][:, I_ONES, c, d * 4:d * 4 + 4]), ALU.mult),
                        [(nm, cur), ("decs", li)], [htk])
                    P.op("vector", lambda e, ht=ht, ps=ps, Hn=Hn: e.tensor_tensor(Hn[:, :], ht[:, :], ps[:, 0:256], ALU.add),
                         [htk, psk], [(nm, 1 - cur)])
                    hpar[d] = 1 - cur

                hpar = [0, 0]
                for i in range(n_ord):
                    li, c = order[i]
                    fl = chain[li] if (c == 0 and li > 0) else None
                    state_step(0, li, c, i == 0, fl)
                    li2, c2 = order[n_ord - 1 - i]
                    fl2 = chain[li2 + 1] if (c2 == TPS - 1 and li2 + 1 < len(chain)) else None
                    state_step(1, li2, c2, i == 0, fl2)

                stC = {}

                def c0(i):
                    li, c = order[i]
                    S = stC[i] = {"Ls": []}
                    for d, trix in enumerate((K_GT, K_LT)):
                        L, Lk = Lr.next()
                        P.op("gpsimd" if d else "vector", lambda e, L=L, d=d, trix=trix, li=li, c=c: e.tensor_tensor(
                            L[:, :, :], kf(trix).unsqueeze(1).to_broadcast([128, 4, 128]),
                            dtA[li][:, c, d * 4:d * 4 + 4].unsqueeze(2).to_broadcast([128, 4, 128]), ALU.mult),
                            ["kc_f", ("dtA", li)], [Lk])
                        S["Ls"].append((L, Lk))

                def c1(i):
                    li, c = order[i]
                    S = stC[i]
                    tok = slice(c * 128, (c + 1) * 128)
                    pS, pSk = sc_slots.next()
                    P.op("tensor", lambda e, pS=pS, li=li, tok=tok: e.matmul(
                        pS, BT[li][:, tok], CT[li][:, tok], start=True, stop=True),
                        [("BT", li, c // 4), ("CT", li, c // 4)], [pSk])
                    pO, pOk = o_slots.next()
                    P.op("tensor", lambda e, pO=pO, li=li, tok=tok, c=c: e.matmul(
                        pO[:, 0:256], CT[li][:, tok], prevf[li][:, c, :], start=True, stop=True),
                        [("CT", li, c // 4), ("prev", 0, li, c)], [pOk])
                    P.op("tensor", lambda e, pO=pO, li=li, tok=tok, c=c: e.matmul(
                        pO[:, 256:512], CT[li][:, tok], prevb[li][:, c, :], start=True, stop=True),
                        [("CT", li, c // 4), ("prev", 1, li, c)], [pOk])
                    pgs = []
                    for d, triy in enumerate((K_LE, K_GE)):
                        L, Lk = S["Ls"][d]
                        pg, pgk = psG.next()
                        for h in range(4):
                            P.op("tensor", lambda e, pg=pg, L=L, h=h, triy=triy: e.matmul(
                                pg[:, h * 128:(h + 1) * 128], L[:, h, :], KB(G, triy), start=True, stop=True), [Lk, "kc_b"], [pgk])
                        pgs.append((pg, pgk))
                    S.update(pS=pS, pSk=pSk, pO=pO, pOk=pOk, pgs=pgs)

                def c2(i):
                    li, c = order[i]
                    S = stC[i]
                    pS, pSk, pO, pOk = S["pS"], S["pSk"], S["pO"], S["pOk"]
                    sm_, smk = sm.next()
                    P.op("vector", lambda e, sm_=sm_, pS=pS: e.tensor_tensor(sm_[:, 0, :], pS, kf(K_LE), ALU.mult),
                         [pSk, "kc_f"], [(smk, 0)])
                    P.op("vector", lambda e, sm_=sm_, pS=pS: e.tensor_tensor(sm_[:, 1, :], pS, kf(K_GE), ALU.mult),
                         [pSk, "kc_f"], [(smk, 1)])
                    a_, ak = ya.next()
                    b_, bk = yb.next()
                    P.op("vector", lambda e, a_=a_, pO=pO, li=li, c=c: e.tensor_tensor(
                        r4(a_[:, :]), r4(pO[:, 0:256]), bc(decs[li][:, I_LE, c, 0:4]), ALU.mult), [pOk, ("decs", li)], [ak])
                    P.op("vector", lambda e, b_=b_, pO=pO, li=li, c=c: e.tensor_tensor(
                        r4(b_[:, :]), r4(pO[:, 256:512]), bc(decs[li][:, I_GE, c, 4:8]), ALU.mult), [pOk, ("decs", li)], [bk])
                    Es = []
                    for d in range(2):
                        pg, pgk = S["pgs"][d]
                        E, Ek = Er.next()
                        for h in range(4):
                            P.op("scalar", lambda e, E=E, pg=pg, h=h, li=li, c=c, d=d: e.activation(
                                E[:, h, :], pg[:, h * 128:(h + 1) * 128], AF.Exp, bias=lndt[li][:, c, d * 4 + h:d * 4 + h + 1]),
                                [pgk, ("lndt", li, d)], [(Ek, h)])
                        Es.append((E, Ek))
                    S.update(sm_=sm_, smk=smk, a_=a_, ak=ak, b_=b_, bk=bk, Es=Es)

                def c3(i):
                    S = stC[i]
                    sm_, smk = S["sm_"], S["smk"]
                    MTs = []
                    for d in range(2):
                        E, Ek = S["Es"][d]
                        MT, MTk = MTr.next()
                        P.op("vector", lambda e, MT=MT, E=E, sm_=sm_, d=d: e.tensor_tensor(
                            MT[:, :, :], E[:, :, :], sm_[:, d, :].unsqueeze(1).to_broadcast([128, 4, 128]), ALU.mult),
                            [(Ek, 0), (Ek, 1), (Ek, 2), (Ek, 3), (smk, d)], [MTk])
                        MTs.append((MT, MTk))
                    S["MTs"] = MTs

                def c4(i):
                    li, c = order[i]
                    S = stC[i]
                    pY, pYk = y_slots.next()
                    for h in range(4):
                        xs_h = xb_tm[li][:, c, h * 64:(h + 1) * 64]
                        P.op("tensor", lambda e, pY=pY, h=h, xs_h=xs_h, dD=dD: e.matmul(
                            pY[:, h * 64:(h + 1) * 64], dD[:, h, :], xs_h, start=(h == 0), stop=False),
                            [(dDk, h), ("xb", li, c)], [pYk])
                        for d in range(2):
                            MT, MTk = S["MTs"][d]
                            P.op("tensor", lambda e, pY=pY, MT=MT, xs_h=xs_h, h=h, d=d: e.matmul(
                                pY[:, h * 64:(h + 1) * 64], MT[:, h, :], xs_h,
                                start=False, stop=(h == 3 and d == 1)), [MTk, ("xb", li, c)], [pYk])
                    S.update(pY=pY, pYk=pYk)

                def c5(i):
                    S = stC[i]
                    y_, yk = yy.next()
                    P.op("vector", lambda e, y_=y_, pY=S["pY"], a_=S["a_"]: e.tensor_tensor(y_[:, :], pY, a_[:, :], ALU.add),
                         [S["pYk"], S["ak"]], [yk])
                    S.update(y_=y_, yk=yk)

                def c6(i):
                    li, c = order[i]
                    S = stC[i]
                    y_, yk = S["y_"], S["yk"]
                    P.op("gpsimd", lambda e, y_=y_, b_=S["b_"]: e.tensor_tensor(y_[:, :], y_[:, :], b_[:, :], ALU.add), [yk, S["bk"]], [yk])
                    P.op("gpsimd", lambda e, y_=y_, li=li, c=c: e.tensor_tensor(y_[:, :], y_[:, :], zs[li][:, c, :], ALU.mult),
                         [yk, ("zs", li)], [yk])

                def c7(i):
                    S = stC[i]
                    y_, yk = S["y_"], S["yk"]
                    jt, jk = yj.next()
                    s_, sk = yst.next()
                    P.op("scalar", lambda e, jt=jt, y_=y_, s_=s_: e.activation(jt[:, :], y_[:, :], AF.Square, accum_out=s_[:, 0:1]),
                         [yk], [jk, sk])
                    P.op("scalar", lambda e, s_=s_: e.activation(s_[:, 1:2], s_[:, 0:1], AF.Ln, bias=EPS, scale=1.0 / 256), [sk], [sk])
                    P.op("scalar", lambda e, s_=s_: e.activation(s_[:, 2:3], s_[:, 1:2], AF.Exp, scale=-0.5), [sk], [sk])
                    S.update(s_=s_, sk=sk)

                def c8(i):
                    S = stC[i]
                    n_, nk = yn.next()
                    P.op("vector", lambda e, n_=n_, y_=S["y_"], s_=S["s_"], normg=normg: e.scalar_tensor_tensor(
                        n_[:, :], y_[:, :], s_[:, 2:3], normg[:, :], ALU.mult, ALU.mult),
                        [S["yk"], S["sk"], ngk], [nk])
                    S.update(n_=n_, nk=nk)

                def c9(i):
                    S = stC[i]
                    n_, nk = S["n_"], S["nk"]
                    pT, pTk = t_slots.next()
                    for q in range(2):
                        P.op("tensor", lambda e, pT=pT, n_=n_, q=q: e.transpose(pT[:, q, :], n_[:, q * 128:(q + 1) * 128], ident),
                             [nk, "kc_b"], [pTk])
                    S.update(pT=pT, pTk=pTk)

                def c10(i):
                    li, c = order[i]
                    S = stC.pop(i)
                    pT, pTk = S["pT"], S["pTk"]
                    if c % 4 == 0:
                        ysr[0] = ystage.next()
                    ys, ysk = ysr[0]
                    P.op("scalar", lambda e, ys=ys, pT=pT, c=c: e.activation(ys[:, :, (c % 4) * 128:(c % 4 + 1) * 128], pT, AF.Copy),
                         [pTk], [(ysk, c % 4)])
                    if c % 4 == 3:
                        tg = chain[li] * SEG + (c - 3) * 128
                        P.dma("scalar", lambda e, ys=ys, tg=tg, g=g: e.dma_start(
                            out=yTv[:, 2 * g:2 * g + 2, tg:tg + 512], in_=ys[:, :, :]), [(ysk, q) for q in range(4)], [])

                ysr = [None]
                stages = [c0, c1, c2, c3, c4, c5, c6, c7, c8, c9, c10]
                NS = len(stages)
                sorder = [2, 5, 10] + [s for s in reversed(range(NS)) if s not in (2, 5, 10)]
                for t in range(n_ord + NS - 1):
                    for s in sorder:
                        if 0 <= t - s < n_ord:
                            stages[s](t - s)


def rep128(v):
    v = np.asarray(v, np.float32).reshape(1, -1)
    return np.ascontiguousarray(np.broadcast_to(v, (128, v.shape[1])))


def colT(v, n):
    return np.ascontiguousarray(np.asarray(v, np.float32).reshape(n, 128).T)


def shared_inputs(p):
    d = {}
    d["w_in"] = np.ascontiguousarray(p["w_in"][0])
    d["consts"] = host_consts()
    d["gmixT"] = colT(p["g_mix"][0], 8)
    d["bgT"] = colT(p["b_gate"][0], 16)
    d["qkg"] = np.ascontiguousarray(np.stack([np.tile(p["q_norm_g"][0], 2), np.tile(p["k_norm_g"][0], 2)], 1).astype(np.float32))
    d["dtbias"] = rep128(np.concatenate([p["dt_bias_f"][0], p["dt_bias_b"][0]]))
    cw = p["ssd_conv_w"][0]
    d["convw"] = np.ascontiguousarray(cw.reshape(5, 32, 128).transpose(2, 1, 0).reshape(128, 160))
    d["convbc"] = colT(p["ssd_conv_b"][0], 32)
    d["convbr"] = np.ascontiguousarray(p["ssd_conv_b"][0].reshape(1, 4096))
    d["alog"] = rep128(np.concatenate([p["A_log_f"][0], p["A_log_b"][0]]))
    d["dskip"] = rep128(p["D_skip"][0])
    d["normg"] = rep128(p["ssd_norm_g"][0])
    return d


ND = 7


def att_slots(t, tps):
    if t == 0:
        return list(range(-2, 4))
    if t == tps - 1:
        return list(range(-3, 3))
    return list(range(-2, 3))


def bias_gather_index():
    a = np.arange(128)[:, None] // 64
    kc = np.arange(128)[:, None] % 64
    b = np.arange(128)[None, :] // 64
    qc = np.arange(128)[None, :] % 64
    cs = np.clip(qc - 8, 0, 48)
    col_ok = (kc >= cs) & (kc < cs + 16)
    ridx = np.zeros((ND, 128, 128), np.int64)
    cidx = np.zeros((ND, 128, 128), np.int64)
    ok = np.zeros((ND, 128, 128), bool)
    for di in range(ND):
        dr = 2 * (di - 3) + a - b
        dc = kc - qc
        ok[di] = col_ok & (np.abs(dr) <= 7) & (np.abs(dc) <= 15)
        ridx[di] = np.clip(dr + 7, 0, 14)
        cidx[di] = np.clip(dc + 15, 0, 30)
    return ridx, cidx, ok


def att_shared_inputs(p):
    ridx, cidx, ok = bias_gather_index()
    rb = np.asarray(p["rel_bias"][0], np.float32)
    d = {}
    d["biasx"] = np.ascontiguousarray(rb[:, ridx, cidx])
    d["cmask"] = np.ascontiguousarray(np.where(ok, 0.0, NEG).astype(np.float32))
    hs = np.zeros((128, 128), np.float32)
    hs[0, :64] = 1.0
    hs[1, 64:] = 1.0
    d["halfsel"] = hs.astype(ml_dtypes.bfloat16)
    return d


def rowmask_table(cfg, flags_vec):
    tps, nt = cfg.TPS, cfg.NT
    seq_start = np.zeros(cfg.NSEG, np.int64)
    seq_len = np.zeros(cfg.NSEG, np.int64)
    s = 0
    while s < cfg.NSEG:
        e = s
        while e + 1 < cfg.NSEG and flags_vec[e + 1] > 0:
            e += 1
        for k in range(s, e + 1):
            seq_start[k] = s
            seq_len[k] = e - s + 1
        s = e + 1
    tab = np.full((2, nt, 6, 128), NEG, np.float32)
    bq = np.arange(128) // 64
    for T in range(nt):
        sg, t = divmod(T, tps)
        t0 = seq_start[sg] * tps
        rows = seq_len[sg] * tps * 2
        qr = (T - t0) * 2 + bq
        rs = np.clip(qr - 4, 0, rows - 8)
        for si, dlt in enumerate(att_slots(t, tps)):
            KT = T + dlt
            if KT < t0 or KT >= t0 + seq_len[sg] * tps:
                continue
            for a in range(2):
                kr = (KT - t0) * 2 + a
                tab[a, T, si] = np.where((kr >= rs) & (kr < rs + 8), 0.0, NEG)
    return np.ascontiguousarray(tab.reshape(2, nt * 6 * 128)).astype(ml_dtypes.bfloat16)


def phase3(nc, P, cfg, D, G):
    NT, TPS = cfg.NT, cfg.TPS
    NB = NT // 4
    with contextlib.ExitStack() as es:
        sb = lambda n, s, d: es.enter_context(nc.sbuf_tensor("s3_" + n, list(s), d))
        biasE = sb("biasE", [128, 16, ND, 128], BF16)
        bstage = Ring(es, nc, "s3_bst", [128, ND, 128], F32, 2)
        cmask = sb("cmask", [128, ND, 128], F32)
        halfsel = sb("halfsel", [128, 128], BF16)
        Kr = [sb(f"K{i}", [128, 8, 512], BF16) for i in range(4)]
        Vr = [sb(f"V{i}", [128, 4, 16, 65], BF16) for i in range(4)]
        Qr = Ring(es, nc, "s3_Q", [128, 8, 2, 512], BF16, 2)
        rmr = Ring(es, nc, "s3_rm", [128, 6, 128], BF16, 4)
        Ep = Ring(es, nc, "s3_Ep", [128, 6, 128], BF16, 3)
        Eb = Ring(es, nc, "s3_Eb", [128, 6, 128], BF16, 5)
        rec = Ring(es, nc, "s3_rec", [128, 1], F32, 4)
        atm = Ring(es, nc, "s3_atm", [128, 1024], BF16, 3)
        ast = Ring(es, nc, "s3_ast", [128, 8, 512], BF16, 2)
        psAB = Ring(es, nc, "s3_psAB", [128, 1024], F32, 2, psum=True)
        psO = Ring(es, nc, "s3_psO", [128, 512], F32, 2, psum=True)
        psT = Ring(es, nc, "s3_psT", [128, 8, 128], BF16, 1, psum=True)
        ident = KB(G, K_IDENT)

        P.dma("sync", lambda e: e.dma_start(out=cmask[:, :, :], in_=D["cmask"].rearrange("d k q -> k d q")), [], ["cmask"])
        P.dma("sync", lambda e: e.dma_start(out=halfsel[:, :], in_=D["halfsel"]), [], ["halfsel"])
        for h in range(16):
            bs, bsk = bstage.next()
            P.dma("sync", lambda e, h=h, bs=bs: e.dma_start(out=bs[:, :, :], in_=D["biasx"][h].rearrange("d k q -> k d q")),
                  [], [bsk])
            P.op("vector", lambda e, bs=bs: e.tensor_tensor(bs[:, :, :], bs[:, :, :], cmask[:, :, :], ALU.add),
                 [bsk, "cmask"], [bsk])
            P.op("scalar", lambda e, h=h, bs=bs: e.activation(biasE[:, h, :, :], bs[:, :, :], AF.Exp), [bsk], [("biasE", h)])
        for i in range(4):
            P.op("gpsimd", lambda e, i=i: e.memset(Vr[i][:, :, :, :], 1.0), [], [("V", i)])
        for i in range(2):
            P.op("vector", lambda e, i=i: e.memset(Qr.t[i][:, :, :, :], 0.0), [], [Qr.k[i]])
        for i in range(4):
            P.op("vector", lambda e, i=i: e.memset(rmr.t[i][:, :, :], 0.0), [], [rmr.k[i]])

        qTv = D["qT"].rearrange("(hp two p) t -> two p hp t", two=2, p=64)
        kTv = D["kT"].rearrange("(hp p) t -> p hp t", p=128)
        aTv = D["attT"].rearrange("(cc p) t -> p cc t", p=128)
        loaded = set()

        def load_kv(b):
            if b < 0 or b >= NB or b in loaded:
                return
            loaded.add(b)
            i = b % 4
            P.dma("sync", lambda e, b=b, i=i: e.dma_start(out=Kr[i][:, :, :], in_=kTv[:, :, b * 512:(b + 1) * 512]), [], [("K", i)])
            for c in range(4):
                P.dma("sync", lambda e, b=b, i=i, c=c: e.dma_start(
                    out=Vr[i][:, c, :, 0:64],
                    in_=D["v"][b * 512 + c * 128:b * 512 + (c + 1) * 128, :].rearrange("p (h d) -> p h d", d=64)),
                    [("V", i)], [("V", i)])

        items = [(b, tt, h) for b in range(NB) for tt in range(4) for h in range(16)]
        stT = {}
        stH = {}

        def part_a(k):
            b, tt, h = items[k]
            T = b * 4 + tt
            if tt == 0 and h == 0:
                load_kv(b - 1)
                load_kv(b)
                load_kv(b + 1)
                Qt, Qk = Qr.next()
                for hh in range(2):
                    P.dma("sync", lambda e, Qt=Qt, b=b, hh=hh: e.dma_start(
                        out=Qt[hh * 64:(hh + 1) * 64, :, hh, :], in_=qTv[hh][:, :, b * 512:(b + 1) * 512]), [Qk], [Qk])
                stT[("b", b)] = (Qt, Qk) + ast.next()
            Qt, Qk, as_, ask = stT[("b", b)]
            t = T % TPS
            slots = att_slots(t, TPS)
            nsl = len(slots)
            if h == 0:
                rm, rmk = rmr.next()
                P.dma("sync", lambda e, rm=rm, T=T: e.dma_start(
                    out=rm[0:2, :, :], in_=D["rowmask"][:, T * 768:(T + 1) * 768].rearrange("a (s q) -> a s q", q=128)), [rmk], [rmk])
                stT[T] = (rm, rmk) + atm.next()
            rm, rmk, at, atk = stT[T]
            hp, hh = divmod(h, 2)
            pr = slice(hh * 64, hh * 64 + 64)
            pab, pabk = psAB.next()
            kts = []
            for si, dlt in enumerate(slots):
                KT = min(max(T + dlt, 0), NT - 1)
                kb, kt = divmod(KT, 4)
                kts.append((kb % 4, kt))
                o0 = si * 128
                P.op("tensor", lambda e, pab=pab, o0=o0, kb=kb, kt=kt, hp=hp, hh=hh, Qt=Qt, tt=tt: e.matmul(
                    pab[:, o0:o0 + 128], Kr[kb % 4][:, hp, kt * 128:(kt + 1) * 128], Qt[:, hp, hh, tt * 128:(tt + 1) * 128],
                    start=True, stop=False), [("K", kb % 4), Qk], [pabk])
                P.op("tensor", lambda e, pab=pab, o0=o0, rm=rm, si=si: e.matmul(
                    pab[:, o0:o0 + 128], halfsel[:, :], rm[:, si, :], start=False, stop=True), ["halfsel", rmk], [pabk])
            ep, epk = Ep.next()
            d0 = slots[0] + 3
            P.op("scalar", lambda e, ep=ep, pab=pab, nsl=nsl: e.activation(
                ep[:, 0:nsl, :], pab[:, 0:nsl * 128].rearrange("p (s q) -> p s q", q=128), AF.Exp), [pabk], [epk])
            eb, ebk = Eb.next()
            P.op("vector", lambda e, eb=eb, ep=ep, h=h, d0=d0, nsl=nsl: e.tensor_tensor(
                eb[:, 0:nsl, :], ep[:, 0:nsl, :], biasE[:, h, d0:d0 + nsl, :], ALU.mult), [epk, ("biasE", h)], [ebk])
            stH[k] = (eb, ebk, kts, nsl)

        def part_b(k):
            b, tt, h = items[k]
            T = b * 4 + tt
            eb, ebk, kts, nsl = stH.pop(k)
            rm, rmk, at, atk = stT[T]
            Qt, Qk, as_, ask = stT[("b", b)]
            po, pok = psO.next()
            for si in range(nsl):
                vb, vt = kts[si]
                P.op("tensor", lambda e, po=po, eb=eb, si=si, vb=vb, vt=vt, h=h, nsl=nsl: e.matmul(
                    po[:, 0:65], eb[:, si, :], Vr[vb][:, vt, h, :], start=(si == 0), stop=(si == nsl - 1)),
                    [ebk, ("V", vb)], [pok])
            rc, rck = rec.next()
            P.op("vector", lambda e, rc=rc, po=po: e.reciprocal(rc[:, :], po[:, 64:65]), [pok], [rck])
            P.op("vector", lambda e, at=at, po=po, rc=rc, h=h: e.tensor_scalar(
                at[:, h * 64:(h + 1) * 64], po[:, 0:64], rc[:, 0:1], None, ALU.mult), [pok, rck], [(atk, h)])
            if h == 15:
                pt, ptk2 = psT.next()
                for cc in range(8):
                    P.op("tensor", lambda e, pt=pt, at=at, cc=cc: e.transpose(pt[:, cc, :], at[:, cc * 128:(cc + 1) * 128], ident),
                         [(atk, 2 * cc), (atk, 2 * cc + 1), "kc_b"], [ptk2])
                P.op("scalar", lambda e, as_=as_, pt=pt, tt=tt: e.activation(as_[:, :, tt * 128:(tt + 1) * 128], pt[:, :, :], AF.Copy),
                     [ptk2], [(ask, tt)])
                del stT[T]
                if tt == 3:
                    P.dma("scalar", lambda e, as_=as_, b=b: e.dma_start(out=aTv[:, :, b * 512:(b + 1) * 512], in_=as_[:, :, :]),
                          [(ask, q) for q in range(4)], [])
                    del stT[("b", b)]

        SK = 2
        for k in range(len(items) + SK):
            if k < len(items):
                part_a(k)
            if k - SK >= 0:
                part_b(k - SK)


def phase4a(nc, P, cfg, D, G):
    NTOK, SEG, HP, NSEG = cfg.NTOK, cfg.SEG, cfg.HP, cfg.NSEG
    with contextlib.ExitStack() as es:
        sb = lambda n, s, d: es.enter_context(nc.sbuf_tensor("s4_" + n, list(s), d))
        Wao = sb("Wao", [128, 8, 1024], BF16)
        Wso = sb("Wso", [128, 16, 1024], BF16)
        Wo = sb("Wo", [128, 8, 1024], BF16)
        gffn = sb("gffn", [128, 8], F32)
        zc = sb("zc", [128, 8, 1], BF16)
        aT = Ring(es, nc, "s4_aT", [128, 8, 512], BF16, 2)
        yT = Ring(es, nc, "s4_yT", [128, 16, 512], BF16, 2)
        gT = Ring(es, nc, "s4_gT", [128, 16, 512], BF16, 2)
        xr = Ring(es, nc, "s4_x", [128, 1024], F32, 2)
        t1 = Ring(es, nc, "s4_t1", [128, 512], F32, 2)
        t2 = Ring(es, nc, "s4_t2", [128, 512], F32, 2)
        mT = Ring(es, nc, "s4_mT", [128, 8, 512], BF16, 1)
        x1r = Ring(es, nc, "s4_x1", [128, 1024], F32, 2)
        junk = Ring(es, nc, "s4_junk", [128, 1024], BF16, 1)
        st = Ring(es, nc, "s4_st", [128, 4], F32, 2)
        xn = Ring(es, nc, "s4_xn", [128, 1024], BF16, 2)
        hst = Ring(es, nc, "s4_hst", [128, 8, 512], BF16, 2)
        pdr = Ring(es, nc, "s4_pd", [128, 8, 1], BF16, 4)
        psA = Ring(es, nc, "s4_psA", [128, 512], F32, 2, psum=True)
        psB = Ring(es, nc, "s4_psB", [128, 512], F32, 2, psum=True)
        psX = Ring(es, nc, "s4_psX", [128, 512], F32, 2, psum=True)
        psT = Ring(es, nc, "s4_psT", [128, 8, 128], BF16, 2, psum=True)
        ident = KB(G, K_IDENT)
        flags = G["flags"]

        wao_v = D["w_att_out"].rearrange("(kc p) d -> p kc d", p=128)
        wso_v = D["w_ssd_out"].rearrange("(kc p) d -> p kc d", p=128)
        wo_v = D["w_o"].rearrange("(kc p) d -> p kc d", p=128)
        P.dma("gpsimd", lambda e: e.dma_start(out=Wao[:, :, :], in_=wao_v), [], ["Wao"])
        P.dma("gpsimd", lambda e: e.dma_start(out=Wso[:, 0:8, :], in_=wso_v[:, 0:8, :]), [], [("Wso", 0)])
        P.dma("gpsimd", lambda e: e.dma_start(out=Wso[:, 8:16, :], in_=wso_v[:, 8:16, :]), [], [("Wso", 1)])
        P.dma("gpsimd", lambda e: e.dma_start(out=Wo[:, :, :], in_=wo_v), [], ["Wo"])
        P.dma("sync", lambda e: e.dma_start(out=gffn[:, :], in_=D["gffnT"]), [], ["gffn"])
        P.op("vector", lambda e: e.memset(zc[:, :, :], 0.0), [], ["zc"])
        hv = D["h2T"].rearrange("(kc p) t -> p kc t", p=128)
        P.dma("gpsimd", lambda e: e.dma_start(out=hv[:, :, 0:1], in_=zc[:, :, :], allow_slow_non_contiguous=True), ["zc"], [])
        P.dma("gpsimd", lambda e: e.dma_start(out=hv[:, :, NSEG * HP - 1:NSEG * HP], in_=zc[:, :, :], allow_slow_non_contiguous=True), ["zc"], [])
        aTv = D["attT"].rearrange("(cc p) t -> p cc t", p=128)
        yTv = D["yT"].rearrange("(cc p) t -> p cc t", p=128)
        gTv = D["gT"].rearrange("(cc p) t -> p cc t", p=128)

        def load_blk(tb):
            a_, ak = aT.next()
            y_, yk = yT.next()
            g_, gk = gT.next()
            P.dma("sync", lambda e, a_=a_, tb=tb: e.dma_start(out=a_[:, :, :], in_=aTv[:, :, tb:tb + 512]), [], [ak])
            P.dma("sync", lambda e, y_=y_, tb=tb: e.dma_start(out=y_[:, :, :], in_=yTv[:, :, tb:tb + 512]), [], [yk])
            P.dma("sync", lambda e, g_=g_, tb=tb: e.dma_start(out=g_[:, :, :], in_=gTv[:, :, tb:tb + 512]), [], [gk])
            return a_, ak, y_, yk, g_, gk

        nxt = load_blk(0)
        for tb in range(0, NTOK, 512):
            a_, ak, y_, yk, g_, gk = nxt
            if tb + 512 < NTOK:
                nxt = load_blk(tb + 512)
            m_, mk = mT.next()
            for dmc in range(8):
                pa, pak = psA.next()
                pb, pbk = psB.next()
                for kc in range(8):
                    P.op("tensor", lambda e, pa=pa, a_=a_, kc=kc, dmc=dmc: e.matmul(
                        pa[:, :], Wao[:, kc, dmc * 128:(dmc + 1) * 128], a_[:, kc, :], start=(kc == 0), stop=(kc == 7)),
                        ["Wao", ak], [pak])
                for kc in range(16):
                    P.op("tensor", lambda e, pb=pb, y_=y_, kc=kc, dmc=dmc: e.matmul(
                        pb[:, :], Wso[:, kc, dmc * 128:(dmc + 1) * 128], y_[:, kc, :], start=(kc == 0), stop=(kc == 15)),
                        [("Wso", kc // 8), yk], [pbk])
                u1, u1k = t1.next()
                u2, u2k = t2.next()
                P.op("vector", lambda e, u1=u1, pa=pa, g_=g_, dmc=dmc: e.tensor_tensor(u1[:, :], pa[:, :], g_[:, dmc, :], ALU.mult),
                     [pak, gk], [u1k])
                P.op("vector", lambda e, u2=u2, pb=pb, g_=g_, dmc=dmc: e.tensor_tensor(u2[:, :], pb[:, :], g_[:, 8 + dmc, :], ALU.mult),
                     [pbk, gk], [u2k])
                P.op("gpsimd", lambda e, m_=m_, u1=u1, u2=u2, dmc=dmc: e.tensor_tensor(m_[:, dmc, :], u1[:, :], u2[:, :], ALU.add),
                     [u1k, u2k], [(mk, dmc)])
            hs, hsk = hst.next()
            pend = []
            for tt in range(4):
                t0 = tb + tt * 128
                x_, xk = xr.next()
                P.dma("sync", lambda e, x_=x_, t0=t0: e.dma_start(out=x_[:, :], in_=D["x"][t0:t0 + 128, :]), [], [xk])
                x1, x1k = x1r.next()
                for dh in range(2):
                    px, pxk = psX.next()
                    for kc in range(8):
                        P.op("tensor", lambda e, px=px, m_=m_, kc=kc, tt=tt, dh=dh: e.matmul(
                            px[:, :], m_[:, kc, tt * 128:(tt + 1) * 128], Wo[:, kc, dh * 512:(dh + 1) * 512],
                            start=(kc == 0), stop=(kc == 7)), [(mk, kc), "Wo"], [pxk])
                    P.op("vector", lambda e, x1=x1, px=px, x_=x_, dh=dh: e.tensor_tensor(
                        x1[:, dh * 512:(dh + 1) * 512], px[:, :], x_[:, dh * 512:(dh + 1) * 512], ALU.add), [pxk, xk], [(x1k, dh)])
                def fin(x1=x1, x1k=x1k, tt=tt, hs=hs, hsk=hsk, t0=t0):
                    P.dma("scalar", lambda e, x1=x1, t0=t0: e.dma_start(out=D["x1"][t0:t0 + 128, :], in_=x1[:, :]),
                          [(x1k, 0), (x1k, 1)], [])
                    jt, jk = junk.next()
                    s_, sk = st.next()
                    P.op("scalar", lambda e, jt=jt, x1=x1, s_=s_: e.activation(jt[:, :], x1[:, :], AF.Square, accum_out=s_[:, 0:1]),
                         [(x1k, 0), (x1k, 1)], [jk, sk])
                    P.op("scalar", lambda e, s_=s_: e.activation(s_[:, 1:2], s_[:, 0:1], AF.Ln, bias=EPS, scale=1.0 / 1024), [sk], [sk])
                    P.op("scalar", lambda e, s_=s_: e.activation(s_[:, 2:3], s_[:, 1:2], AF.Exp, scale=-0.5), [sk], [sk])
                    n_, nk = xn.next()
                    P.op("scalar", lambda e, n_=n_, x1=x1, s_=s_: e.activation(n_[:, :], x1[:, :], AF.Copy, scale=s_[:, 2:3]),
                         [(x1k, 0), (x1k, 1), sk], [nk])
                    pt, ptk = psT.next()
                    for kc in range(8):
                        P.op("tensor", lambda e, pt=pt, n_=n_, kc=kc: e.transpose(pt[:, kc, :], n_[:, kc * 128:(kc + 1) * 128], ident),
                             [nk, "kc_b"], [ptk])
                    P.op("vector", lambda e, hs=hs, pt=pt, tt=tt: e.tensor_tensor(
                        hs[:, :, tt * 128:(tt + 1) * 128], pt[:, :, :], gffn[:, :].unsqueeze(2).to_broadcast([128, 8, 128]), ALU.mult),
                        [ptk, "gffn"], [(hsk, tt)])
                if pend:
                    pend.pop(0)()
                pend.append(fin)
            while pend:
                pend.pop(0)()
            s, tin = divmod(tb, SEG)
            c0 = s * HP + 1 + tin
            hkeys = [(hsk, i) for i in range(4)]
            P.dma("scalar", lambda e, hs=hs, c0=c0: e.dma_start(out=hv[:, :, c0:c0 + 512], in_=hs[:, :, :]), hkeys, [])
            if tin == 0 and s > 0:
                p_, pk_ = pdr.next()
                P.op("vector", lambda e, p_=p_, hs=hs, s=s: e.tensor_scalar(p_[:, :, :], hs[:, :, 0:1], flags[:, s:s + 1], None, ALU.mult),
                     [(hsk, 0), "flags"], [pk_])
                cp = (s - 1) * HP + 1 + SEG
                P.dma("scalar", lambda e, p_=p_, cp=cp: e.dma_start(out=hv[:, :, cp:cp + 1], in_=p_[:, :, :], allow_slow_non_contiguous=True), [pk_], [])
            if tin + 512 == SEG and s + 1 < NSEG:
                p_, pk_ = pdr.next()
                P.op("vector", lambda e, p_=p_, hs=hs, s=s: e.tensor_scalar(p_[:, :, :], hs[:, :, 511:512], flags[:, s + 1:s + 2], None, ALU.mult),
                     [(hsk, 3), "flags"], [pk_])
                cp = (s + 1) * HP
                P.dma("scalar", lambda e, p_=p_, cp=cp: e.dma_start(out=hv[:, :, cp:cp + 1], in_=p_[:, :, :], allow_slow_non_contiguous=True), [pk_], [])


def phase4b(nc, P, cfg, D, G):
    NTOK, SEG, HP = cfg.NTOK, cfg.SEG, cfg.HP
    with contextlib.ExitStack() as es:
        sb = lambda n, s, d: es.enter_context(nc.sbuf_tensor("s5_" + n, list(s), d))
        Wup = sb("Wup", [128, 8, 5632], BF16)
        Wdn = sb("Wdn", [128, 22, 1024], BF16)
        fcw = sb("fcw", [128, 44, 3], F32)
        fcb = sb("fcb", [128, 44], F32)
        act = sb("act", [128, 22, 256], BF16)
        h2 = Ring(es, nc, "s5_h2", [128, 8, 258], BF16, 2)
        x1r = Ring(es, nc, "s5_x1", [128, 1024], F32, 2)
        ua = Ring(es, nc, "s5_ua", [128, 256], F32, 3)
        ug = Ring(es, nc, "s5_ug", [128, 256], F32, 3)
        sg = Ring(es, nc, "s5_sg", [128, 256], F32, 3)
        orr = Ring(es, nc, "s5_o", [128, 1024], F32, 2)
        psU = Ring(es, nc, "s5_psU", [128, 512], F32, 4, psum=True)
        psD = Ring(es, nc, "s5_psD", [128, 512], F32, 2, psum=True)

        wup_v = D["w_up"].rearrange("(kc p) f -> p kc f", p=128)
        wdn_v = D["w_down"].rearrange("(fa p) d -> p fa d", p=128)
        for q in range(4):
            P.dma("gpsimd", lambda e, q=q: e.dma_start(out=Wup[:, :, q * 1408:(q + 1) * 1408], in_=wup_v[:, :, q * 1408:(q + 1) * 1408]),
                  [], [("Wup", q)])
        for q in range(2):
            P.dma("gpsimd", lambda e, q=q: e.dma_start(out=Wdn[:, q * 11:(q + 1) * 11, :], in_=wdn_v[:, q * 11:(q + 1) * 11, :]),
                  [], [("Wdn", q)])
        P.dma("sync", lambda e: e.dma_start(out=fcw[:, :, :], in_=D["fcw"].rearrange("p (c j) -> p c j", j=3)), [], ["fcw"])
        P.dma("sync", lambda e: e.dma_start(out=fcb[:, :], in_=D["fcb"]), [], ["fcb"])
        hv = D["h2T"].rearrange("(kc p) t -> p kc t", p=128)

        def load_h(tb):
            s, tin = divmod(tb, SEG)
            c0 = s * HP + tin
            h_, hk = h2.next()
            P.dma("sync", lambda e, h_=h_, c0=c0: e.dma_start(out=h_[:, :, :], in_=hv[:, :, c0:c0 + 258]), [], [hk])
            return h_, hk

        nxt = load_h(0)
        for tb in range(0, NTOK, 256):
            h_, hk = nxt
            if tb + 256 < NTOK:
                nxt = load_h(tb + 256)
            x1s = []
            for tt in range(2):
                x1, x1k = x1r.next()
                P.dma("sync", lambda e, x1=x1, t0=tb + tt * 128: e.dma_start(out=x1[:, :], in_=D["x1"][t0:t0 + 128, :]), [], [x1k])
                x1s.append((x1, x1k))
            pend = []
            for fa in range(22):
                res = []
                for which, cc in ((0, fa), (1, 22 + fa)):
                    pu, puk = psU.next()
                    for kc in range(8):
                        P.op("tensor", lambda e, pu=pu, h_=h_, kc=kc, cc=cc: e.matmul(
                            pu[:, 0:258], Wup[:, kc, cc * 128:(cc + 1) * 128], h_[:, kc, :], start=(kc == 0), stop=(kc == 7)),
                            [("Wup", (cc * 128) // 1408), ("Wup", (cc * 128 + 127) // 1408), hk], [puk])
                    u_, uk = (ua if which == 0 else ug).next()
                    P.op("scalar", lambda e, u_=u_, pu=pu, cc=cc: e.activation(
                        u_[:, :], pu[:, 1:257], AF.Identity, bias=fcb[:, cc:cc + 1], scale=fcw[:, cc, 1:2]), [puk, "fcw", "fcb"], [uk])
                    P.op("vector", lambda e, u_=u_, pu=pu, cc=cc: e.scalar_tensor_tensor(
                        u_[:, :], pu[:, 0:256], fcw[:, cc, 0:1], u_[:, :], ALU.mult, ALU.add), [puk, "fcw", uk], [uk])
                    P.op("vector", lambda e, u_=u_, pu=pu, cc=cc: e.scalar_tensor_tensor(
                        u_[:, :], pu[:, 2:258], fcw[:, cc, 2:3], u_[:, :], ALU.mult, ALU.add), [puk, "fcw", uk], [uk])
                    res.append((u_, uk))
                (a_, ak), (g_, gk) = res

                def fin(a_=a_, ak=ak, g_=g_, gk=gk, fa=fa):
                    s_, sk = sg.next()
                    P.op("scalar", lambda e, s_=s_, g_=g_: e.activation(s_[:, :], g_[:, :], AF.Silu), [gk], [sk])
                    P.op("gpsimd", lambda e, a_=a_, s_=s_, fa=fa: e.tensor_tensor(act[:, fa, :], a_[:, :], s_[:, :], ALU.mult),
                         [ak, sk], [("act", fa)])
                if pend:
                    pend.pop(0)()
                pend.append(fin)
            while pend:
                pend.pop(0)()
            for tt in range(2):
                t0 = tb + tt * 128
                x1, x1k = x1s[tt]
                o_, ok = orr.next()
                for dh in range(2):
                    pd_, pdk = psD.next()
                    for fa in range(22):
                        P.op("tensor", lambda e, pd_=pd_, fa=fa, tt=tt, dh=dh: e.matmul(
                            pd_[:, :], act[:, fa, tt * 128:(tt + 1) * 128], Wdn[:, fa, dh * 512:(dh + 1) * 512],
                            start=(fa == 0), stop=(fa == 21)), [("act", fa), ("Wdn", fa // 11)], [pdk])
                    P.op("vector", lambda e, o_=o_, pd_=pd_, x1=x1, dh=dh: e.tensor_tensor(
                        o_[:, dh * 512:(dh + 1) * 512], pd_[:, :], x1[:, dh * 512:(dh + 1) * 512], ALU.add), [pdk, x1k], [(ok, dh)])
                P.dma("sync", lambda e, o_=o_, t0=t0: e.dma_start(out=D["out"][t0:t0 + 128, :], in_=o_[:, :]),
                      [(ok, 0), (ok, 1)], [])


def ffn_shared_inputs(p):
    d = {}
    d["w_att_out"] = np.ascontiguousarray(p["w_att_out"][0])
    d["w_ssd_out"] = np.ascontiguousarray(p["w_ssd_out"][0])
    d["w_o"] = np.ascontiguousarray(p["w_o"][0])
    d["w_up"] = np.ascontiguousarray(p["w_up"][0])
    d["w_down"] = np.ascontiguousarray(p["w_down"][0])
    d["gffnT"] = colT(p["g_ffn"][0], 8)
    fw = p["ffn_conv_w"][0]
    d["fcw"] = np.ascontiguousarray(fw.reshape(3, 44, 128).transpose(2, 1, 0).reshape(128, 132))
    d["fcb"] = colT(p["ffn_conv_b"][0], 44)
    return d


N_CORES = 8
_NC_CACHE = {}


def full_cfg():
    return Cfg(nseg=5, seg=2048, chains=((0,), (1,), (2,), (3, 4)), tg=5120)


def core_segments(c):
    if c < 4:
        segs = [("p", 3 * c + i, 0) for i in range(3)] + [("s", c, 0), ("s", c, 2048)]
        linked = True
    else:
        segs = [("p", 12 + 5 * (c - 4) + i, 0) for i in range(5)]
        linked = False
    return segs, linked


def kernel(x_prompt, x_sample, g_mix, w_in, b_gate, q_norm_g, k_norm_g, rel_bias, ssd_conv_w,
           ssd_conv_b, dt_bias_f, dt_bias_b, A_log_f, A_log_b, D_skip, ssd_norm_g, w_att_out,
           w_ssd_out, w_o, g_ffn, w_up, ffn_conv_w, ffn_conv_b, w_down):
    p = dict(g_mix=g_mix, w_in=w_in, b_gate=b_gate, q_norm_g=q_norm_g, k_norm_g=k_norm_g, rel_bias=rel_bias,
             ssd_conv_w=ssd_conv_w, ssd_conv_b=ssd_conv_b, dt_bias_f=dt_bias_f, dt_bias_b=dt_bias_b,
             A_log_f=A_log_f, A_log_b=A_log_b, D_skip=D_skip, ssd_norm_g=ssd_norm_g, w_att_out=w_att_out,
             w_ssd_out=w_ssd_out, w_o=w_o, g_ffn=g_ffn, w_up=w_up, ffn_conv_w=ffn_conv_w, ffn_conv_b=ffn_conv_b,
             w_down=w_down)
    p = {k: np.asarray(v, np.float32) for k, v in p.items()}
    x_prompt = np.asarray(x_prompt, np.float32)
    x_sample = np.asarray(x_sample, np.float32)
    cfg = full_cfg()
    if "nc" not in _NC_CACHE:
        _NC_CACHE["nc"] = build(cfg)
    nc = _NC_CACHE["nc"]
    shared = {}
    shared.update(shared_inputs(p))
    shared.update(att_shared_inputs(p))
    shared.update(ffn_shared_inputs(p))
    in_maps = []
    for c in range(N_CORES):
        segs, linked = core_segments(c)
        xs = []
        for which, bi, off in segs:
            src = x_prompt if which == "p" else x_sample
            xs.append(src[bi, off:off + 2048])
        flags = np.zeros((128, 8), np.float32)
        if linked:
            flags[:, 4] = 1.0
        m = dict(shared)
        m["x"] = np.ascontiguousarray(np.concatenate(xs, 0))
        m["flags"] = flags
        m["rowmask"] = rowmask_table(cfg, flags[0])
        in_maps.append(m)
    res = run_bass_kernel_spmd(nc, in_maps, core_ids=list(range(N_CORES)))
    y_prompt = np.empty((32, 2048, 1024), np.float32)
    y_sample = np.empty((4, 4096, 1024), np.float32)
    for c in range(N_CORES):
        o = np.asarray(res.results[c]["out"], np.float32)
        segs, _ = core_segments(c)
        for i, (which, bi, off) in enumerate(segs):
            dst = y_prompt if which == "p" else y_sample
            dst[bi, off:off + 2048] = o[i * 2048:(i + 1) * 2048]
    return y_prompt, y_sample
```

```python
import contextlib
import numpy as np
import ml_dtypes
import concourse.bass as bass
import concourse.mybir as mybir
from concourse.bass_utils import run_bass_kernel_spmd

F32 = mybir.dt.float32
BF16 = mybir.dt.bfloat16
AF = mybir.ActivationFunctionType
ALU = mybir.AluOpType
AX = mybir.AxisListType

D_MODEL = 1024
GRID_W = 64
ATT_HEADS = 16
HEAD_DIM = 64
SSD_INNER = 2048
SSD_HEADS = 32
SSD_GROUPS = 8
D_FF = 2816
IN_W = 11328
EPS = 1e-6
NEG = -30000.0

C_Q, C_K, C_V, C_Z, C_XBC, C_DTF, C_DTB, C_GA, C_GS = 0, 1024, 2048, 3072, 5120, 9216, 9248, 9280, 10304


class _Op:
    __slots__ = ("eng", "fn", "deps", "is_dma", "needs_inc", "tok", "idx")


class Prog:
    ENGINES = ("tensor", "vector", "scalar", "gpsimd", "sync")

    def __init__(self, nc, n_dma_sems=12, dma_queues=("sync", "gpsimd", "scalar")):
        self.nc = nc
        self.ops = {e: [] for e in self.ENGINES}
        self.res = {}
        self.n_dma_sems = n_dma_sems
        self.dma_queues = dma_queues
        self.all_ops = []

    def _add(self, eng, fn, reads, writes, is_dma):
        op = _Op()
        op.eng, op.fn, op.is_dma, op.needs_inc, op.tok = eng, fn, is_dma, is_dma, None
        deps = []
        for r in reads:
            st = self.res.get(r)
            if st is not None and st[0] is not None:
                deps.append(st[0])
        for w in writes:
            st = self.res.get(w)
            if st is not None:
                if st[0] is not None:
                    deps.append(st[0])
                deps.extend(st[1])
        for r in reads:
            st = self.res.setdefault(r, [None, []])
            st[1].append(op)
        for w in writes:
            self.res[w] = [op, []]
        seen = set()
        op.deps = []
        for d in deps:
            if id(d) in seen or d is op:
                continue
            seen.add(id(d))
            if (not d.is_dma) and d.eng == eng == "tensor" and not is_dma:
                continue
            op.deps.append(d)
            d.needs_inc = True
        op.idx = len(self.all_ops)
        self.all_ops.append(op)
        self.ops[eng].append(op)
        return op

    def op(self, eng, fn, reads=(), writes=()):
        return self._add(eng, fn, reads, writes, False)

    def dma(self, eng, fn, reads=(), writes=()):
        return self._add(eng, fn, reads, writes, True)

    def setup(self, es):
        nc = self.nc
        self.csem = {e: es.enter_context(nc.semaphore(f"c_{e}")) for e in ("tensor", "vector", "scalar", "gpsimd")}
        self.dsem = {q: [es.enter_context(nc.semaphore(f"d_{q}{i}")) for i in range(self.n_dma_sems)]
                     for q in self.dma_queues}
        self.cnt = {e: 0 for e in self.csem}
        self.dcnt = {q: [0] * self.n_dma_sems for q in self.dma_queues}
        self.drr = {q: 0 for q in self.dma_queues}

    def emit(self):
        nc = self.nc
        csem, dsem, cnt, dcnt, drr = self.csem, self.dsem, self.cnt, self.dcnt, self.drr
        prev_tok = {}
        for op in self.all_ops:
            if op.is_dma:
                q = op.eng
                i = drr[q]
                drr[q] = (i + 1) % self.n_dma_sems
                prev = dcnt[q][i]
                dcnt[q][i] += 16
                op.tok = (dsem[q][i], dcnt[q][i], ("d", q, i))
                prev_tok[id(op)] = (dsem[q][i], prev, ("d", q, i)) if prev > 0 else None
            elif op.needs_inc:
                cnt[op.eng] += 1
                op.tok = (csem[op.eng], cnt[op.eng], ("c", op.eng))
        n_dma_sems, dma_queues = self.n_dma_sems, self.dma_queues
        all_ops_by_eng = self.ops

        def make(engname):
            ops = all_ops_by_eng[engname]

            def body(eng):
                waited = {}

                def wait(tok):
                    if tok is None:
                        return
                    sem, val, key = tok
                    if waited.get(key, 0) >= val:
                        return
                    eng.wait_ge(sem, val)
                    waited[key] = val

                for op in ops:
                    for d in op.deps:
                        wait(d.tok)
                    if op.is_dma:
                        wait(prev_tok[id(op)])
                        op.fn(eng).then_inc(op.tok[0], 16)
                    else:
                        ins = op.fn(eng)
                        if op.needs_inc:
                            ins.then_inc(op.tok[0], 1)
                if engname == "sync":
                    for q in dma_queues:
                        for i in range(n_dma_sems):
                            if dcnt[q][i] > 0:
                                wait((dsem[q][i], dcnt[q][i], ("d", q, i)))
                    for e in csem:
                        if cnt[e] > 0:
                            wait((csem[e], cnt[e], ("c", e)))
            return body

        with nc.Block() as blk:
            blk.tensor(make("tensor"))
            blk.vector(make("vector"))
            blk.scalar(make("scalar"))
            blk.gpsimd(make("gpsimd"))
            blk.sync(make("sync"))
        self.n_emitted = getattr(self, "n_emitted", 0) + len(self.all_ops)
        self.ops = {e: [] for e in self.ENGINES}
        self.res = {}
        self.all_ops = []


class Ring:
    def __init__(self, es, nc, name, shape, dtype, n, psum=False):
        alloc = nc.psum_tensor if psum else nc.sbuf_tensor
        self.t = [es.enter_context(alloc(f"{name}{i}", list(shape), dtype)) for i in range(n)]
        self.k = [f"{name}{i}" for i in range(n)]
        self.i = 0

    def next(self):
        i = self.i
        self.i = (i + 1) % len(self.t)
        return self.t[i], self.k[i]


class SlotRing:
    def __init__(self, items):
        self.items = items
        self.i = 0

    def next(self):
        it = self.items[self.i]
        self.i = (self.i + 1) % len(self.items)
        return it


class Cfg:
    def __init__(self, nseg=5, seg=2048, chains=((0,), (1,), (2,), (3, 4)), tg=5120, debug=False, stop_after=9):
        self.NSEG, self.SEG, self.chains, self.debug = nseg, seg, chains, debug
        self.stop_after = stop_after
        self.NTOK = nseg * seg
        self.TG = min(tg, self.NTOK)
        self.XP = seg + 4
        self.HP = seg + 2
        self.NT = self.NTOK // 128
        self.TPS = seg // 128
        assert self.NTOK % self.TG == 0 and self.TG % 512 == 0 and seg % 512 == 0


K_IDENT, K_BLK64, K_LE, K_GT, K_GE, K_LT, K_ONES, NK = 0, 1, 2, 3, 4, 5, 6, 7


def host_consts():
    k = np.arange(128)[:, None]
    i = np.arange(128)[None, :]
    m = np.zeros((NK, 128, 128), np.float32)
    m[K_IDENT] = (k == i)
    m[K_BLK64] = ((k // 64) == (i // 64)) / 64.0
    m[K_LE] = (k <= i)
    m[K_GT] = (k > i)
    m[K_GE] = (k >= i)
    m[K_LT] = (k < i)
    m[K_ONES] = 1.0
    return np.ascontiguousarray(m.transpose(1, 0, 2).reshape(128, NK * 128))


LAST_INPUT_NAMES = []


def build(cfg):
    nc = bass.Bass("TRN2", target_bir_lowering=False)
    NTOK, SEG, NSEG = cfg.NTOK, cfg.SEG, cfg.NSEG
    dbg = cfg.debug
    skind = "ExternalOutput" if dbg else "Internal"

    LAST_INPUT_NAMES.clear()

    def din(name, shape, dt=F32):
        LAST_INPUT_NAMES.append(name)
        return nc.dram_tensor(name, list(shape), dt, kind="ExternalInput").ap()

    def dscr(name, shape, dt):
        return nc.dram_tensor(name, list(shape), dt, kind=skind).ap()

    D = {}
    D["x"] = din("x", [NTOK, 1024])
    D["w_in"] = din("w_in", [1024, IN_W])
    D["consts"] = din("consts", [128, NK * 128])
    D["gmixT"] = din("gmixT", [128, 8])
    D["bgT"] = din("bgT", [128, 16])
    D["qkg"] = din("qkg", [128, 2])
    D["dtbias"] = din("dtbias", [128, 64])
    D["flags"] = din("flags", [128, 8])
    D["convw"] = din("convw", [128, 160])
    D["convbc"] = din("convbc", [128, 32])
    D["convbr"] = din("convbr", [1, 4096])
    D["alog"] = din("alog", [128, 64])
    D["dskip"] = din("dskip", [128, 32])
    D["normg"] = din("normg", [128, 2048])
    D["biasx"] = din("biasx", [16, ND, 128, 128])
    D["cmask"] = din("cmask", [ND, 128, 128])
    D["halfsel"] = din("halfsel", [128, 128], BF16)
    D["rowmask"] = din("rowmask", [2, cfg.NT * 768], BF16)
    D["w_att_out"] = din("w_att_out", [1024, 1024])
    D["w_ssd_out"] = din("w_ssd_out", [2048, 1024])
    D["w_o"] = din("w_o", [1024, 1024])
    D["w_up"] = din("w_up", [1024, 5632])
    D["w_down"] = din("w_down", [2816, 1024])
    D["gffnT"] = din("gffnT", [128, 8])
    D["fcw"] = din("fcw", [128, 132])
    D["fcb"] = din("fcb", [128, 44])
    D["qT"] = dscr("qT", [1024, NTOK], BF16)
    D["kT"] = dscr("kT", [1024, NTOK], BF16)
    D["v"] = dscr("v", [NTOK, 1024], BF16)
    D["zs"] = dscr("zs", [NTOK, 2048], BF16)
    D["xbcT"] = dscr("xbcT", [4096, NSEG * cfg.XP], BF16)
    D["dt"] = dscr("dt", [NTOK, 64], F32)
    D["gT"] = dscr("gT", [2048, NTOK], BF16)
    D["yT"] = dscr("yT", [2048, NTOK], BF16)
    D["attT"] = dscr("attT", [1024, NTOK], BF16)
    D["x1"] = dscr("x1", [NTOK, 1024], F32)
    D["h2T"] = dscr("h2T", [1024, NSEG * cfg.HP], BF16)
    D["out"] = nc.dram_tensor("out", [NTOK, 1024], F32, kind="ExternalOutput").ap()

    with contextlib.ExitStack() as ges:
        P = Prog(nc)
        P.setup(ges)
        kc_f = ges.enter_context(nc.sbuf_tensor("kc_f", [128, NK * 128], F32))
        kc_b = ges.enter_context(nc.sbuf_tensor("kc_b", [128, NK * 128], BF16))
        flags = ges.enter_context(nc.sbuf_tensor("flags_sb", [128, 8], F32))
        P.dma("sync", lambda e: e.dma_start(out=kc_f[:, :], in_=D["consts"]), [], ["kc_f"])
        P.dma("sync", lambda e: e.dma_start(out=flags[:, :], in_=D["flags"]), [], ["flags"])
        P.op("vector", lambda e: e.tensor_copy(kc_b[:, :], kc_f[:, :]), ["kc_f"], ["kc_b"])
        G = dict(kc_b=kc_b, kc_f=kc_f, flags=flags)
        phase1(nc, P, cfg, D, G)
        P.emit()
        if cfg.stop_after >= 2:
            phase2(nc, P, cfg, D, G)
            P.emit()
        if cfg.stop_after >= 3:
            phase3(nc, P, cfg, D, G)
            P.emit()
        if cfg.stop_after >= 4:
            phase4a(nc, P, cfg, D, G)
            P.emit()
        if cfg.stop_after >= 5:
            phase4b(nc, P, cfg, D, G)
            P.emit()
    return nc


def KB(G, k):
    return G["kc_b"][:, k * 128:(k + 1) * 128]


def phase1(nc, P, cfg, D, G):
    NTOK, SEG, TG = cfg.NTOK, cfg.SEG, cfg.TG
    with contextlib.ExitStack() as es:
        sb = lambda n, s, d: es.enter_context(nc.sbuf_tensor(n, list(s), d))
        hT = sb("hT", [128, 8, TG], BF16)
        gmixT = sb("gmixT_sb", [128, 8], F32)
        bgT = sb("bgT_sb", [128, 16], F32)
        qkg = sb("qkg_sb", [128, 2], F32)
        qkgs = sb("qkgs_sb", [128, 2], F32)
        dtb = sb("dtb_sb", [128, 64], F32)
        zpad = sb("zpad", [128, 32, 2], BF16)
        xin = Ring(es, nc, "xin", [128, 1024], F32, 2)
        xn = Ring(es, nc, "xn", [128, 1024], BF16, 2)
        junk = Ring(es, nc, "junk", [128, 1024], BF16, 1)
        st = Ring(es, nc, "st", [128, 4], F32, 2)
        Wr = Ring(es, nc, "W", [128, 8, 512], BF16, 2)
        sq = Ring(es, nc, "sq", [128, 512], BF16, 2)
        lnb = Ring(es, nc, "lnb", [128, 512], F32, 2)
        ob = Ring(es, nc, "ob", [128, 512], BF16, 4)
        pd = Ring(es, nc, "pd", [128, 4], BF16, 4)
        dts = Ring(es, nc, "dts", [128, 64], F32, 2)
        psT = Ring(es, nc, "psT", [128, 8, 128], BF16, 2, psum=True)
        psM = Ring(es, nc, "psM", [128, 512], F32, 4, psum=True)
        psN = Ring(es, nc, "psN", [128, 512], F32, 2, psum=True)

        P.dma("sync", lambda e: e.dma_start(out=gmixT[:, :], in_=D["gmixT"]), [], ["gmixT"])
        P.dma("sync", lambda e: e.dma_start(out=bgT[:, :], in_=D["bgT"]), [], ["bgT"])
        P.dma("sync", lambda e: e.dma_start(out=qkg[:, :], in_=D["qkg"]), [], ["qkg"])
        P.dma("sync", lambda e: e.dma_start(out=dtb[:, :], in_=D["dtbias"]), [], ["dtb"])
        P.op("vector", lambda e: e.tensor_scalar(qkgs[:, 0:1], qkg[:, 0:1], HEAD_DIM ** -0.5, None, ALU.mult),
             ["qkg"], ["qkgs"])
        P.op("vector", lambda e: e.tensor_copy(qkgs[:, 1:2], qkg[:, 1:2]), ["qkg", "qkgs"], ["qkgs"])
        P.op("vector", lambda e: e.memset(zpad[:, :, :], 0.0), [], ["zpad"])
        xv = D["xbcT"].rearrange("(cc p) t -> p cc t", p=128)
        XP = cfg.XP
        P.dma("gpsimd", lambda e: e.dma_start(out=xv[:, :, 0:2], in_=zpad[:, :, :]), ["zpad"], [])
        P.dma("gpsimd", lambda e: e.dma_start(out=xv[:, :, cfg.NSEG * XP - 2:cfg.NSEG * XP], in_=zpad[:, :, :]),
              ["zpad"], [])

        wv = D["w_in"].rearrange("(kc p) c -> p kc c", p=128)
        ident = KB(G, K_IDENT)
        blk64 = KB(G, K_BLK64)

        def load_w(c0, ncols):
            Wt, Wk = Wr.next()
            P.dma("gpsimd", lambda e: e.dma_start(out=Wt[:, :, 0:ncols], in_=wv[:, :, c0:c0 + ncols]), [], [Wk])
            return Wt, Wk

        for tg0 in range(0, NTOK, TG):
            for ti in range(TG // 128):
                t0 = tg0 + ti * 128
                xt, xk = xin.next()
                P.dma("sync", lambda e, xt=xt, t0=t0: e.dma_start(out=xt[:, :], in_=D["x"][t0:t0 + 128, :]), [], [xk])
                s_, sk = st.next()
                jt, jk = junk.next()
                P.op("scalar", lambda e, jt=jt, xt=xt, s_=s_: e.activation(jt[:, :], xt[:, :], AF.Square, accum_out=s_[:, 0:1]),
                     [xk], [jk, sk])
                P.op("scalar", lambda e, s_=s_: e.activation(s_[:, 1:2], s_[:, 0:1], AF.Ln, bias=EPS, scale=1.0 / 1024),
                     [sk], [sk])
                P.op("scalar", lambda e, s_=s_: e.activation(s_[:, 2:3], s_[:, 1:2], AF.Exp, scale=-0.5), [sk], [sk])
                xnt, xnk = xn.next()
                P.op("vector", lambda e, xnt=xnt, xt=xt, s_=s_: e.tensor_scalar(xnt[:, :], xt[:, :], s_[:, 2:3], None, ALU.mult),
                     [xk, sk], [xnk])
                pt, pk = psT.next()
                for kc in range(8):
                    P.op("tensor", lambda e, pt=pt, xnt=xnt, kc=kc: e.transpose(pt[:, kc, :], xnt[:, kc * 128:(kc + 1) * 128], ident),
                         [xnk, "kc_b"], [pk])
                P.op("vector", lambda e, pt=pt, ti=ti: e.tensor_tensor(
                    hT[:, :, ti * 128:(ti + 1) * 128], pt[:, :, :],
                    gmixT[:, :].unsqueeze(2).to_broadcast([128, 8, 128]), ALU.mult),
                    [pk, "gmixT"], [("hT", ti)])

            def hkeys(ta, tb):
                return [("hT", i) for i in range(ta // 128, tb // 128)]

            fjobs = [("q", C_Q, 1024), ("k", C_K, 1024), ("xbc", C_XBC, 4096), ("g", C_GA, 2048)]
            pend = []
            for name, cbase, ctot in fjobs:
                for cb in range(0, ctot, 512):
                    Wt, Wk = load_w(cbase + cb, 512)
                    for tb in range(0, TG, 512):
                        tglob = tg0 + tb
                        for cc in range(4):
                            f0 = cb + cc * 128
                            pm, pmk = psM.next()
                            for kc in range(8):
                                P.op("tensor", lambda e, pm=pm, Wt=Wt, cc=cc, kc=kc, tb=tb: e.matmul(
                                    pm[:, :], Wt[:, kc, cc * 128:(cc + 1) * 128], hT[:, kc, tb:tb + 512],
                                    start=(kc == 0), stop=(kc == 7)), [Wk] + hkeys(tb, tb + 512), [pmk])
                            while len(pend) > 0 and (name not in ("q", "k") or len(pend) > 1 or True):
                                pend.pop(0)()
                            if name not in ("q", "k"):
                                o, ok = ob.next()
                            if name in ("q", "k"):
                                col = 0 if name == "q" else 1
                                s2, s2k = sq.next()
                                P.op("scalar", lambda e, s2=s2, pm=pm: e.activation(s2[:, :], pm[:, :], AF.Square), [pmk], [s2k])
                                def fin(pm=pm, pmk=pmk, s2=s2, s2k=s2k, col=col, f0=f0, tglob=tglob, name=name):
                                    o, ok = ob.next()
                                    pn, pnk = psN.next()
                                    P.op("tensor", lambda e, pn=pn, s2=s2: e.matmul(pn[:, :], blk64, s2[:, :], start=True, stop=True),
                                         [s2k, "kc_b"], [pnk])
                                    lb, lbk = lnb.next()
                                    P.op("scalar", lambda e, lb=lb, pn=pn: e.activation(lb[:, :], pn[:, :], AF.Ln, bias=EPS), [pnk], [lbk])
                                    P.op("scalar", lambda e, lb=lb: e.activation(lb[:, :], lb[:, :], AF.Exp, scale=-0.5), [lbk], [lbk])
                                    P.op("vector", lambda e, o=o, pm=pm, lb=lb, col=col: e.scalar_tensor_tensor(
                                        o[:, :], pm[:, :], qkgs[:, col:col + 1], lb[:, :], ALU.mult, ALU.mult),
                                        [pmk, lbk, "qkgs"], [ok])
                                    dst = D["qT" if name == "q" else "kT"]
                                    P.dma("sync", lambda e, o=o, dst=dst, f0=f0, tglob=tglob: e.dma_start(
                                        out=dst[f0:f0 + 128, tglob:tglob + 512], in_=o[:, :]), [ok], [])
                                pend.append(fin)
                            elif name == "g":
                                gi = f0 // 128
                                P.op("scalar", lambda e, o=o, pm=pm, gi=gi: e.activation(
                                    o[:, :], pm[:, :], AF.Sigmoid, bias=bgT[:, gi:gi + 1]), [pmk, "bgT"], [ok])
                                P.dma("sync", lambda e, o=o, f0=f0, tglob=tglob: e.dma_start(
                                    out=D["gT"][f0:f0 + 128, tglob:tglob + 512], in_=o[:, :]), [ok], [])
                            else:
                                P.op("vector", lambda e, o=o, pm=pm: e.tensor_copy(o[:, :], pm[:, :]), [pmk], [ok])
                                s = tglob // SEG
                                tin = tglob % SEG
                                c0 = s * XP + 2 + tin
                                P.dma("sync", lambda e, o=o, f0=f0, c0=c0: e.dma_start(
                                    out=D["xbcT"][f0:f0 + 128, c0:c0 + 512], in_=o[:, :]), [ok], [])
                                if tin == 0 and s > 0:
                                    p_, pk_ = pd.next()
                                    P.op("vector", lambda e, p_=p_, pm=pm, s=s: e.tensor_scalar(
                                        p_[:, 0:2], pm[:, 0:2], G["flags"][:, s:s + 1], None, ALU.mult), [pmk, "flags"], [pk_])
                                    cp = (s - 1) * XP + 2 + SEG
                                    P.dma("sync", lambda e, p_=p_, f0=f0, cp=cp: e.dma_start(
                                        out=D["xbcT"][f0:f0 + 128, cp:cp + 2], in_=p_[:, 0:2]), [pk_], [])
                                if tin + 512 == SEG and s + 1 < cfg.NSEG:
                                    p_, pk_ = pd.next()
                                    P.op("vector", lambda e, p_=p_, pm=pm, s=s: e.tensor_scalar(
                                        p_[:, 0:2], pm[:, 510:512], G["flags"][:, s + 1:s + 2], None, ALU.mult), [pmk, "flags"], [pk_])
                                    cp = (s + 1) * XP
                                    P.dma("sync", lambda e, p_=p_, f0=f0, cp=cp: e.dma_start(
                                        out=D["xbcT"][f0:f0 + 128, cp:cp + 2], in_=p_[:, 0:2]), [pk_], [])
            tjobs = [("v", C_V, 1024), ("z", C_Z, 2048), ("dt", C_DTF, 64)]
            for name, cbase, ctot in tjobs:
                for cb in range(0, ctot, 512):
                    ncols = min(512, ctot - cb)
                    Wt, Wk = load_w(cbase + cb, ncols)
                    for tt in range(0, TG, 128):
                        tglob = tg0 + tt
                        pm, pmk = psM.next()
                        for kc in range(8):
                            P.op("tensor", lambda e, pm=pm, Wt=Wt, kc=kc, tt=tt, ncols=ncols: e.matmul(
                                pm[:, 0:ncols], hT[:, kc, tt:tt + 128], Wt[:, kc, 0:ncols],
                                start=(kc == 0), stop=(kc == 7)), [Wk, ("hT", tt // 128)], [pmk])
                        if name == "dt":
                            d_, dk = dts.next()
                            P.op("vector", lambda e, d_=d_, pm=pm: e.tensor_tensor(d_[:, :], pm[:, 0:64], dtb[:, :], ALU.add),
                                 [pmk, "dtb"], [dk])
                            P.op("scalar", lambda e, d_=d_: e.activation(d_[:, :], d_[:, :], AF.Exp), [dk], [dk])
                            P.op("scalar", lambda e, d_=d_: e.activation(d_[:, :], d_[:, :], AF.Ln, bias=1.0), [dk], [dk])
                            P.dma("sync", lambda e, d_=d_, tglob=tglob: e.dma_start(
                                out=D["dt"][tglob:tglob + 128, :], in_=d_[:, :]), [dk], [])
                        else:
                            o, ok = ob.next()
                            if name == "v":
                                P.op("vector", lambda e, o=o, pm=pm: e.tensor_copy(o[:, :], pm[:, :]), [pmk], [ok])
                                P.dma("sync", lambda e, o=o, cb=cb, tglob=tglob: e.dma_start(
                                    out=D["v"][tglob:tglob + 128, cb:cb + 512], in_=o[:, :]), [ok], [])
                            else:
                                P.op("scalar", lambda e, o=o, pm=pm: e.activation(o[:, :], pm[:, :], AF.Silu), [pmk], [ok])
                                P.dma("sync", lambda e, o=o, cb=cb, tglob=tglob: e.dma_start(
                                    out=D["zs"][tglob:tglob + 128, cb:cb + 512], in_=o[:, :]), [ok], [])


def phase2(nc, P, cfg, D, G):
    SEG, TPS, XP = cfg.SEG, cfg.TPS, cfg.XP
    maxc = max(len(c) for c in cfg.chains)
    with contextlib.ExitStack() as es:
        sb = lambda n, s, d: es.enter_context(nc.sbuf_tensor("s2_" + n, list(s), d))
        convw = sb("convw", [128, 32, 5], F32)
        convbc = sb("convbc", [128, 32], F32)
        convbr_r = Ring(es, nc, "s2_convbr", [1, 384], BF16, 2)
        alog = sb("alog", [128, 64], F32)
        Arow = sb("Arow", [128, 64], F32)
        dsk = sb("dsk", [128, 32], F32)
        normg_r = Ring(es, nc, "s2_normg", [128, 256], F32, 2)
        diag = Ring(es, nc, "s2_diag", [128, 4, 5, 128], BF16, 2)
        xin = Ring(es, nc, "s2_xinc", [128, 4, XP], BF16, 2)
        xb_tm = [sb(f"xb_tm{i}", [128, TPS, 384], BF16) for i in range(maxc)]
        BT = [sb(f"BT{i}", [128, SEG], BF16) for i in range(maxc)]
        CT = [sb(f"CT{i}", [128, SEG], BF16) for i in range(maxc)]
        prevb = [sb(f"prevb{i}", [128, TPS, 256], BF16) for i in range(maxc)]
        zs = [sb(f"zs{i}", [128, TPS, 256], BF16) for i in range(maxc)]
        dtseg = [sb(f"dtseg{i}", [128, TPS, 64], F32) for i in range(maxc)]
        dtA = [sb(f"dtA{i}", [128, TPS, 8], F32) for i in range(maxc)]
        dtAb = [sb(f"dtAb{i}", [128, TPS, 8], BF16) for i in range(maxc)]
        decs = [sb(f"decs{i}", [128, 5, TPS, 8], F32) for i in range(maxc)]
        dtee = [sb(f"dtee{i}", [128, TPS, 8], F32) for i in range(maxc)]
        lndt = [sb(f"lndt{i}", [128, TPS, 8], F32) for i in range(maxc)]
        diagD_r = Ring(es, nc, "s2_diagD", [128, 4, 128], BF16, 2)
        Hfs = [sb(f"Hf{i}", [128, 256], F32) for i in range(2)]
        Hbs = [sb(f"Hb{i}", [128, 256], F32) for i in range(2)]
        prevf = [sb(f"prevf{i}", [128, TPS, 256], BF16) for i in range(maxc)]
        Ht = Ring(es, nc, "s2_Ht", [128, 256], F32, 3)
        xdeb = Ring(es, nc, "s2_xdeb", [128, 256], BF16, 4)
        sm = Ring(es, nc, "s2_sm", [128, 2, 128], BF16, 3)
        Lr = Ring(es, nc, "s2_Lseg", [128, 4, 128], BF16, 4)
        Er = Ring(es, nc, "s2_Eseg", [128, 4, 128], BF16, 4)
        MTr = Ring(es, nc, "s2_MT", [128, 4, 128], BF16, 4)
        ya = Ring(es, nc, "s2_ya", [128, 256], F32, 4)
        yb = Ring(es, nc, "s2_yb", [128, 256], F32, 5)
        yy = Ring(es, nc, "s2_yy", [128, 256], F32, 4)
        yj = Ring(es, nc, "s2_yj", [128, 256], BF16, 1)
        yst = Ring(es, nc, "s2_yst", [128, 4], F32, 4)
        yn = Ring(es, nc, "s2_yn", [128, 256], BF16, 3)
        ystage = Ring(es, nc, "s2_ystage", [128, 2, 512], BF16, 2)
        psG = Ring(es, nc, "s2_ps2g", [128, 512], F32, 3, psum=True)
        psA = Ring(es, nc, "s2_ps2a", [128, 512], F32, 2, psum=True)
        psAll = Ring.__new__(Ring)
        psAll.t, psAll.k, psAll.i = psG.t + psA.t, psG.k + psA.k, 0
        bank_sc = es.enter_context(nc.psum_tensor("s2_bank_sc", [128, 512], F32))
        bank_o = es.enter_context(nc.psum_tensor("s2_bank_o", [128, 512], F32))
        bank_t = es.enter_context(nc.psum_tensor("s2_bank_t", [128, 2, 128], BF16))
        sc_slots = SlotRing([(bank_sc[:, 0:128], "s2_scy")])
        o_slots = SlotRing([(bank_o, "s2_o")])
        y_slots = SlotRing([(bank_sc[:, 128:384], "s2_scy")])
        t_slots = SlotRing([(bank_t[:, :, :], "s2_t")])

        P.dma("sync", lambda e: e.dma_start(out=convw[:, :, :], in_=D["convw"].rearrange("p (c j) -> p c j", j=5)), [], ["convw"])
        P.dma("sync", lambda e: e.dma_start(out=convbc[:, :], in_=D["convbc"]), [], ["convbc"])
        P.dma("sync", lambda e: e.dma_start(out=alog[:, :], in_=D["alog"]), [], ["alog"])
        P.dma("sync", lambda e: e.dma_start(out=dsk[:, :], in_=D["dskip"]), [], ["dsk"])
        P.op("scalar", lambda e: e.activation(Arow[:, :], alog[:, :], AF.Exp), ["alog"], ["Arow"])
        P.op("vector", lambda e: e.tensor_scalar(Arow[:, :], Arow[:, :], -1.0, None, ALU.mult), ["Arow"], ["Arow"])

        ident = KB(G, K_IDENT)
        kf = lambda k: G["kc_f"][:, k * 128:(k + 1) * 128]
        ones_row = G["kc_b"][0:1, K_ONES * 128:(K_ONES + 1) * 128]
        xv = D["xbcT"].rearrange("(cc p) t -> p cc t", p=128)
        yTv = D["yT"].rearrange("(cc p) t -> p cc t", p=128)
        TRIS = (K_LE, K_GT, K_GE, K_LT, K_ONES)
        I_LE, I_GT, I_GE, I_LT, I_ONES = 0, 1, 2, 3, 4
        flags = G["flags"]

        for g in range(8):
            ccs = (2 * g, 2 * g + 1, 16 + g, 24 + g)
            dg, dgk = diag.next()
            convbr, cbk = convbr_r.next()
            dD, dDk = diagD_r.next()
            for h in range(4):
                P.op("vector", lambda e, dD=dD, h=h, g=g: e.tensor_scalar(
                    dD[:, h, :], ident, dsk[:, 4 * g + h:4 * g + h + 1], None, ALU.mult), ["kc_b", "dsk"], [(dDk, h)])
            normg, ngk = normg_r.next()
            for ci in range(3):
                P.dma("gpsimd", lambda e, convbr=convbr, ci=ci, ch0=ccs[ci] * 128: e.dma_start(
                    out=convbr[:, ci * 128:(ci + 1) * 128], in_=D["convbr"][:, ch0:ch0 + 128]), [], [(cbk, ci)])
            P.dma("sync", lambda e, normg=normg, g=g: e.dma_start(out=normg[:, :], in_=D["normg"][:, g * 256:(g + 1) * 256]), [], [ngk])
            for ci, cc in enumerate(ccs):
                for j in range(5):
                    P.op("vector", lambda e, dg=dg, ci=ci, cc=cc, j=j: e.tensor_scalar(
                        dg[:, ci, j, :], ident, convw[:, cc, j:j + 1], None, ALU.mult), ["kc_b", "convw"], [dgk])
            units = []
            tog = 0
            for chain in cfg.chains:
                if len(chain) == 1:
                    units.append([(tog, chain[0])])
                    tog = 1 - tog
                else:
                    units.append([(tog, chain[0]), (1 - tog, chain[1])])

            def make_pieces(segs_):
                pcs = []
                for (li, s) in segs_:
                    st_ = {}

                    def prep(ring, li=li, s=s, st_=st_):
                        xt, xk = xin.next()
                        st_["xt"], st_["xk"] = xt, xk
                        c0 = s * XP
                        P.dma("sync", lambda e, xt=xt, c0=c0, g=g: e.dma_start(out=xt[:, 0:2, :], in_=xv[:, 2 * g:2 * g + 2, c0:c0 + XP]), [], [xk])
                        P.dma("sync", lambda e, xt=xt, c0=c0, g=g: e.dma_start(out=xt[:, 2, :], in_=xv[:, 16 + g, c0:c0 + XP]), [xk], [xk])
                        P.dma("sync", lambda e, xt=xt, c0=c0, g=g: e.dma_start(out=xt[:, 3, :], in_=xv[:, 24 + g, c0:c0 + XP]), [xk], [xk])
                        t0 = s * SEG
                        P.dma("sync", lambda e, li=li, t0=t0, g=g: e.dma_start(
                            out=zs[li][:, :, :], in_=D["zs"][t0:t0 + SEG, g * 256:(g + 1) * 256].rearrange("(c p) f -> p c f", p=128)),
                            [], [("zs", li)])
                        P.dma("sync", lambda e, li=li, t0=t0: e.dma_start(
                            out=dtseg[li][:, :, :], in_=D["dt"][t0:t0 + SEG, :].rearrange("(c p) f -> p c f", p=128)), [], [("dt", li)])
                        for d in range(2):
                            h0 = d * 32 + 4 * g
                            P.op("vector", lambda e, li=li, d=d, h0=h0: e.tensor_tensor(
                                dtA[li][:, :, d * 4:d * 4 + 4], dtseg[li][:, :, h0:h0 + 4],
                                Arow[:, h0:h0 + 4].unsqueeze(1).to_broadcast([128, TPS, 4]), ALU.mult),
                                [("dt", li), "Arow"], [("dtA", li)])
                        P.op("vector", lambda e, li=li: e.tensor_copy(dtAb[li][:, :, :], dtA[li][:, :, :]), [("dtA", li)], [("dtAb", li)])
                        for d in range(2):
                            h0 = d * 32 + 4 * g
                            P.op("scalar", lambda e, li=li, d=d, h0=h0: e.activation(
                                lndt[li][:, :, d * 4:d * 4 + 4], dtseg[li][:, :, h0:h0 + 4], AF.Ln), [("dt", li)], [("lndt", li, d)])
                        pa, pak = ring.next()
                        pb, pbk = ring.next()
                        for ti, tk in enumerate(TRIS):
                            pt_, ptk = (pa, pak) if ti < 4 else (pb, pbk)
                            o0 = (ti % 4) * TPS * 8
                            P.op("tensor", lambda e, pt_=pt_, o0=o0, tk=tk, li=li: e.matmul(
                                pt_[:, o0:o0 + TPS * 8], KB(G, tk), dtAb[li][:, :, :].rearrange("p c h -> p (c h)"),
                                start=True, stop=True), [("dtAb", li), "kc_b"], [ptk])
                        P.op("scalar", lambda e, li=li, pa=pa: e.activation(
                            decs[li][:, 0:4, :, :].rearrange("p a c h -> p (a c h)"), pa[:, 0:4 * TPS * 8], AF.Exp), [pak], [("decs", li)])
                        P.op("scalar", lambda e, li=li, pb=pb: e.activation(
                            decs[li][:, 4, :, :].rearrange("p c h -> p (c h)"), pb[:, 0:TPS * 8], AF.Exp), [pbk, ("decs", li)], [("decs", li)])
                        for d, tri in ((0, I_GT), (1, I_LT)):
                            h0 = d * 32 + 4 * g
                            P.op("vector", lambda e, li=li, d=d, h0=h0, tri=tri: e.tensor_tensor(
                                dtee[li][:, :, d * 4:d * 4 + 4], dtseg[li][:, :, h0:h0 + 4],
                                decs[li][:, tri, :, d * 4:d * 4 + 4], ALU.mult), [("dt", li), ("decs", li)], [("dtee", li)])

                    pcs.append((li, prep))
                    for c in range(TPS):
                        def conv(ring, li=li, s=s, st_=st_, c=c):
                            xt, xk = st_["xt"], st_["xk"]
                            if True:
                                pc, pck = ring.next()
                                for ci in range(3):
                                    ch0 = ccs[ci] * 128
                                    for j in range(5):
                                        P.op("tensor", lambda e, pc=pc, xt=xt, ci=ci, j=j, c=c, dg=dg: e.matmul(
                                            pc[:, ci * 128:(ci + 1) * 128], xt[:, ci, c * 128 + j:c * 128 + j + 128], dg[:, ci, j, :],
                                            start=(j == 0), stop=False), [xk, dgk], [pck])
                                    P.op("tensor", lambda e, pc=pc, ci=ci, convbr=convbr: e.matmul(
                                        pc[:, ci * 128:(ci + 1) * 128], ones_row, convbr[0:1, ci * 128:(ci + 1) * 128], start=False, stop=True),
                                        ["kc_b", (cbk, ci)], [pck])
                                P.op("scalar", lambda e, pc=pc, li=li, c=c: e.activation(xb_tm[li][:, c, :], pc[:, 0:384], AF.Silu),
                                     [pck], [("xb", li, c)])
                            if c % 4 == 0:
                                tb = (c // 4) * 512
                                for ci, dstl, nm in ((2, BT, "BT"), (3, CT, "CT")):
                                    pf, pfk = ring.next()
                                    for j in range(5):
                                        P.op("tensor", lambda e, pf=pf, xt=xt, ci=ci, j=j, tb=tb, dg=dg: e.matmul(
                                            pf[:, :], dg[:, ci, j, :], xt[:, ci, tb + j:tb + j + 512], start=(j == 0), stop=(j == 4)),
                                            [xk, dgk], [pfk])
                                    cc = ccs[ci]
                                    P.op("scalar", lambda e, pf=pf, dstl=dstl, li=li, tb=tb, cc=cc: e.activation(
                                        dstl[li][:, tb:tb + 512], pf[:, :], AF.Silu, bias=convbc[:, cc:cc + 1]),
                                        [pfk, "convbc"], [(nm, li, tb // 512)])


                        pcs.append((li, conv))
                return pcs

            pend_pieces = make_pieces(units[0])
            for ui, segs in enumerate(units):
                for _, fn_ in pend_pieces:
                    fn_(psAll)
                pend_pieces = make_pieces(units[ui + 1]) if ui + 1 < len(units) else []
                busy = {li for li, _ in segs}
                segof = {li: s for li, s in segs}
                order = [(li, c) for (li, s) in segs for c in range(TPS)]
                n_ord = len(order)
                bc = lambda ap: ap.unsqueeze(2).to_broadcast([128, 4, 64])
                r4 = lambda ap: ap.rearrange("p (h q) -> p h q", h=4)

                LOOK = 2
                pre = {}

                def state_pre(d, li, c):
                    xd, xdk = xdeb.next()
                    P.op("gpsimd", lambda e, xd=xd, li=li, c=c, d=d: e.tensor_tensor(
                        r4(xd[:, :]), r4(xb_tm[li][:, c, 0:256]), bc(dtee[li][:, c, d * 4:d * 4 + 4]), ALU.mult),
                        [("xb", li, c), ("dtee", li)], [xdk])
                    ps, psk = psG.next()
                    P.op("tensor", lambda e, ps=ps, xd=xd, li=li, c=c: e.matmul(
                        ps[:, 0:256], xb_tm[li][:, c, 256:384], xd[:, :], start=True, stop=True), [("xb", li, c), xdk], [psk])
                    pre[(d, li, c)] = (ps, psk)

                def state_step(d, li, c, first, bnd_flag):
                    Hs, nm = (Hfs, "Hf") if d == 0 else (Hbs, "Hb")
                    prev = prevf if d == 0 else prevb
                    cur = hpar[d]
                    Hc, Hn = Hs[cur], Hs[1 - cur]
                    if first:
                        P.op("vector", lambda e, Hc=Hc: e.memset(Hc[:, :], 0.0), [], [(nm, cur)])
                    if bnd_flag is not None:
                        P.op("vector", lambda e, Hc=Hc, f=bnd_flag: e.tensor_scalar(
                            Hc[:, :], Hc[:, :], flags[:, f:f + 1], None, ALU.mult), [(nm, cur), "flags"], [(nm, cur)])
                    P.op("scalar", lambda e, Hc=Hc, prev=prev, li=li, c=c: e.activation(prev[li][:, c, :], Hc[:, :], AF.Copy),
                         [(nm, cur)], [("prev", d, li, c)])
                    ps, psk = pre.pop((d, li, c))
                    ht, htk = Ht.next()
                    P.op("vector", lambda e, ht=ht, Hc=Hc, li=li, c=c, d=d: e.tensor_tensor(
                        r4(ht[:, :]), r4(Hc[:, :]), bc(decs[li][:, I_ONES, c, d * 4:d * 4 + 4]), ALU.mult),
                        [(nm, cur), ("decs", li)], [htk])
                    P.op("vector", lambda e, ht=ht, ps=ps, Hn=Hn: e.tensor_tensor(Hn[:, :], ht[:, :], ps[:, 0:256], ALU.add),
                         [htk, psk], [(nm, 1 - cur)])
                    hpar[d] = 1 - cur

                hpar = [0, 0]
                steps = []
                for i in range(n_ord):
                    li, c = order[i]
                    pos = i // TPS
                    steps.append((0, li, c, i == 0, segs[pos][1] if (c == 0 and pos > 0) else None))
                    i2 = n_ord - 1 - i
                    li2, c2 = order[i2]
                    pos2 = i2 // TPS
                    steps.append((1, li2, c2, i == 0, segs[pos2 + 1][1] if (c2 == TPS - 1 and pos2 + 1 < len(segs)) else None))
                for j in range(min(LOOK, len(steps))):
                    state_pre(*steps[j][:3])
                for j, stp in enumerate(steps):
                    if j + LOOK < len(steps):
                        state_pre(*steps[j + LOOK][:3])
                    state_step(*stp)
                    if j % 2 == 1 and pend_pieces and pend_pieces[0][0] not in busy:
                        pend_pieces.pop(0)[1](psA)

                stC = {}

                def c0(i):
                    li, c = order[i]
                    S = stC[i] = {"Ls": []}
                    for d, trix in enumerate((K_LE, K_GE)):
                        L, Lk = Lr.next()
                        P.op("gpsimd" if d else "vector", lambda e, L=L, d=d, trix=trix, li=li, c=c: e.tensor_tensor(
                            L[:, :, :], kf(trix).unsqueeze(1).to_broadcast([128, 4, 128]),
                            dtA[li][:, c, d * 4:d * 4 + 4].unsqueeze(2).to_broadcast([128, 4, 128]), ALU.mult),
                            ["kc_f", ("dtA", li)], [Lk])
                        S["Ls"].append((L, Lk))

                def c1(i):
                    li, c = order[i]
                    S = stC[i]
                    tok = slice(c * 128, (c + 1) * 128)
                    pS, pSk = sc_slots.next()
                    P.op("tensor", lambda e, pS=pS, li=li, tok=tok: e.matmul(
                        pS, BT[li][:, tok], CT[li][:, tok], start=True, stop=True),
                        [("BT", li, c // 4), ("CT", li, c // 4)], [pSk])
                    pO, pOk = o_slots.next()
                    P.op("tensor", lambda e, pO=pO, li=li, tok=tok, c=c: e.matmul(
                        pO[:, 0:256], CT[li][:, tok], prevf[li][:, c, :], start=True, stop=True),
                        [("CT", li, c // 4), ("prev", 0, li, c)], [pOk])
                    P.op("tensor", lambda e, pO=pO, li=li, tok=tok, c=c: e.matmul(
                        pO[:, 256:512], CT[li][:, tok], prevb[li][:, c, :], start=True, stop=True),
                        [("CT", li, c // 4), ("prev", 1, li, c)], [pOk])
                    pgs = []
                    for d, trix in enumerate((K_GT, K_LT)):
                        L, Lk = S["Ls"][d]
                        pg, pgk = psG.next()
                        P.op("tensor", lambda e, pg=pg, L=L, trix=trix: e.matmul(
                            pg[:, :], KB(G, trix), L[:, :, :].rearrange("p h i -> p (h i)"), start=True, stop=True),
                            [Lk, "kc_b"], [pgk])
                        pgs.append((pg, pgk))
                    S.update(pS=pS, pSk=pSk, pO=pO, pOk=pOk, pgs=pgs)

                def c2(i):
                    li, c = order[i]
                    S = stC[i]
                    pS, pSk, pO, pOk = S["pS"], S["pSk"], S["pO"], S["pOk"]
                    sm_, smk = sm.next()
                    P.op("vector", lambda e, sm_=sm_, pS=pS: e.tensor_tensor(sm_[:, 0, :], pS, kf(K_LE), ALU.mult),
                         [pSk, "kc_f"], [(smk, 0)])
                    P.op("vector", lambda e, sm_=sm_, pS=pS: e.tensor_tensor(sm_[:, 1, :], pS, kf(K_GE), ALU.mult),
                         [pSk, "kc_f"], [(smk, 1)])
                    a_, ak = ya.next()
                    b_, bk = yb.next()
                    P.op("vector", lambda e, a_=a_, pO=pO, li=li, c=c: e.tensor_tensor(
                        r4(a_[:, :]), r4(pO[:, 0:256]), bc(decs[li][:, I_LE, c, 0:4]), ALU.mult), [pOk, ("decs", li)], [ak])
                    P.op("vector", lambda e, b_=b_, pO=pO, li=li, c=c: e.tensor_tensor(
                        r4(b_[:, :]), r4(pO[:, 256:512]), bc(decs[li][:, I_GE, c, 4:8]), ALU.mult), [pOk, ("decs", li)], [bk])
                    Es = []
                    for d in range(2):
                        pg, pgk = S["pgs"][d]
                        E, Ek = Er.next()
                        for h in range(4):
                            P.op("scalar", lambda e, E=E, pg=pg, h=h, li=li, c=c, d=d: e.activation(
                                E[:, h, :], pg[:, h * 128:(h + 1) * 128], AF.Exp, bias=lndt[li][:, c, d * 4 + h:d * 4 + h + 1]),
                                [pgk, ("lndt", li, d)], [(Ek, h)])
                        Es.append((E, Ek))
                    S.update(sm_=sm_, smk=smk, a_=a_, ak=ak, b_=b_, bk=bk, Es=Es)

                def c3(i):
                    S = stC[i]
                    sm_, smk = S["sm_"], S["smk"]
                    MTs = []
                    for d in range(2):
                        E, Ek = S["Es"][d]
                        MT, MTk = MTr.next()
                        P.op("vector", lambda e, MT=MT, E=E, sm_=sm_, d=d: e.tensor_tensor(
                            MT[:, :, :], E[:, :, :], sm_[:, d, :].unsqueeze(1).to_broadcast([128, 4, 128]), ALU.mult),
                            [(Ek, 0), (Ek, 1), (Ek, 2), (Ek, 3), (smk, d)], [MTk])
                        MTs.append((MT, MTk))
                    S["MTs"] = MTs

                def c4(i):
                    li, c = order[i]
                    S = stC[i]
                    pY, pYk = y_slots.next()
                    for h in range(4):
                        xs_h = xb_tm[li][:, c, h * 64:(h + 1) * 64]
                        P.op("tensor", lambda e, pY=pY, h=h, xs_h=xs_h, dD=dD: e.matmul(
                            pY[:, h * 64:(h + 1) * 64], dD[:, h, :], xs_h, start=(h == 0), stop=False),
                            [(dDk, h), ("xb", li, c)], [pYk])
                        for d in range(2):
                            MT, MTk = S["MTs"][d]
                            P.op("tensor", lambda e, pY=pY, MT=MT, xs_h=xs_h, h=h, d=d: e.matmul(
                                pY[:, h * 64:(h + 1) * 64], MT[:, h, :], xs_h,
                                start=False, stop=(h == 3 and d == 1)), [MTk, ("xb", li, c)], [pYk])
                    S.update(pY=pY, pYk=pYk)

                def c5(i):
                    S = stC[i]
                    y_, yk = yy.next()
                    P.op("vector", lambda e, y_=y_, pY=S["pY"], a_=S["a_"]: e.tensor_tensor(y_[:, :], pY, a_[:, :], ALU.add),
                         [S["pYk"], S["ak"]], [yk])
                    S.update(y_=y_, yk=yk)

                def c6(i):
                    li, c = order[i]
                    S = stC[i]
                    y_, yk = S["y_"], S["yk"]
                    P.op("gpsimd", lambda e, y_=y_, b_=S["b_"]: e.tensor_tensor(y_[:, :], y_[:, :], b_[:, :], ALU.add), [yk, S["bk"]], [yk])
                    P.op("gpsimd", lambda e, y_=y_, li=li, c=c: e.tensor_tensor(y_[:, :], y_[:, :], zs[li][:, c, :], ALU.mult),
                         [yk, ("zs", li)], [yk])

                def c7(i):
                    S = stC[i]
                    y_, yk = S["y_"], S["yk"]
                    jt, jk = yj.next()
                    s_, sk = yst.next()
                    P.op("scalar", lambda e, jt=jt, y_=y_, s_=s_: e.activation(jt[:, :], y_[:, :], AF.Square, accum_out=s_[:, 0:1]),
                         [yk], [jk, sk])
                    P.op("scalar", lambda e, s_=s_: e.activation(s_[:, 1:2], s_[:, 0:1], AF.Ln, bias=EPS, scale=1.0 / 256), [sk], [sk])
                    P.op("scalar", lambda e, s_=s_: e.activation(s_[:, 2:3], s_[:, 1:2], AF.Exp, scale=-0.5), [sk], [sk])
                    S.update(s_=s_, sk=sk)

                def c8(i):
                    S = stC[i]
                    n_, nk = yn.next()
                    P.op("vector", lambda e, n_=n_, y_=S["y_"], s_=S["s_"], normg=normg: e.scalar_tensor_tensor(
                        n_[:, :], y_[:, :], s_[:, 2:3], normg[:, :], ALU.mult, ALU.mult),
                        [S["yk"], S["sk"], ngk], [nk])
                    S.update(n_=n_, nk=nk)

                def c9(i):
                    S = stC[i]
                    n_, nk = S["n_"], S["nk"]
                    pT, pTk = t_slots.next()
                    for q in range(2):
                        P.op("tensor", lambda e, pT=pT, n_=n_, q=q: e.transpose(pT[:, q, :], n_[:, q * 128:(q + 1) * 128], ident),
                             [nk, "kc_b"], [pTk])
                    S.update(pT=pT, pTk=pTk)

                def c10(i):
                    li, c = order[i]
                    S = stC.pop(i)
                    pT, pTk = S["pT"], S["pTk"]
                    if c % 4 == 0:
                        ysr[0] = ystage.next()
                    ys, ysk = ysr[0]
                    P.op("scalar", lambda e, ys=ys, pT=pT, c=c: e.activation(ys[:, :, (c % 4) * 128:(c % 4 + 1) * 128], pT, AF.Copy),
                         [pTk], [(ysk, c % 4)])
                    if c % 4 == 3:
                        tg = segof[li] * SEG + (c - 3) * 128
                        P.dma("scalar", lambda e, ys=ys, tg=tg, g=g: e.dma_start(
                            out=yTv[:, 2 * g:2 * g + 2, tg:tg + 512], in_=ys[:, :, :]), [(ysk, q) for q in range(4)], [])

                ysr = [None]
                stages = [c0, c1, c2, c3, c4, c5, c6, c7, c8, c9, c10]
                NS = len(stages)
                sorder = [2, 5, 10] + [s for s in reversed(range(NS)) if s not in (2, 5, 10)]
                for t in range(n_ord + NS - 1):
                    for s in sorder:
                        if 0 <= t - s < n_ord:
                            stages[s](t - s)


def rep128(v):
    v = np.asarray(v, np.float32).reshape(1, -1)
    return np.ascontiguousarray(np.broadcast_to(v, (128, v.shape[1])))


def colT(v, n):
    return np.ascontiguousarray(np.asarray(v, np.float32).reshape(n, 128).T)


def shared_inputs(p):
    d = {}
    d["w_in"] = np.ascontiguousarray(p["w_in"][0])
    d["consts"] = host_consts()
    d["gmixT"] = colT(p["g_mix"][0], 8)
    d["bgT"] = colT(p["b_gate"][0], 16)
    d["qkg"] = np.ascontiguousarray(np.stack([np.tile(p["q_norm_g"][0], 2), np.tile(p["k_norm_g"][0], 2)], 1).astype(np.float32))
    d["dtbias"] = rep128(np.concatenate([p["dt_bias_f"][0], p["dt_bias_b"][0]]))
    cw = p["ssd_conv_w"][0]
    d["convw"] = np.ascontiguousarray(cw.reshape(5, 32, 128).transpose(2, 1, 0).reshape(128, 160))
    d["convbc"] = colT(p["ssd_conv_b"][0], 32)
    d["convbr"] = np.ascontiguousarray(p["ssd_conv_b"][0].reshape(1, 4096))
    d["alog"] = rep128(np.concatenate([p["A_log_f"][0], p["A_log_b"][0]]))
    d["dskip"] = rep128(p["D_skip"][0])
    d["normg"] = rep128(p["ssd_norm_g"][0])
    return d


ND = 7


def att_slots(t, tps):
    if t == 0:
        return list(range(-2, 4))
    if t == tps - 1:
        return list(range(-3, 3))
    return list(range(-2, 3))


def bias_gather_index():
    a = np.arange(128)[:, None] // 64
    kc = np.arange(128)[:, None] % 64
    b = np.arange(128)[None, :] // 64
    qc = np.arange(128)[None, :] % 64
    cs = np.clip(qc - 8, 0, 48)
    col_ok = (kc >= cs) & (kc < cs + 16)
    ridx = np.zeros((ND, 128, 128), np.int64)
    cidx = np.zeros((ND, 128, 128), np.int64)
    ok = np.zeros((ND, 128, 128), bool)
    for di in range(ND):
        dr = 2 * (di - 3) + a - b
        dc = kc - qc
        ok[di] = col_ok & (np.abs(dr) <= 7) & (np.abs(dc) <= 15)
        ridx[di] = np.clip(dr + 7, 0, 14)
        cidx[di] = np.clip(dc + 15, 0, 30)
    return ridx, cidx, ok


def att_shared_inputs(p):
    ridx, cidx, ok = bias_gather_index()
    rb = np.asarray(p["rel_bias"][0], np.float32)
    d = {}
    d["biasx"] = np.ascontiguousarray(rb[:, ridx, cidx])
    d["cmask"] = np.ascontiguousarray(np.where(ok, 0.0, NEG).astype(np.float32))
    hs = np.zeros((128, 128), np.float32)
    hs[0, :64] = 1.0
    hs[1, 64:] = 1.0
    d["halfsel"] = hs.astype(ml_dtypes.bfloat16)
    return d


def rowmask_table(cfg, flags_vec):
    tps, nt = cfg.TPS, cfg.NT
    seq_start = np.zeros(cfg.NSEG, np.int64)
    seq_len = np.zeros(cfg.NSEG, np.int64)
    s = 0
    while s < cfg.NSEG:
        e = s
        while e + 1 < cfg.NSEG and flags_vec[e + 1] > 0:
            e += 1
        for k in range(s, e + 1):
            seq_start[k] = s
            seq_len[k] = e - s + 1
        s = e + 1
    tab = np.full((2, nt, 6, 128), NEG, np.float32)
    bq = np.arange(128) // 64
    for T in range(nt):
        sg, t = divmod(T, tps)
        t0 = seq_start[sg] * tps
        rows = seq_len[sg] * tps * 2
        qr = (T - t0) * 2 + bq
        rs = np.clip(qr - 4, 0, rows - 8)
        for si, dlt in enumerate(att_slots(t, tps)):
            KT = T + dlt
            if KT < t0 or KT >= t0 + seq_len[sg] * tps:
                continue
            for a in range(2):
                kr = (KT - t0) * 2 + a
                tab[a, T, si] = np.where((kr >= rs) & (kr < rs + 8), 0.0, NEG)
    return np.ascontiguousarray(tab.reshape(2, nt * 6 * 128)).astype(ml_dtypes.bfloat16)


def phase3(nc, P, cfg, D, G):
    NT, TPS = cfg.NT, cfg.TPS
    NB = NT // 4
    with contextlib.ExitStack() as es:
        sb = lambda n, s, d: es.enter_context(nc.sbuf_tensor("s3_" + n, list(s), d))
        biasE = sb("biasE", [128, 16, ND, 128], BF16)
        bstage = Ring(es, nc, "s3_bst", [128, ND, 128], F32, 2)
        cmask = sb("cmask", [128, ND, 128], F32)
        halfsel = sb("halfsel", [128, 128], BF16)
        Kr = [sb(f"K{i}", [128, 8, 512], BF16) for i in range(4)]
        Vr = [sb(f"V{i}", [128, 4, 16, 65], BF16) for i in range(4)]
        Qr = Ring(es, nc, "s3_Q", [128, 8, 2, 512], BF16, 2)
        rmr = Ring(es, nc, "s3_rm", [128, 6, 128], BF16, 4)
        Ep = Ring(es, nc, "s3_Ep", [128, 6, 128], BF16, 3)
        Eb = Ring(es, nc, "s3_Eb", [128, 6, 128], BF16, 5)
        rec = Ring(es, nc, "s3_rec", [128, 1], F32, 4)
        atm = Ring(es, nc, "s3_atm", [128, 1024], BF16, 3)
        ast = Ring(es, nc, "s3_ast", [128, 8, 512], BF16, 2)
        psAB = Ring(es, nc, "s3_psAB", [128, 1024], F32, 2, psum=True)
        psO = Ring(es, nc, "s3_psO", [128, 512], F32, 2, psum=True)
        psT = Ring(es, nc, "s3_psT", [128, 8, 128], BF16, 1, psum=True)
        ident = KB(G, K_IDENT)

        P.dma("sync", lambda e: e.dma_start(out=cmask[:, :, :], in_=D["cmask"].rearrange("d k q -> k d q")), [], ["cmask"])
        P.dma("sync", lambda e: e.dma_start(out=halfsel[:, :], in_=D["halfsel"]), [], ["halfsel"])
        for h in range(16):
            bs, bsk = bstage.next()
            P.dma("sync", lambda e, h=h, bs=bs: e.dma_start(out=bs[:, :, :], in_=D["biasx"][h].rearrange("d k q -> k d q")),
                  [], [bsk])
            P.op("vector", lambda e, bs=bs: e.tensor_tensor(bs[:, :, :], bs[:, :, :], cmask[:, :, :], ALU.add),
                 [bsk, "cmask"], [bsk])
            P.op("scalar", lambda e, h=h, bs=bs: e.activation(biasE[:, h, :, :], bs[:, :, :], AF.Exp), [bsk], [("biasE", h)])
        for i in range(4):
            P.op("gpsimd", lambda e, i=i: e.memset(Vr[i][:, :, :, :], 1.0), [], [("V", i)])
        for i in range(2):
            P.op("vector", lambda e, i=i: e.memset(Qr.t[i][:, :, :, :], 0.0), [], [Qr.k[i]])
        for i in range(4):
            P.op("vector", lambda e, i=i: e.memset(rmr.t[i][:, :, :], 0.0), [], [rmr.k[i]])

        qTv = D["qT"].rearrange("(hp two p) t -> two p hp t", two=2, p=64)
        kTv = D["kT"].rearrange("(hp p) t -> p hp t", p=128)
        aTv = D["attT"].rearrange("(cc p) t -> p cc t", p=128)
        loaded = set()

        def load_kv(b):
            if b < 0 or b >= NB or b in loaded:
                return
            loaded.add(b)
            i = b % 4
            P.dma("sync", lambda e, b=b, i=i: e.dma_start(out=Kr[i][:, :, :], in_=kTv[:, :, b * 512:(b + 1) * 512]), [], [("K", i)])
            for c in range(4):
                P.dma("sync", lambda e, b=b, i=i, c=c: e.dma_start(
                    out=Vr[i][:, c, :, 0:64],
                    in_=D["v"][b * 512 + c * 128:b * 512 + (c + 1) * 128, :].rearrange("p (h d) -> p h d", d=64)),
                    [("V", i)], [("V", i)])

        items = [(b, tt, h) for b in range(NB) for tt in range(4) for h in range(16)]
        stT = {}
        stH = {}

        def part_a(k):
            b, tt, h = items[k]
            T = b * 4 + tt
            if tt == 0 and h == 0:
                load_kv(b - 1)
                load_kv(b)
                load_kv(b + 1)
                Qt, Qk = Qr.next()
                for hh in range(2):
                    P.dma("sync", lambda e, Qt=Qt, b=b, hh=hh: e.dma_start(
                        out=Qt[hh * 64:(hh + 1) * 64, :, hh, :], in_=qTv[hh][:, :, b * 512:(b + 1) * 512]), [Qk], [Qk])
                stT[("b", b)] = (Qt, Qk) + ast.next()
            Qt, Qk, as_, ask = stT[("b", b)]
            t = T % TPS
            slots = att_slots(t, TPS)
            nsl = len(slots)
            if h == 0:
                rm, rmk = rmr.next()
                P.dma("sync", lambda e, rm=rm, T=T: e.dma_start(
                    out=rm[0:2, :, :], in_=D["rowmask"][:, T * 768:(T + 1) * 768].rearrange("a (s q) -> a s q", q=128)), [rmk], [rmk])
                stT[T] = (rm, rmk) + atm.next()
            rm, rmk, at, atk = stT[T]
            hp, hh = divmod(h, 2)
            pr = slice(hh * 64, hh * 64 + 64)
            pab, pabk = psAB.next()
            kts = []
            for si, dlt in enumerate(slots):
                KT = min(max(T + dlt, 0), NT - 1)
                kb, kt = divmod(KT, 4)
                kts.append((kb % 4, kt))
                o0 = si * 128
                P.op("tensor", lambda e, pab=pab, o0=o0, kb=kb, kt=kt, hp=hp, hh=hh, Qt=Qt, tt=tt: e.matmul(
                    pab[:, o0:o0 + 128], Kr[kb % 4][:, hp, kt * 128:(kt + 1) * 128], Qt[:, hp, hh, tt * 128:(tt + 1) * 128],
                    start=True, stop=False), [("K", kb % 4), Qk], [pabk])
                P.op("tensor", lambda e, pab=pab, o0=o0, rm=rm, si=si: e.matmul(
                    pab[:, o0:o0 + 128], halfsel[:, :], rm[:, si, :], start=False, stop=True), ["halfsel", rmk], [pabk])
            ep, epk = Ep.next()
            d0 = slots[0] + 3
            P.op("scalar", lambda e, ep=ep, pab=pab, nsl=nsl: e.activation(
                ep[:, 0:nsl, :], pab[:, 0:nsl * 128].rearrange("p (s q) -> p s q", q=128), AF.Exp), [pabk], [epk])
            eb, ebk = Eb.next()
            P.op("vector", lambda e, eb=eb, ep=ep, h=h, d0=d0, nsl=nsl: e.tensor_tensor(
                eb[:, 0:nsl, :], ep[:, 0:nsl, :], biasE[:, h, d0:d0 + nsl, :], ALU.mult), [epk, ("biasE", h)], [ebk])
            stH[k] = (eb, ebk, kts, nsl)

        def part_b(k):
            b, tt, h = items[k]
            T = b * 4 + tt
            eb, ebk, kts, nsl = stH.pop(k)
            rm, rmk, at, atk = stT[T]
            Qt, Qk, as_, ask = stT[("b", b)]
            po, pok = psO.next()
            for si in range(nsl):
                vb, vt = kts[si]
                P.op("tensor", lambda e, po=po, eb=eb, si=si, vb=vb, vt=vt, h=h, nsl=nsl: e.matmul(
                    po[:, 0:65], eb[:, si, :], Vr[vb][:, vt, h, :], start=(si == 0), stop=(si == nsl - 1)),
                    [ebk, ("V", vb)], [pok])
            rc, rck = rec.next()
            P.op("vector", lambda e, rc=rc, po=po: e.reciprocal(rc[:, :], po[:, 64:65]), [pok], [rck])
            P.op("vector", lambda e, at=at, po=po, rc=rc, h=h: e.tensor_scalar(
                at[:, h * 64:(h + 1) * 64], po[:, 0:64], rc[:, 0:1], None, ALU.mult), [pok, rck], [(atk, h)])
            if h == 15:
                pt, ptk2 = psT.next()
                for cc in range(8):
                    P.op("tensor", lambda e, pt=pt, at=at, cc=cc: e.transpose(pt[:, cc, :], at[:, cc * 128:(cc + 1) * 128], ident),
                         [(atk, 2 * cc), (atk, 2 * cc + 1), "kc_b"], [ptk2])
                P.op("scalar", lambda e, as_=as_, pt=pt, tt=tt: e.activation(as_[:, :, tt * 128:(tt + 1) * 128], pt[:, :, :], AF.Copy),
                     [ptk2], [(ask, tt)])
                del stT[T]
                if tt == 3:
                    P.dma("scalar", lambda e, as_=as_, b=b: e.dma_start(out=aTv[:, :, b * 512:(b + 1) * 512], in_=as_[:, :, :]),
                          [(ask, q) for q in range(4)], [])
                    del stT[("b", b)]

        SK = 2
        for k in range(len(items) + SK):
            if k < len(items):
                part_a(k)
            if k - SK >= 0:
                part_b(k - SK)


def phase4a(nc, P, cfg, D, G):
    NTOK, SEG, HP, NSEG = cfg.NTOK, cfg.SEG, cfg.HP, cfg.NSEG
    with contextlib.ExitStack() as es:
        sb = lambda n, s, d: es.enter_context(nc.sbuf_tensor("s4_" + n, list(s), d))
        Wao = sb("Wao", [128, 8, 1024], BF16)
        Wso = sb("Wso", [128, 16, 1024], BF16)
        Wo = sb("Wo", [128, 8, 1024], BF16)
        gffn = sb("gffn", [128, 8], F32)
        zc = sb("zc", [128, 8, 1], BF16)
        aT = Ring(es, nc, "s4_aT", [128, 8, 512], BF16, 2)
        yT = Ring(es, nc, "s4_yT", [128, 16, 512], BF16, 2)
        gT = Ring(es, nc, "s4_gT", [128, 16, 512], BF16, 2)
        xr = Ring(es, nc, "s4_x", [128, 1024], F32, 2)
        t1 = Ring(es, nc, "s4_t1", [128, 512], F32, 2)
        t2 = Ring(es, nc, "s4_t2", [128, 512], F32, 2)
        mT = Ring(es, nc, "s4_mT", [128, 8, 512], BF16, 1)
        x1r = Ring(es, nc, "s4_x1", [128, 1024], F32, 2)
        junk = Ring(es, nc, "s4_junk", [128, 1024], BF16, 1)
        st = Ring(es, nc, "s4_st", [128, 4], F32, 2)
        xn = Ring(es, nc, "s4_xn", [128, 1024], BF16, 2)
        hst = Ring(es, nc, "s4_hst", [128, 8, 512], BF16, 2)
        pdr = Ring(es, nc, "s4_pd", [128, 8, 1], BF16, 4)
        psA = Ring(es, nc, "s4_psA", [128, 512], F32, 2, psum=True)
        psB = Ring(es, nc, "s4_psB", [128, 512], F32, 2, psum=True)
        psX = Ring(es, nc, "s4_psX", [128, 512], F32, 2, psum=True)
        psT = Ring(es, nc, "s4_psT", [128, 8, 128], BF16, 2, psum=True)
        ident = KB(G, K_IDENT)
        flags = G["flags"]

        wao_v = D["w_att_out"].rearrange("(kc p) d -> p kc d", p=128)
        wso_v = D["w_ssd_out"].rearrange("(kc p) d -> p kc d", p=128)
        wo_v = D["w_o"].rearrange("(kc p) d -> p kc d", p=128)
        P.dma("gpsimd", lambda e: e.dma_start(out=Wao[:, :, :], in_=wao_v), [], ["Wao"])
        P.dma("gpsimd", lambda e: e.dma_start(out=Wso[:, 0:8, :], in_=wso_v[:, 0:8, :]), [], [("Wso", 0)])
        P.dma("gpsimd", lambda e: e.dma_start(out=Wso[:, 8:16, :], in_=wso_v[:, 8:16, :]), [], [("Wso", 1)])
        P.dma("gpsimd", lambda e: e.dma_start(out=Wo[:, :, :], in_=wo_v), [], ["Wo"])
        P.dma("sync", lambda e: e.dma_start(out=gffn[:, :], in_=D["gffnT"]), [], ["gffn"])
        P.op("vector", lambda e: e.memset(zc[:, :, :], 0.0), [], ["zc"])
        hv = D["h2T"].rearrange("(kc p) t -> p kc t", p=128)
        P.dma("gpsimd", lambda e: e.dma_start(out=hv[:, :, 0:1], in_=zc[:, :, :], allow_slow_non_contiguous=True), ["zc"], [])
        P.dma("gpsimd", lambda e: e.dma_start(out=hv[:, :, NSEG * HP - 1:NSEG * HP], in_=zc[:, :, :], allow_slow_non_contiguous=True), ["zc"], [])
        aTv = D["attT"].rearrange("(cc p) t -> p cc t", p=128)
        yTv = D["yT"].rearrange("(cc p) t -> p cc t", p=128)
        gTv = D["gT"].rearrange("(cc p) t -> p cc t", p=128)

        def load_blk(tb):
            a_, ak = aT.next()
            y_, yk = yT.next()
            g_, gk = gT.next()
            P.dma("sync", lambda e, a_=a_, tb=tb: e.dma_start(out=a_[:, :, :], in_=aTv[:, :, tb:tb + 512]), [], [ak])
            P.dma("sync", lambda e, y_=y_, tb=tb: e.dma_start(out=y_[:, :, :], in_=yTv[:, :, tb:tb + 512]), [], [yk])
            P.dma("sync", lambda e, g_=g_, tb=tb: e.dma_start(out=g_[:, :, :], in_=gTv[:, :, tb:tb + 512]), [], [gk])
            return a_, ak, y_, yk, g_, gk

        nxt = load_blk(0)
        for tb in range(0, NTOK, 512):
            a_, ak, y_, yk, g_, gk = nxt
            if tb + 512 < NTOK:
                nxt = load_blk(tb + 512)
            m_, mk = mT.next()
            for dmc in range(8):
                pa, pak = psA.next()
                pb, pbk = psB.next()
                for kc in range(8):
                    P.op("tensor", lambda e, pa=pa, a_=a_, kc=kc, dmc=dmc: e.matmul(
                        pa[:, :], Wao[:, kc, dmc * 128:(dmc + 1) * 128], a_[:, kc, :], start=(kc == 0), stop=(kc == 7)),
                        ["Wao", ak], [pak])
                for kc in range(16):
                    P.op("tensor", lambda e, pb=pb, y_=y_, kc=kc, dmc=dmc: e.matmul(
                        pb[:, :], Wso[:, kc, dmc * 128:(dmc + 1) * 128], y_[:, kc, :], start=(kc == 0), stop=(kc == 15)),
                        [("Wso", kc // 8), yk], [pbk])
                u1, u1k = t1.next()
                u2, u2k = t2.next()
                P.op("vector", lambda e, u1=u1, pa=pa, g_=g_, dmc=dmc: e.tensor_tensor(u1[:, :], pa[:, :], g_[:, dmc, :], ALU.mult),
                     [pak, gk], [u1k])
                P.op("vector", lambda e, u2=u2, pb=pb, g_=g_, dmc=dmc: e.tensor_tensor(u2[:, :], pb[:, :], g_[:, 8 + dmc, :], ALU.mult),
                     [pbk, gk], [u2k])
                P.op("gpsimd", lambda e, m_=m_, u1=u1, u2=u2, dmc=dmc: e.tensor_tensor(m_[:, dmc, :], u1[:, :], u2[:, :], ALU.add),
                     [u1k, u2k], [(mk, dmc)])
            hs, hsk = hst.next()
            pend = []
            for tt in range(4):
                t0 = tb + tt * 128
                x_, xk = xr.next()
                P.dma("sync", lambda e, x_=x_, t0=t0: e.dma_start(out=x_[:, :], in_=D["x"][t0:t0 + 128, :]), [], [xk])
                x1, x1k = x1r.next()
                for dh in range(2):
                    px, pxk = psX.next()
                    for kc in range(8):
                        P.op("tensor", lambda e, px=px, m_=m_, kc=kc, tt=tt, dh=dh: e.matmul(
                            px[:, :], m_[:, kc, tt * 128:(tt + 1) * 128], Wo[:, kc, dh * 512:(dh + 1) * 512],
                            start=(kc == 0), stop=(kc == 7)), [(mk, kc), "Wo"], [pxk])
                    P.op("vector", lambda e, x1=x1, px=px, x_=x_, dh=dh: e.tensor_tensor(
                        x1[:, dh * 512:(dh + 1) * 512], px[:, :], x_[:, dh * 512:(dh + 1) * 512], ALU.add), [pxk, xk], [(x1k, dh)])
                def fin(x1=x1, x1k=x1k, tt=tt, hs=hs, hsk=hsk, t0=t0):
                    P.dma("scalar", lambda e, x1=x1, t0=t0: e.dma_start(out=D["x1"][t0:t0 + 128, :], in_=x1[:, :]),
                          [(x1k, 0), (x1k, 1)], [])
                    jt, jk = junk.next()
                    s_, sk = st.next()
                    P.op("scalar", lambda e, jt=jt, x1=x1, s_=s_: e.activation(jt[:, :], x1[:, :], AF.Square, accum_out=s_[:, 0:1]),
                         [(x1k, 0), (x1k, 1)], [jk, sk])
                    P.op("scalar", lambda e, s_=s_: e.activation(s_[:, 1:2], s_[:, 0:1], AF.Ln, bias=EPS, scale=1.0 / 1024), [sk], [sk])
                    P.op("scalar", lambda e, s_=s_: e.activation(s_[:, 2:3], s_[:, 1:2], AF.Exp, scale=-0.5), [sk], [sk])
                    n_, nk = xn.next()
                    P.op("scalar", lambda e, n_=n_, x1=x1, s_=s_: e.activation(n_[:, :], x1[:, :], AF.Copy, scale=s_[:, 2:3]),
                         [(x1k, 0), (x1k, 1), sk], [nk])
                    pt, ptk = psT.next()
                    for kc in range(8):
                        P.op("tensor", lambda e, pt=pt, n_=n_, kc=kc: e.transpose(pt[:, kc, :], n_[:, kc * 128:(kc + 1) * 128], ident),
                             [nk, "kc_b"], [ptk])
                    P.op("vector", lambda e, hs=hs, pt=pt, tt=tt: e.tensor_tensor(
                        hs[:, :, tt * 128:(tt + 1) * 128], pt[:, :, :], gffn[:, :].unsqueeze(2).to_broadcast([128, 8, 128]), ALU.mult),
                        [ptk, "gffn"], [(hsk, tt)])
                if pend:
                    pend.pop(0)()
                pend.append(fin)
            while pend:
                pend.pop(0)()
            s, tin = divmod(tb, SEG)
            c0 = s * HP + 1 + tin
            hkeys = [(hsk, i) for i in range(4)]
            P.dma("scalar", lambda e, hs=hs, c0=c0: e.dma_start(out=hv[:, :, c0:c0 + 512], in_=hs[:, :, :]), hkeys, [])
            if tin == 0 and s > 0:
                p_, pk_ = pdr.next()
                P.op("vector", lambda e, p_=p_, hs=hs, s=s: e.tensor_scalar(p_[:, :, :], hs[:, :, 0:1], flags[:, s:s + 1], None, ALU.mult),
                     [(hsk, 0), "flags"], [pk_])
                cp = (s - 1) * HP + 1 + SEG
                P.dma("scalar", lambda e, p_=p_, cp=cp: e.dma_start(out=hv[:, :, cp:cp + 1], in_=p_[:, :, :], allow_slow_non_contiguous=True), [pk_], [])
            if tin + 512 == SEG and s + 1 < NSEG:
                p_, pk_ = pdr.next()
                P.op("vector", lambda e, p_=p_, hs=hs, s=s: e.tensor_scalar(p_[:, :, :], hs[:, :, 511:512], flags[:, s + 1:s + 2], None, ALU.mult),
                     [(hsk, 3), "flags"], [pk_])
                cp = (s + 1) * HP
                P.dma("scalar", lambda e, p_=p_, cp=cp: e.dma_start(out=hv[:, :, cp:cp + 1], in_=p_[:, :, :], allow_slow_non_contiguous=True), [pk_], [])


def phase4b(nc, P, cfg, D, G):
    NTOK, SEG, HP = cfg.NTOK, cfg.SEG, cfg.HP
    with contextlib.ExitStack() as es:
        sb = lambda n, s, d: es.enter_context(nc.sbuf_tensor("s5_" + n, list(s), d))
        Wup = sb("Wup", [128, 8, 5632], BF16)
        Wdn = sb("Wdn", [128, 22, 1024], BF16)
        fcw = sb("fcw", [128, 44, 3], F32)
        fcb = sb("fcb", [128, 44], F32)
        act = sb("act", [128, 22, 256], BF16)
        h2 = Ring(es, nc, "s5_h2", [128, 8, 258], BF16, 2)
        x1r = Ring(es, nc, "s5_x1", [128, 1024], F32, 2)
        ua = Ring(es, nc, "s5_ua", [128, 256], F32, 3)
        ug = Ring(es, nc, "s5_ug", [128, 256], F32, 3)
        sg = Ring(es, nc, "s5_sg", [128, 256], F32, 3)
        orr = Ring(es, nc, "s5_o", [128, 1024], F32, 2)
        psU = Ring(es, nc, "s5_psU", [128, 512], F32, 4, psum=True)
        psD = Ring(es, nc, "s5_psD", [128, 512], F32, 2, psum=True)

        wup_v = D["w_up"].rearrange("(kc p) f -> p kc f", p=128)
        wdn_v = D["w_down"].rearrange("(fa p) d -> p fa d", p=128)
        for q in range(4):
            P.dma("gpsimd", lambda e, q=q: e.dma_start(out=Wup[:, :, q * 1408:(q + 1) * 1408], in_=wup_v[:, :, q * 1408:(q + 1) * 1408]),
                  [], [("Wup", q)])
        for q in range(2):
            P.dma("gpsimd", lambda e, q=q: e.dma_start(out=Wdn[:, q * 11:(q + 1) * 11, :], in_=wdn_v[:, q * 11:(q + 1) * 11, :]),
                  [], [("Wdn", q)])
        P.dma("sync", lambda e: e.dma_start(out=fcw[:, :, :], in_=D["fcw"].rearrange("p (c j) -> p c j", j=3)), [], ["fcw"])
        P.dma("sync", lambda e: e.dma_start(out=fcb[:, :], in_=D["fcb"]), [], ["fcb"])
        hv = D["h2T"].rearrange("(kc p) t -> p kc t", p=128)

        def load_h(tb):
            s, tin = divmod(tb, SEG)
            c0 = s * HP + tin
            h_, hk = h2.next()
            P.dma("sync", lambda e, h_=h_, c0=c0: e.dma_start(out=h_[:, :, :], in_=hv[:, :, c0:c0 + 258]), [], [hk])
            return h_, hk

        nxt = load_h(0)
        for tb in range(0, NTOK, 256):
            h_, hk = nxt
            if tb + 256 < NTOK:
                nxt = load_h(tb + 256)
            x1s = []
            for tt in range(2):
                x1, x1k = x1r.next()
                P.dma("sync", lambda e, x1=x1, t0=tb + tt * 128: e.dma_start(out=x1[:, :], in_=D["x1"][t0:t0 + 128, :]), [], [x1k])
                x1s.append((x1, x1k))
            pend = []
            for fa in range(22):
                res = []
                for which, cc in ((0, fa), (1, 22 + fa)):
                    pu, puk = psU.next()
                    for kc in range(8):
                        P.op("tensor", lambda e, pu=pu, h_=h_, kc=kc, cc=cc: e.matmul(
                            pu[:, 0:258], Wup[:, kc, cc * 128:(cc + 1) * 128], h_[:, kc, :], start=(kc == 0), stop=(kc == 7)),
                            [("Wup", (cc * 128) // 1408), ("Wup", (cc * 128 + 127) // 1408), hk], [puk])
                    u_, uk = (ua if which == 0 else ug).next()
                    P.op("scalar", lambda e, u_=u_, pu=pu, cc=cc: e.activation(
                        u_[:, :], pu[:, 1:257], AF.Identity, bias=fcb[:, cc:cc + 1], scale=fcw[:, cc, 1:2]), [puk, "fcw", "fcb"], [uk])
                    P.op("vector", lambda e, u_=u_, pu=pu, cc=cc: e.scalar_tensor_tensor(
                        u_[:, :], pu[:, 0:256], fcw[:, cc, 0:1], u_[:, :], ALU.mult, ALU.add), [puk, "fcw", uk], [uk])
                    P.op("vector", lambda e, u_=u_, pu=pu, cc=cc: e.scalar_tensor_tensor(
                        u_[:, :], pu[:, 2:258], fcw[:, cc, 2:3], u_[:, :], ALU.mult, ALU.add), [puk, "fcw", uk], [uk])
                    res.append((u_, uk))
                (a_, ak), (g_, gk) = res

                def fin(a_=a_, ak=ak, g_=g_, gk=gk, fa=fa):
                    s_, sk = sg.next()
                    P.op("scalar", lambda e, s_=s_, g_=g_: e.activation(s_[:, :], g_[:, :], AF.Silu), [gk], [sk])
                    P.op("gpsimd", lambda e, a_=a_, s_=s_, fa=fa: e.tensor_tensor(act[:, fa, :], a_[:, :], s_[:, :], ALU.mult),
                         [ak, sk], [("act", fa)])
                if pend:
                    pend.pop(0)()
                pend.append(fin)
            while pend:
                pend.pop(0)()
            for tt in range(2):
                t0 = tb + tt * 128
                x1, x1k = x1s[tt]
                o_, ok = orr.next()
                for dh in range(2):
                    pd_, pdk = psD.next()
                    for fa in range(22):
                        P.op("tensor", lambda e, pd_=pd_, fa=fa, tt=tt, dh=dh: e.matmul(
                            pd_[:, :], act[:, fa, tt * 128:(tt + 1) * 128], Wdn[:, fa, dh * 512:(dh + 1) * 512],
                            start=(fa == 0), stop=(fa == 21)), [("act", fa), ("Wdn", fa // 11)], [pdk])
                    P.op("vector", lambda e, o_=o_, pd_=pd_, x1=x1, dh=dh: e.tensor_tensor(
                        o_[:, dh * 512:(dh + 1) * 512], pd_[:, :], x1[:, dh * 512:(dh + 1) * 512], ALU.add), [pdk, x1k], [(ok, dh)])
                P.dma("sync", lambda e, o_=o_, t0=t0: e.dma_start(out=D["out"][t0:t0 + 128, :], in_=o_[:, :]),
                      [(ok, 0), (ok, 1)], [])


def ffn_shared_inputs(p):
    d = {}
    d["w_att_out"] = np.ascontiguousarray(p["w_att_out"][0])
    d["w_ssd_out"] = np.ascontiguousarray(p["w_ssd_out"][0])
    d["w_o"] = np.ascontiguousarray(p["w_o"][0])
    d["w_up"] = np.ascontiguousarray(p["w_up"][0])
    d["w_down"] = np.ascontiguousarray(p["w_down"][0])
    d["gffnT"] = colT(p["g_ffn"][0], 8)
    fw = p["ffn_conv_w"][0]
    d["fcw"] = np.ascontiguousarray(fw.reshape(3, 44, 128).transpose(2, 1, 0).reshape(128, 132))
    d["fcb"] = colT(p["ffn_conv_b"][0], 44)
    return d


N_CORES = 8
_NC_CACHE = {}


def full_cfg():
    return Cfg(nseg=5, seg=2048, chains=((0,), (1,), (2,), (3, 4)), tg=5120)


def core_segments(c):
    if c < 4:
        segs = [("p", 3 * c + i, 0) for i in range(3)] + [("s", c, 0), ("s", c, 2048)]
        linked = True
    else:
        segs = [("p", 12 + 5 * (c - 4) + i, 0) for i in range(5)]
        linked = False
    return segs, linked


def kernel(x_prompt, x_sample, g_mix, w_in, b_gate, q_norm_g, k_norm_g, rel_bias, ssd_conv_w,
           ssd_conv_b, dt_bias_f, dt_bias_b, A_log_f, A_log_b, D_skip, ssd_norm_g, w_att_out,
           w_ssd_out, w_o, g_ffn, w_up, ffn_conv_w, ffn_conv_b, w_down):
    p = dict(g_mix=g_mix, w_in=w_in, b_gate=b_gate, q_norm_g=q_norm_g, k_norm_g=k_norm_g, rel_bias=rel_bias,
             ssd_conv_w=ssd_conv_w, ssd_conv_b=ssd_conv_b, dt_bias_f=dt_bias_f, dt_bias_b=dt_bias_b,
             A_log_f=A_log_f, A_log_b=A_log_b, D_skip=D_skip, ssd_norm_g=ssd_norm_g, w_att_out=w_att_out,
             w_ssd_out=w_ssd_out, w_o=w_o, g_ffn=g_ffn, w_up=w_up, ffn_conv_w=ffn_conv_w, ffn_conv_b=ffn_conv_b,
             w_down=w_down)
    p = {k: np.asarray(v, np.float32) for k, v in p.items()}
    x_prompt = np.asarray(x_prompt, np.float32)
    x_sample = np.asarray(x_sample, np.float32)
    cfg = full_cfg()
    if "nc" not in _NC_CACHE:
        _NC_CACHE["nc"] = build(cfg)
    nc = _NC_CACHE["nc"]
    shared = {}
    shared.update(shared_inputs(p))
    shared.update(att_shared_inputs(p))
    shared.update(ffn_shared_inputs(p))
    in_maps = []
    for c in range(N_CORES):
        segs, linked = core_segments(c)
        xs = []
        for which, bi, off in segs:
            src = x_prompt if which == "p" else x_sample
            xs.append(src[bi, off:off + 2048])
        flags = np.zeros((128, 8), np.float32)
        if linked:
            flags[:, 4] = 1.0
        m = dict(shared)
        m["x"] = np.ascontiguousarray(np.concatenate(xs, 0))
        m["flags"] = flags
        m["rowmask"] = rowmask_table(cfg, flags[0])
        in_maps.append(m)
    res = run_bass_kernel_spmd(nc, in_maps, core_ids=list(range(N_CORES)))
    y_prompt = np.empty((32, 2048, 1024), np.float32)
    y_sample = np.empty((4, 4096, 1024), np.float32)
    for c in range(N_CORES):
        o = np.asarray(res.results[c]["out"], np.float32)
        segs, _ = core_segments(c)
        for i, (which, bi, off) in enumerate(segs):
            dst = y_prompt if which == "p" else y_sample
            dst[bi, off:off + 2048] = o[i * 2048:(i + 1) * 2048]
    return y_prompt, y_sample
```
